# Optimizing a Trainium2 kernel written in Bass

```python
import math
import jax
import jax.numpy as jnp
from jax import lax
import numpy as np


D_MODEL = 1024
BATCH = 8
SEQ = 4096
DEPTH = 2
DEC_BATCH = 32
DEC_SEQ = 2048
PAST_LEN = 128

GRID_W = 64
N_EVEN = (DEPTH + 1) // 2
N_ODD = DEPTH // 2

S5_WIDTH = D_MODEL // 2
S5_GROUP = 16
S5_GROUPS = S5_WIDTH // S5_GROUP
S5_STATE = 64
DT_MIN = 1e-3
DT_MAX = 1e-1

NA_HEAD_DIM = 64
NA_HEADS = (D_MODEL // 2) // NA_HEAD_DIM
NA_WIDTH = NA_HEADS * NA_HEAD_DIM
NA_ROWS_MAX = 8
NA_COLS = 16

EVEN_IN = S5_WIDTH + 3 * NA_WIDTH
EVEN_MIX = S5_WIDTH + NA_WIDTH

GQA_HEAD_DIM = 64
GQA_HEADS = D_MODEL // GQA_HEAD_DIM
GQA_KV_HEADS = GQA_HEADS // 4
GQA_GROUP = GQA_HEADS // GQA_KV_HEADS
WINDOW = 128
BLOCK = 128
ODD_IN = (GQA_HEADS + 2 * GQA_KV_HEADS) * GQA_HEAD_DIM
ODD_MIX = GQA_HEADS * GQA_HEAD_DIM

T5_BUCKETS = 32
T5_MAX_DIST = 128

D_FF = 2816
RMS_EPS = 1e-6
NEG_INF = -1e30

kernel_name = 'hybrid_s5_natten_swa_encoder'


def rms_norm(x, g):
    xf = x.astype(jnp.float32)
    y = xf * lax.rsqrt(jnp.mean(xf * xf, axis=-1, keepdims=True) + RMS_EPS)
    return (y * g.astype(jnp.float32)).astype(x.dtype)


def swiglu(x, wg, wu, wd):
    return (jax.nn.silu(x @ wg) * (x @ wu)) @ wd


def _cmul(ar, ai, br, bi):
    return ar * br - ai * bi, ar * bi + ai * br


def _ssm_combine(e1, e2):
    a1r, a1i, b1r, b1i = e1
    a2r, a2i, b2r, b2i = e2
    ar, ai = _cmul(a2r, a2i, a1r, a1i)
    br, bi = _cmul(a2r, a2i, b1r, b1i)
    return ar, ai, br + b2r, bi + b2i


def s5_scan(ug, lam_re, lam_im, log_dt, b_re, b_im, c_re, c_im, reverse):
    dt = jnp.exp(log_dt)[:, None]
    mag = jnp.exp(lam_re * dt)
    ab_re = mag * jnp.cos(lam_im * dt)
    ab_im = mag * jnp.sin(lam_im * dt)
    den = lam_re * lam_re + lam_im * lam_im
    nr = ab_re - 1.0
    cr = (nr * lam_re + ab_im * lam_im) / den
    ci = (ab_im * lam_re - nr * lam_im) / den
    bb_re = cr[..., None] * b_re - ci[..., None] * b_im
    bb_im = cr[..., None] * b_im + ci[..., None] * b_re
    bu_re = jnp.einsum('blgc,gpc->blgp', ug, bb_re)
    bu_im = jnp.einsum('blgc,gpc->blgp', ug, bb_im)
    a_re = jnp.broadcast_to(ab_re, bu_re.shape)
    a_im = jnp.broadcast_to(ab_im, bu_im.shape)
    _, _, s_re, s_im = lax.associative_scan(_ssm_combine, (a_re, a_im, bu_re, bu_im), reverse=reverse, axis=1)
    return jnp.einsum('blgp,gcp->blgc', s_re, c_re) - jnp.einsum('blgp,gcp->blgc', s_im, c_im)


def s5_mixer(u, lam_re, lam_im, log_dt, b_re, b_im, c_re, c_im, d_skip, w_glu, b_glu):
    bsz, seq = u.shape[0], u.shape[1]
    f32 = jnp.float32
    uf = u.astype(f32)
    ug = uf.reshape(bsz, seq, S5_GROUPS, S5_GROUP)
    y = d_skip.astype(f32) * uf
    for direction in range(2):
        yd = s5_scan(ug, lam_re[direction].astype(f32), lam_im[direction].astype(f32),
                     log_dt[direction].astype(f32), b_re[direction].astype(f32), b_im[direction].astype(f32),
                     c_re[direction].astype(f32), c_im[direction].astype(f32), reverse=(direction == 1))
        y = y + yd.reshape(bsz, seq, S5_WIDTH)
    g = jax.nn.gelu(y)
    out = g * jax.nn.sigmoid(g @ w_glu.astype(f32) + b_glu.astype(f32))
    return out.astype(u.dtype)


def neighbourhood_attention(q, k, v, rpb):
    bsz, seq, h, dh = q.shape
    rows = seq // GRID_W
    kh = min(NA_ROWS_MAX, rows)
    kw = NA_COLS
    f32 = jnp.float32
    qg = q.reshape(bsz, rows, GRID_W, h, dh)
    kg = k.reshape(bsz, rows, GRID_W, h, dh)
    vg = v.reshape(bsz, rows, GRID_W, h, dh)
    row_start = jnp.clip(jnp.arange(rows) - kh // 2, 0, rows - kh)
    rel_row = row_start[:, None] + jnp.arange(kh)[None, :] - jnp.arange(rows)[:, None]
    col_idx = jnp.clip(jnp.arange(GRID_W) - kw // 2, 0, GRID_W - kw)[:, None] + jnp.arange(kw)[None, :]
    rel_col = col_idx - jnp.arange(GRID_W)[:, None]
    scale = dh ** -0.5

    def one_row(args):
        q_r, rs, rr = args
        k_win = lax.dynamic_slice_in_dim(kg, rs, kh, axis=1)[:, :, col_idx]
        v_win = lax.dynamic_slice_in_dim(vg, rs, kh, axis=1)[:, :, col_idx]
        bias = rpb[:, rr[:, None, None] + NA_ROWS_MAX - 1, rel_col[None] + NA_COLS - 1]
        bias = jnp.transpose(bias, (2, 0, 1, 3)).astype(f32)
        s = jnp.einsum('bqhd,bkqjhd->bqhkj', q_r, k_win).astype(f32) * scale + bias[None]
        p = jax.nn.softmax(s.reshape(bsz, GRID_W, h, kh * kw), axis=-1).reshape(s.shape)
        return jnp.einsum('bqhkj,bkqjhd->bqhd', p.astype(v.dtype), v_win)

    out = lax.map(one_row, (jnp.moveaxis(qg, 1, 0), row_start, rel_row))
    return jnp.moveaxis(out, 0, 1).reshape(bsz, seq, h * dh)


def t5_bucket(rel):
    half = T5_BUCKETS // 2
    max_exact = half // 2
    ret = jnp.where(rel > 0, half, 0)
    n = jnp.abs(rel)
    nf = jnp.maximum(n, 1).astype(jnp.float32)
    large = max_exact + (jnp.log(nf / max_exact) / math.log(T5_MAX_DIST / max_exact)
                         * (half - max_exact)).astype(jnp.int32)
    large = jnp.minimum(large, half - 1)
    return ret + jnp.where(n < max_exact, n, large)


def windowed_gqa(q, k, v, sink, t5_table):
    bsz, seq = q.shape[0], q.shape[1]
    nb = seq // BLOCK
    f32 = jnp.float32
    qb = jnp.moveaxis(q.reshape(bsz, nb, BLOCK, GQA_KV_HEADS, GQA_GROUP, GQA_HEAD_DIM), 1, 0)
    pad = ((0, 0), (BLOCK, BLOCK), (0, 0), (0, 0))
    kp = jnp.pad(k, pad)
    vp = jnp.pad(v, pad)
    qi = jnp.arange(BLOCK)[:, None]
    kj = jnp.arange(3 * BLOCK)[None, :]
    rel = kj - BLOCK - qi
    bias = jnp.transpose(t5_table[t5_bucket(rel)], (2, 0, 1)).astype(f32)
    bias = bias.reshape(GQA_KV_HEADS, GQA_GROUP, BLOCK, 3 * BLOCK)
    in_window = jnp.abs(rel) <= WINDOW
    sk = sink.astype(f32).reshape(GQA_KV_HEADS, GQA_GROUP)[None, :, :, None, None]
    scale = GQA_HEAD_DIM ** -0.5

    def one_block(args):
        q_n, n = args
        start = n * BLOCK
        k_n = lax.dynamic_slice_in_dim(kp, start, 3 * BLOCK, axis=1)
        v_n = lax.dynamic_slice_in_dim(vp, start, 3 * BLOCK, axis=1)
        kpos = start + kj - BLOCK
        valid = in_window & (kpos >= 0) & (kpos < seq)
        s = jnp.einsum('bqgrd,bkgd->bgrqk', q_n, k_n).astype(f32) * scale + bias[None]
        s = jnp.where(valid, s, NEG_INF)
        m = jnp.maximum(jnp.max(s, axis=-1, keepdims=True), sk)
        p = jnp.exp(s - m)
        denom = jnp.sum(p, axis=-1, keepdims=True) + jnp.exp(sk - m)
        return jnp.einsum('bgrqk,bkgd->bqgrd', (p / denom).astype(v.dtype), v_n)

    out = lax.map(one_block, (qb, jnp.arange(nb)))
    return jnp.moveaxis(out, 0, 1).reshape(bsz, seq, ODD_MIX)


def trunk(x, norm_ffn, w_ffn_gate, w_ffn_up, w_ffn_down, norm_mix, w_in_even,
          s5_lam_re, s5_lam_im, s5_log_dt, s5_b_re, s5_b_im, s5_c_re, s5_c_im,
          s5_d, s5_w_glu, s5_b_glu, na_rpb, w_out_even, w_in_odd, gqa_sink, w_out_odd,
          t5_table, norm_final):
    bsz, seq = x.shape[0], x.shape[1]
    for layer in range(DEPTH):
        i = layer // 2
        x = x + 0.5 * swiglu(rms_norm(x, norm_ffn[layer, 0]), w_ffn_gate[layer, 0],
                             w_ffn_up[layer, 0], w_ffn_down[layer, 0])
        hn = rms_norm(x, norm_mix[layer])
        if layer % 2 == 0:
            z = hn @ w_in_even[i]
            u = z[..., :S5_WIDTH]
            q, k, v = jnp.split(z[..., S5_WIDTH:], 3, axis=-1)
            q = q.reshape(bsz, seq, NA_HEADS, NA_HEAD_DIM)
            k = k.reshape(bsz, seq, NA_HEADS, NA_HEAD_DIM)
            v = v.reshape(bsz, seq, NA_HEADS, NA_HEAD_DIM)
            y_a = s5_mixer(u, s5_lam_re[i], s5_lam_im[i], s5_log_dt[i], s5_b_re[i], s5_b_im[i],
                           s5_c_re[i], s5_c_im[i], s5_d[i], s5_w_glu[i], s5_b_glu[i])
            y_b = neighbourhood_attention(q, k, v, na_rpb[i])
            x = x + jnp.concatenate([y_a, y_b], axis=-1) @ w_out_even[i]
        else:
            z = hn @ w_in_odd[i]
            nq = GQA_HEADS * GQA_HEAD_DIM
            nkv = GQA_KV_HEADS * GQA_HEAD_DIM
            q = z[..., :nq].reshape(bsz, seq, GQA_HEADS, GQA_HEAD_DIM)
            k = z[..., nq:nq + nkv].reshape(bsz, seq, GQA_KV_HEADS, GQA_HEAD_DIM)
            v = z[..., nq + nkv:].reshape(bsz, seq, GQA_KV_HEADS, GQA_HEAD_DIM)
            x = x + windowed_gqa(q, k, v, gqa_sink[i], t5_table) @ w_out_odd[i]
        x = x + 0.5 * swiglu(rms_norm(x, norm_ffn[layer, 1]), w_ffn_gate[layer, 1],
                             w_ffn_up[layer, 1], w_ffn_down[layer, 1])
    return rms_norm(x, norm_final)


def setup_inputs(seed: int = 0) -> dict:
    key = jax.random.key(seed)
    ks = jax.random.split(key, 32)
    f = jnp.float32

    def nrm(k, shape, scale):
        return jax.random.normal(k, shape, f) * scale

    lam_im_base = jnp.pi * jnp.arange(S5_STATE, dtype=f)
    return {
        'x_prompt': nrm(ks[0], (BATCH, SEQ, D_MODEL), 1.0),
        'x_sample': nrm(ks[1], (DEC_BATCH, DEC_SEQ, D_MODEL), 1.0),
        'norm_ffn': 1.0 + nrm(ks[2], (DEPTH, 2, D_MODEL), 0.02),
        'w_ffn_gate': nrm(ks[3], (DEPTH, 2, D_MODEL, D_FF), D_MODEL ** -0.5),
        'w_ffn_up': nrm(ks[4], (DEPTH, 2, D_MODEL, D_FF), D_MODEL ** -0.5),
        'w_ffn_down': nrm(ks[5], (DEPTH, 2, D_FF, D_MODEL), D_FF ** -0.5),
        'norm_mix': 1.0 + nrm(ks[6], (DEPTH, D_MODEL), 0.02),
        'w_in_even': nrm(ks[7], (N_EVEN, D_MODEL, EVEN_IN), D_MODEL ** -0.5),
        's5_lam_re': -0.5 + nrm(ks[8], (N_EVEN, 2, S5_GROUPS, S5_STATE), 0.01),
        's5_lam_im': lam_im_base + nrm(ks[9], (N_EVEN, 2, S5_GROUPS, S5_STATE), 0.01),
        's5_log_dt': jax.random.uniform(ks[10], (N_EVEN, 2, S5_GROUPS), f,
                                        minval=math.log(DT_MIN), maxval=math.log(DT_MAX)),
        's5_b_re': nrm(ks[11], (N_EVEN, 2, S5_GROUPS, S5_STATE, S5_GROUP), S5_GROUP ** -0.5),
        's5_b_im': nrm(ks[12], (N_EVEN, 2, S5_GROUPS, S5_STATE, S5_GROUP), S5_GROUP ** -0.5),
        's5_c_re': nrm(ks[13], (N_EVEN, 2, S5_GROUPS, S5_GROUP, S5_STATE), S5_STATE ** -0.5),
        's5_c_im': nrm(ks[14], (N_EVEN, 2, S5_GROUPS, S5_GROUP, S5_STATE), S5_STATE ** -0.5),
        's5_d': nrm(ks[15], (N_EVEN, S5_WIDTH), 1.0),
        's5_w_glu': nrm(ks[16], (N_EVEN, S5_WIDTH, S5_WIDTH), S5_WIDTH ** -0.5),
        's5_b_glu': nrm(ks[17], (N_EVEN, S5_WIDTH), 0.01),
        'na_rpb': nrm(ks[18], (N_EVEN, NA_HEADS, 2 * NA_ROWS_MAX - 1, 2 * NA_COLS - 1), 0.1),
        'w_out_even': nrm(ks[19], (N_EVEN, EVEN_MIX, D_MODEL), EVEN_MIX ** -0.5),
        'w_in_odd': nrm(ks[20], (N_ODD, D_MODEL, ODD_IN), D_MODEL ** -0.5),
        'gqa_sink': nrm(ks[21], (N_ODD, GQA_HEADS), 0.5),
        'w_out_odd': nrm(ks[22], (N_ODD, ODD_MIX, D_MODEL), ODD_MIX ** -0.5),
        't5_table': nrm(ks[23], (T5_BUCKETS, GQA_HEADS), 0.5),
        'norm_final': 1.0 + nrm(ks[24], (D_MODEL,), 0.02),
    }


def reference(x_prompt, x_sample, norm_ffn, w_ffn_gate, w_ffn_up, w_ffn_down, norm_mix, w_in_even,
              s5_lam_re, s5_lam_im, s5_log_dt, s5_b_re, s5_b_im, s5_c_re, s5_c_im,
              s5_d, s5_w_glu, s5_b_glu, na_rpb, w_out_even, w_in_odd, gqa_sink, w_out_odd,
              t5_table, norm_final):
    y_prompt = trunk(x_prompt, norm_ffn, w_ffn_gate, w_ffn_up, w_ffn_down, norm_mix, w_in_even,
                     s5_lam_re, s5_lam_im, s5_log_dt, s5_b_re, s5_b_im, s5_c_re, s5_c_im,
                     s5_d, s5_w_glu, s5_b_glu, na_rpb, w_out_even, w_in_odd, gqa_sink, w_out_odd,
                     t5_table, norm_final)
    y_sample = trunk(x_sample, norm_ffn, w_ffn_gate, w_ffn_up, w_ffn_down, norm_mix, w_in_even,
                     s5_lam_re, s5_lam_im, s5_log_dt, s5_b_re, s5_b_im, s5_c_re, s5_c_im,
                     s5_d, s5_w_glu, s5_b_glu, na_rpb, w_out_even, w_in_odd, gqa_sink, w_out_odd,
                     t5_table, norm_final)
    return (y_prompt, y_sample)
```

```python
import math
from contextlib import ExitStack
import numpy as np
import ml_dtypes
import concourse.bass as bass
import concourse.mybir as mybir
from concourse.bass_utils import run_bass_kernel_spmd

F32 = mybir.dt.float32
BF16 = mybir.dt.bfloat16
AF = mybir.ActivationFunctionType
ALU = mybir.AluOpType
NPBF = ml_dtypes.bfloat16

D = 1024
DFF = 2816
NFC = 22
EPS = 1e-6
EPOCH_MAX = 12000


def AP(t, off, dims):
    return bass.AP(t, off, [list(d) for d in dims])


class Prog:
    def __init__(self, nc):
        self.nc = nc
        self.ops = []
        self.barriers = []

    def add(self, eng, fn, r=(), w=(), dma=None):
        self.ops.append((eng, fn, tuple(r), tuple(w), dma))

    def barrier(self):
        self.barriers.append(len(self.ops))

    def emit(self, final_wait_keys=()):
        nc = self.nc
        ops = self.ops
        n = len(ops)
        engs = ("pe", "act", "dve", "pool", "sp")
        lastw, lastr = {}, {}
        deps = [None] * n
        bar_set = set(self.barriers)
        last_on_eng = {}
        last_dma = {}
        pending_bar = {}
        prev_same = {}
        for i, (eng, fn, r, w, dma) in enumerate(ops):
            if i in bar_set:
                allprev = list(last_on_eng.values()) + list(last_dma.values())
                for e in engs:
                    pending_bar[e] = list(allprev)
            me = ("dma", dma) if dma else eng
            d = set()
            for k in r:
                d.update(lastw.get(k, {}).values())
            for k in w:
                d.update(lastw.get(k, {}).values())
                d.update(lastr.get(k, {}).values())
            if eng in pending_bar:
                d.update(pending_bar.pop(eng))
            if dma and dma in last_dma:
                d.add(last_dma[dma])
            dd = []
            for j in d:
                ej, _, _, _, dj = ops[j]
                if dj is None and dma is None and ej == eng and (eng == "pe" or j == prev_same.get(eng, -1) and False):
                    continue
                dd.append(j)
            deps[i] = dd
            for k in r:
                lastr.setdefault(k, {})[me] = i
            for k in w:
                lastw.setdefault(k, {})[me] = i
            if dma:
                last_dma[dma] = i
            else:
                last_on_eng[eng] = i
        signalled = [False] * n
        for i in range(n):
            for j in deps[i]:
                signalled[j] = True
        cnt = {e: 0 for e in engs}
        epoch = {e: 0 for e in engs}
        dcnt = {}
        semof = [None] * n
        valof = [0] * n
        for i, (eng, fn, r, w, dma) in enumerate(ops):
            if dma:
                dcnt[dma] = dcnt.get(dma, 0) + 16
                semof[i] = ("dma", dma)
                valof[i] = dcnt[dma]
            elif signalled[i]:
                if cnt[eng] >= EPOCH_MAX:
                    epoch[eng] += 1
                    cnt[eng] = 0
                cnt[eng] += 1
                semof[i] = (eng, epoch[eng])
                valof[i] = cnt[eng]
        semnames = sorted(set(s for s in semof if s is not None), key=str)
        self.n_sems = len(semnames)
        with ExitStack() as es:
            sems = {}
            for sname in semnames:
                nm = "s_" + "_".join(str(x) for x in sname)
                sems[sname] = es.enter_context(nc.semaphore(nm))
            block = es.enter_context(nc.Block())
            per_eng = {e: [] for e in engs}
            for i, op in enumerate(ops):
                per_eng[op[0]].append(i)

            def run_engine(e_name, eobj):
                waited = {}
                for i in per_eng[e_name]:
                    eng, fn, r, w, dma = ops[i]
                    need = {}
                    for j in deps[i]:
                        sj = semof[j]
                        if valof[j] > need.get(sj, 0):
                            need[sj] = valof[j]
                    for sj, v in need.items():
                        if waited.get(sj, 0) >= v:
                            continue
                        waited[sj] = v
                        eobj.wait_ge(sems[sj], v)
                    ins = fn(eobj)
                    if dma:
                        ins.then_inc(sems[semof[i]], 16)
                    elif signalled[i]:
                        ins.then_inc(sems[semof[i]], 1)
                if e_name == "sp":
                    for k in final_wait_keys:
                        if ("dma", k) in sems and dcnt.get(k, 0) > 0:
                            eobj.wait_ge(sems[("dma", k)], dcnt[k])

            @block.tensor
            def _(e):
                run_engine("pe", e)

            @block.scalar
            def _(e):
                run_engine("act", e)

            @block.vector
            def _(e):
                run_engine("dve", e)

            @block.gpsimd
            def _(e):
                run_engine("pool", e)

            @block.sync
            def _(e):
                run_engine("sp", e)


GQA_PAIRS = [(0, 4), (1, 5), (2, 6), (3, 7), (8, 12), (9, 13), (10, 14), (11, 15)]
NA_CLASSES = ["first", "second", "interior", "slast", "last"]
NA_OFFS = {"first": [0, 1, 2, 3], "second": [-1, 0, 1, 2], "interior": [-2, -1, 0, 1, 2],
           "slast": [-2, -1, 0, 1], "last": [-3, -2, -1, 0]}
NA_PBASE = {"first": 0, "second": 4, "interior": 8, "slast": 13, "last": 17}


def _t5_bucket_np(rel):
    half, max_exact = 16, 8
    ret = np.where(rel > 0, half, 0)
    n = np.abs(rel)
    nf = np.maximum(n, 1).astype(np.float32)
    large = max_exact + (np.log(nf / np.float32(max_exact)) / np.float32(math.log(128 / max_exact))
                         * np.float32(half - max_exact)).astype(np.int32)
    large = np.minimum(large, half - 1)
    return ret + np.where(n < max_exact, n, large)


def na_pattern(cls):
    R = 64
    n = {"first": 0, "second": 1, "interior": 10, "slast": 30, "last": 31}[cls]
    out = []
    for off in NA_OFFS[cls]:
        m = n + off
        blk = [[None, None], [None, None]]
        for kr in range(2):
            for qr in range(2):
                r = 2 * n + qr
                krow = 2 * m + kr
                rs = min(max(r - 4, 0), R - 8)
                if rs <= krow < rs + 8:
                    blk[kr][qr] = krow - r + 7
        out.append(blk)
    return out


def host_constants():
    c = {}
    c["c_ident_bf"] = np.eye(128, dtype=np.float32).astype(NPBF)
    c["c_ident_f"] = np.eye(128, dtype=np.float32)
    sel = np.zeros((128, 64, 128), np.float32)
    selT = np.zeros((128, 64, 128), np.float32)
    for g8 in range(8):
        for jl in range(8):
            for cc in range(16):
                sel[16 * g8 + cc, g8 * 8 + jl, 16 * jl + cc] = 1.0
                selT[16 * jl + cc, g8 * 8 + jl, 16 * g8 + cc] = 1.0
    c["c_sel"] = sel.astype(NPBF)
    c["c_selT"] = selT.astype(NPBF)
    jl = np.arange(128) // 16
    msk = np.zeros((128, 2, 128), np.float32)
    msk[:, 0, :] = (jl[:, None] <= jl[None, :])
    msk[:, 1, :] = (jl[:, None] >= jl[None, :])
    c["c_mask"] = msk
    j = np.arange(32, dtype=np.float32)
    ex = np.zeros((64, 3, 2, 32), np.float32)
    ex[:, 0, 0, :] = 31 - j
    ex[:, 0, 1, :] = j
    ex[:, 1, 0, :] = -1 - j
    ex[:, 1, 1, :] = j - 32
    ex[:, 2, 0, :] = j + 1
    ex[:, 2, 1, :] = 32 - j
    c["c_expo"] = ex
    oh = np.zeros((31, 64, 128), np.float32)
    for qc in range(64):
        cs = min(max(qc - 8, 0), 48)
        for kc in range(cs, cs + 16):
            rc = kc - qc + 15
            oh[rc, qc, kc] = 1.0
            oh[rc, qc, 64 + kc] = 1.0
    c["c_ohc"] = oh.astype(NPBF)
    rp = np.arange(512) - 256
    bk = _t5_bucket_np(rp)
    og = np.zeros((32, 512), np.float32)
    for i in range(512):
        if abs(rp[i]) <= 128:
            og[bk[i], i] = 1.0
    c["c_ohg"] = og.astype(NPBF)
    return c


WEIGHT_SPECS = [
    ("norm_ffn", [2, 2, 1024]), ("w_ffn_gate", [2, 2, 1024, 2816]), ("w_ffn_up", [2, 2, 1024, 2816]),
    ("w_ffn_down", [2, 2, 2816, 1024]), ("norm_mix", [2, 1024]), ("w_in_even", [1, 1024, 2048]),
    ("s5_lam_re", [1, 2, 32, 64]), ("s5_lam_im", [1, 2, 32, 64]), ("s5_log_dt", [1, 2, 32]),
    ("s5_b_re", [1, 2, 32, 64, 16]), ("s5_b_im", [1, 2, 32, 64, 16]), ("s5_c_re", [1, 2, 32, 16, 64]),
    ("s5_c_im", [1, 2, 32, 16, 64]), ("s5_d", [1, 512]), ("s5_w_glu", [1, 512, 512]), ("s5_b_glu", [1, 512]),
    ("na_rpb", [1, 8, 15, 31]), ("w_out_even", [1, 1024, 1024]), ("w_in_odd", [1, 1024, 1536]),
    ("gqa_sink", [1, 16]), ("w_out_odd", [1, 1024, 1024]), ("t5_table", [32, 16]), ("norm_final", [1024]),
]
CONST_SPECS = [("c_ident_bf", [128, 128], BF16), ("c_ident_f", [128, 128], F32), ("c_sel", [128, 64, 128], BF16),
               ("c_selT", [128, 64, 128], BF16), ("c_mask", [128, 2, 128], F32), ("c_expo", [64, 3, 2, 32], F32),
               ("c_ohc", [31, 64, 128], BF16), ("c_ohg", [32, 512], BF16)]

DTSIZE = {F32: 4, BF16: 2}


class TT:
    def __init__(self, h, shape, dtype):
        self.h = h
        self.shape = list(shape)
        self.dtype = dtype
        self.row = int(np.prod(shape[1:]))
        st = []
        s = 1
        for d in reversed(shape[1:]):
            st.append(s)
            s *= d
        self.strides = list(reversed(st))

    def ap(self, off, dims, p0=0, pn=None):
        if pn is None:
            pn = self.shape[0] - p0
        return AP(self.h, p0 * self.row + off, [[self.row, pn]] + [list(d) for d in dims])

    def __getitem__(self, idx):
        return self.h[idx]


class DT:
    def __init__(self, h, shape):
        self.h = h
        self.shape = list(shape)

    def ap(self, off, dims):
        return AP(self.h, off, [list(d) for d in dims])


class SBAlloc:
    def __init__(self, nc, base=16512, limit=229376 - 64):
        self.nc = nc
        self.ptr = base
        self.limit = limit
        self.n = 0
        self.hi = base

    def alloc(self, name, shape, dtype):
        size = int(np.prod(shape[1:])) * DTSIZE[dtype]
        size = (size + 63) // 64 * 64
        assert self.ptr + size <= self.limit, f"SBUF overflow at {name}: {self.ptr}+{size}"
        self.n += 1
        h = self.nc.alloc_sbuf_tensor_at(f"{name}_{self.n}", list(shape), dtype, offset=self.ptr)
        self.ptr += size
        self.hi = max(self.hi, self.ptr)
        return TT(h, shape, dtype)


class MK:
    def __init__(self, seq_lens, dbg=False, phases="ABCD"):
        self.seq_lens = list(seq_lens)
        self.NTOK = sum(seq_lens)
        self.LMAX = max(seq_lens)
        self.dbg = dbg
        self.phases = phases
        self.nc = nc = bass.Bass("TRN2", target_bir_lowering=False)
        self.P = Prog(nc)
        self.rr = 0
        self.bankc = 0
        self.out_keys = []
        self.store_eng = "sp"
        self.io()
        self.alloc()

    def dram(self, name, shape, dtype, kind="Internal"):
        h = self.nc.dram_tensor(name, list(shape), dtype, kind=kind)
        return DT(h, shape)

    def io(self):
        W = {}
        W["x_in"] = self.dram("x_in", [self.NTOK, D], F32, "ExternalInput")
        self.y_out = self.dram("y_out", [self.NTOK, D], F32, "ExternalOutput")
        for nm, shp in WEIGHT_SPECS:
            W[nm] = self.dram(nm, shp, F32, "ExternalInput")
        for nm, shp, dt_ in CONST_SPECS:
            W[nm] = self.dram(nm, shp, dt_, "ExternalInput")
        self.W = W
        L = self.LMAX
        kd = "ExternalOutput" if self.dbg else "Internal"
        S = {}
        S["Wgu"] = self.dram("s_Wgu", [4, 22, 128, 2048], BF16)
        S["Wd"] = self.dram("s_Wd", [4, 2, 22, 128, 512], BF16)
        S["WinE"] = self.dram("s_WinE", [4, 128, 8, 512], BF16)
        S["WoutE"] = self.dram("s_WoutE", [2, 128, 8, 512], BF16)
        S["WinO"] = self.dram("s_WinO", [3, 128, 8, 512], BF16)
        S["WoutO"] = self.dram("s_WoutO", [2, 128, 8, 512], BF16)
        S["Wglu"] = self.dram("s_Wglu", [128, 4, 512], BF16)
        S["Ws5"] = self.dram("s_Ws5", [2, 32, 128, 4, 192], BF16, kd)
        S["QT"] = self.dram("s_QT", [2, 32, 64, 2, 512], BF16, kd)
        S["M"] = self.dram("s_M", [32, 128, 7, 128], BF16, kd)
        S["TNA"] = self.dram("s_TNA", [128, 8 * 21 * 128], BF16)
        S["TGQ"] = self.dram("s_TGQ", [128, 16 * 3 * 128], BF16)
        S["x"] = self.dram("s_x", [L, D], F32, kd)
        S["u"] = self.dram("s_u", [4, 128, L], BF16, kd)
        S["q"] = self.dram("s_q", [4, 128, L], BF16, kd)
        S["k"] = self.dram("s_k", [4, 128, L], BF16, kd)
        S["v"] = self.dram("s_v", [L, 528], BF16, kd)
        S["ya"] = self.dram("s_ya", [4, 128, L], BF16, kd)
        S["g"] = self.dram("s_g", [4, 128, L], BF16, kd)
        S["q2"] = self.dram("s_q2", [8, 128, L], BF16, kd)
        S["k2"] = self.dram("s_k2", [2, 128, L], BF16, kd)
        S["v2"] = self.dram("s_v2", [L, 264], BF16, kd)
        self.S = S

    def alloc(self):
        nc = self.nc
        sb = self.sb = SBAlloc(nc)
        a = sb.alloc
        self.identb = a("identb", [128, 128], BF16)
        self.identf = a("identf", [128, 128], F32)
        self.G = a("G", [128, 6, 8], F32)
        self.Gfin = a("Gfin", [128, 1024], F32)
        self.es = a("es", [128, 16], F32)
        self.cm05 = a("cm05", [128, 2], F32)
        self.st = a("st", [128, 8, 4], F32)
        self.A32 = a("A32", [64, 2, 2, 32], F32)
        self.Ac2 = a("Ac2", [64, 2, 2, 32], F32)
        self.bglu = a("bglu", [128, 4], F32)
        self.rec = a("rec", [128, 16], F32)
        self.halfpi = a("halfpi", [128, 2], F32)
        self.psum = TT(nc.alloc_psum_tensor("psum", [128, 4096], F32), [128, 4096], F32)
        self.arena0 = sb.ptr

    def ps(self, bank, off=0, n=512, p0=0, pn=128):
        return self.psum.ap(bank * 512 + off, [[1, n]], p0=p0, pn=pn)

    def psv(self, bank, off, dims, p0=0, pn=128):
        return self.psum.ap(bank * 512 + off, dims, p0=p0, pn=pn)

    def psb(self, bank, off=0, n=1024, p0=0, pn=128):
        full = self.psum.ap(bank * 512, [[1, 512]], p0=p0, pn=pn).bitcast(BF16)
        return full[:, off:off + n]

    def pk(self, bank):
        return ("ps", bank)

    def nbank(self):
        b = self.bankc % 8
        self.bankc += 1
        return b

    def ew(self):
        self.rr += 1
        return "act" if self.rr % 2 else "dve"

    def copy(self, eng, out, in_, r, w):
        if eng == "act":
            self.P.add("act", lambda e: e.copy(out=out, in_=in_), r=r, w=w)
        elif eng == "dve":
            self.P.add("dve", lambda e: e.tensor_copy(out=out, in_=in_), r=r, w=w)
        else:
            self.P.add("pool", lambda e: e.tensor_copy(out=out, in_=in_), r=r, w=w)

    def load(self, out, in_, r, w, key):
        self.P.add("sp", lambda e: e.dma_start(out=out, in_=in_), r=r, w=w, dma=key)

    def store(self, out, in_, r, w, key):
        self.P.add(self.store_eng, lambda e: e.dma_start(out=out, in_=in_), r=r, w=w, dma=key)

    def mm(self, out, lhsT, rhs, start, stop, r, w):
        self.P.add("pe", lambda e: e.matmul(out, lhsT=lhsT, rhs=rhs, start=start, stop=stop), r=r, w=w)

    def tr(self, out, in_, ident, r, w):
        self.P.add("pe", lambda e: e.transpose(out=out, in_=in_, identity=ident), r=r, w=w)

    def tt(self, eng, out, in0, in1, op, r, w):
        self.P.add(eng, lambda e: e.tensor_tensor(out=out, in0=in0, in1=in1, op=op), r=r, w=w)

    def ts(self, eng, out, in0, s1, s2, op0, op1, r, w):
        if s2 is None:
            self.P.add(eng, lambda e: e.tensor_scalar(out=out, in0=in0, scalar1=s1, scalar2=None, op0=op0), r=r, w=w)
        else:
            self.P.add(eng, lambda e: e.tensor_scalar(out=out, in0=in0, scalar1=s1, scalar2=s2, op0=op0, op1=op1), r=r, w=w)

    def stt(self, out, in0, scalar, in1, op0, op1, r, w):
        self.P.add("dve", lambda e: e.scalar_tensor_tensor(out=out, in0=in0, scalar=scalar, in1=in1, op0=op0, op1=op1), r=r, w=w)

    def act(self, out, in_, func, r, w, scale=1.0, bias=0.0, accum=None):
        if accum is not None:
            self.P.add("act", lambda e: e.activation(out=out, in_=in_, func=func, scale=scale, bias=bias, accum_out=accum), r=r, w=w)
        else:
            self.P.add("act", lambda e: e.activation(out=out, in_=in_, func=func, scale=scale, bias=bias), r=r, w=w)

    def setup_consts(self):
        W = self.W
        P = self.P
        self.load(self.identb[:], W["c_ident_bf"].h.ap(), [], ["identb"], "c0")
        self.load(self.identf[:], W["c_ident_f"].h.ap(), [], ["identf"], "c0")
        P.add("pool", lambda e: e.memset(self.cm05[:, 0:1], -0.5), w=["cm05"])
        a = self.sb.alloc
        gt = a("gtmp", [52, 128], F32)
        self.load(gt.ap(0, [[1, 128]], p0=48, pn=4), W["s5_b_glu"].ap(0, [[128, 4], [1, 128]]), [], ["gt"], "c1")
        self.load(gt.ap(0, [[1, 128]], p0=0, pn=32), W["norm_ffn"].ap(0, [[128, 32], [1, 128]]), [], ["gt"], "c1")
        self.load(gt.ap(0, [[1, 128]], p0=32, pn=16), W["norm_mix"].ap(0, [[128, 16], [1, 128]]), [], ["gt"], "c1")
        P.add("pe", lambda e: e.matmul(self.ps(0, 0, 52), lhsT=gt.ap(0, [[1, 128]], pn=52), rhs=self.identf.ap(0, [[1, 52]], pn=52),
                                       start=True, stop=True), r=["gt", "identf"], w=[self.pk(0)])
        src_row = {0: 0, 2: 8, 3: 16, 5: 24, 1: 32, 4: 40}
        for gi, r0 in src_row.items():
            self.copy("dve", self.G.ap(gi * 8, [[1, 8]]), self.ps(0, r0, 8), [self.pk(0)], ["G"])
        self.load(self.Gfin[:], W["norm_final"].ap(0, [[0, 128], [1, 1024]]), [], ["Gfin"], "c1")
        self.load(self.es[:], W["gqa_sink"].ap(0, [[0, 128], [1, 16]]), [], ["es"], "c1")
        self.act(self.es[:], self.es[:], AF.Exp, ["es"], ["es"])
        self.copy("dve", self.bglu.ap(0, [[1, 4]]), self.ps(0, 48, 4), [self.pk(0)], ["bglu"])

    def conv_weights(self):
        sb = self.sb
        mark = sb.ptr
        sb.ptr = sb.limit - 49152 - 512
        cf = [sb.alloc(f"cf{i}", [128, 4096], F32) for i in range(2)]
        cb = [sb.alloc(f"cb{i}", [128, 4096], BF16) for i in range(2)]
        self.cj = 0
        self.conv_q = []

        def job(src_pieces, dst_pieces, n, key):
            self.conv_q.append(lambda: job_emit(src_pieces, dst_pieces, n, key))

        def job_emit(src_pieces, dst_pieces, n, key):
            s = self.cj % 2
            self.cj += 1
            for pi_, (sap, off, dims) in enumerate(src_pieces):
                self.P.add("pool", lambda e, o_=cf[s].ap(off, dims), i_=sap: e.dma_start(out=o_, in_=i_), r=[], w=[("cf", s)], dma=f"cf{s}p{pi_}")
            fl = [[1, n]]
            eng = "act"
            self.copy(eng, cb[s].ap(0, fl), cf[s].ap(0, fl), [("cf", s)], [("cb", s)])
            for pi_, (dap, off, dims) in enumerate(dst_pieces):
                self.P.add("pool", lambda e, dap=dap, off=off, dims=dims: e.dma_start(out=dap, in_=cb[s].ap(off, dims)),
                           r=[("cb", s)], w=[key], dma=f"cs{s}p{pi_}")

        W, S = self.W, self.S
        for f in range(4):
            for gu, nm in enumerate(["w_ffn_gate", "w_ffn_up"]):
                base = f * 1024 * 2816
                for fc0 in range(0, 22, 4):
                    nf = min(4, 22 - fc0)
                    sp, dp = [], []
                    for j in range(nf):
                        fc = fc0 + j
                        sp.append((W[nm].ap(base + fc * 128, [[2816, 128], [128 * 2816, 8], [1, 128]]), j * 1024, [[128, 8], [1, 128]]))
                        dp.append((S["Wgu"].ap(((f * 22 + fc) * 128) * 2048 + gu * 1024, [[2048, 128], [1, 1024]]), j * 1024, [[1, 1024]]))
                    job(sp, dp, nf * 1024, "Wgu")
            base = f * 2816 * 1024
            for half in range(2):
                for fc0 in range(0, 22, 8):
                    nf = min(8, 22 - fc0)
                    sap = W["w_ffn_down"].ap(base + fc0 * 128 * 1024 + half * 512, [[1024, 128], [128 * 1024, nf], [1, 512]])
                    dst = S["Wd"].ap((((f * 2 + half) * 22 + fc0) * 128) * 512, [[512, 128], [128 * 512, nf], [1, 512]])
                    job([(sap, 0, [[512, nf], [1, 512]])], [(dst, 0, [[512, nf], [1, 512]])], nf * 512, "Wd")

        def generic(nm, ncols, slabs, dstname):
            for si, c0 in enumerate(slabs):
                sap = W[nm].ap(c0, [[ncols, 128], [128 * ncols, 8], [1, 512]])
                dst = S[dstname].ap(si * 128 * 4096, [[4096, 128], [1, 4096]])
                job([(sap, 0, [[512, 8], [1, 512]])], [(dst, 0, [[1, 4096]])], 4096, dstname)

        generic("w_in_even", 2048, [0, 512, 1024, 1536], "WinE")
        generic("w_out_even", 1024, [0, 512], "WoutE")
        generic("w_out_odd", 1024, [0, 512], "WoutO")
        for si in range(2):
            pieces = []
            for t_ in range(4):
                for e_ in range(2):
                    h = GQA_PAIRS[si * 4 + t_][e_]
                    sap = W["w_in_odd"].ap(h * 64, [[1536, 128], [128 * 1536, 8], [1, 64]])
                    pieces.append((sap, t_ * 128 + e_ * 64, [[512, 8], [1, 64]]))
            dst = S["WinO"].ap(si * 128 * 4096, [[4096, 128], [1, 4096]])
            job(pieces, [(dst, 0, [[1, 4096]])], 4096, "WinO")
        sap = W["w_in_odd"].ap(1024, [[1536, 128], [128 * 1536, 8], [1, 512]])
        dst = S["WinO"].ap(2 * 128 * 4096, [[4096, 128], [1, 4096]])
        job([(sap, 0, [[512, 8], [1, 512]])], [(dst, 0, [[1, 4096]])], 4096, "WinO")
        sap = W["s5_w_glu"].ap(0, [[512, 128], [128 * 512, 4], [1, 512]])
        dst = S["Wglu"].ap(0, [[2048, 128], [1, 2048]])
        job([(sap, 0, [[512, 4], [1, 512]])], [(dst, 0, [[1, 2048]])], 2048, "Wglu")
        sb.ptr = mark

    def pump(self, n):
        for _ in range(n):
            if self.conv_q:
                self.conv_q.pop(0)()


    def alloc_ffn(self, NB, vkind=None):
        sb = self.sb
        sb.ptr = self.arena0
        a = sb.alloc
        self.NB = NB
        NT = NB * 128
        self.NT = NT
        self.xts = [a(f"xt{i}", [128, NB, 1024], F32) for i in range(2)]
        self.xt, self.xpar = self.xts[0], 0
        self.xs = [a(f"xs{i}", [128, 1024], BF16) for i in range(2)]
        self.junk = a("junk", [128, 1024], BF16)
        self.hn = a("hn", [128, 8, NT], BF16)
        self.h = a("h", [128, 22, NT], BF16)
        self.sg = [a(f"sg{i}", [128, 512], BF16) for i in range(2)]
        self.NGU = 3
        self.gu = [a(f"gu{i}", [128, 2, 8, 128], BF16) for i in range(self.NGU)]
        self.ND = 24
        self.dr = [a(f"dr{i}", [128, 512], BF16) for i in range(self.ND)]
        self.NPW = 2
        self.pw = [a(f"pw{i}", [128, 8, 512], BF16) for i in range(self.NPW)]
        self.stg = [a(f"stg{i}", [128, 2, NT], BF16) for i in range(2)]
        if vkind == "v":
            self.vst = a("vst", [128, NB, 8, 66], BF16)
        if vkind == "v2":
            self.v2st = a("v2st", [128, NB, 4, 66], BF16)
        self.cnt_gu = self.cnt_d = self.cnt_pw = self.cnt_stg = 0
        self.dgrp = 0
        P = self.P
        if vkind == "v":
            va = self.vst.ap(64, [[66, NB * 8], [1, 2]])
            P.add("pool", lambda e, va=va: e.memset(va, 1.0), w=["vst"])
        if vkind == "v2":
            vb = self.v2st.ap(64, [[66, NB * 4], [1, 2]])
            P.add("pool", lambda e, vb=vb: e.memset(vb, 1.0), w=["v2st"])

    def alloc_attn(self, kind):
        sb = self.sb
        sb.ptr = self.arena0
        a = sb.alloc
        self.NB = 4
        self.NT = 512
        self.xts = [a(f"xt{i}", [128, 4, 1024], F32) for i in range(2)]
        self.qTs = [a(f"qT{i}", [128, 8, 512], BF16) for i in range(2)]
        self.kTs = [a(f"kT{i}", [128, 4, 1024], BF16) for i in range(2)]
        self.vTs = [a(f"vT{i}", [128, 8, 528], BF16) for i in range(2)]
        self.yaTs = [a(f"yaT{i}", [128, 4, 512], BF16) for i in range(2)]
        self.set_par(0)
        self.mix = a("mix", [128, 8, 512], BF16)
        self.E = [a(f"E{i}", [128, 640], F32) for i in range(4)]
        self.Pm = [a(f"Pm{i}", [128, 640], BF16) for i in range(4)]
        self.yb = [a(f"yb{i}", [128, 1024], BF16) for i in range(2)]
        self.dsum = a("dsum", [128, 16], F32)
        self.NPW = 2
        self.pw = [a(f"pw{i}", [128, 8, 512], BF16) for i in range(self.NPW)]
        self.cnt_pw = 0
        n = 8 * 21 * 128 if kind == "na" else 16 * 3 * 128
        self.TAB = a("TAB", [128, n], BF16)
        self.load(self.TAB.ap(0, [[1, n]]), self.S["TNA" if kind == "na" else "TGQ"].ap(0, [[n, 128], [1, n]]),
                  ["TABscr"], ["TAB"], "tabld")
        for par in range(2):
            zt = self.qTs[par] if kind == "na" else self.kTs[par]
            za = zt.ap(0, [[1, 4096]])
            self.P.add("pool", lambda e, za=za: e.memset(za, 0.0), w=[("qT", par), ("kT", par)])

    def set_par(self, par):
        self.xpar = par
        self.xt = self.xts[par]
        if hasattr(self, "qTs"):
            self.qT, self.kT, self.vT, self.yaT = self.qTs[par], self.kTs[par], self.vTs[par], self.yaTs[par]

    def stats(self, blk):
        st, x = self.st, self.xt
        self.act(self.junk[:], x.ap(blk * 1024, [[1, 1024]]), AF.Square, [("x", self.xpar, blk)], ["junk", ("st", blk)],
                 accum=st.ap(blk * 4, [[1, 1]]))
        self.ts("dve", st.ap(blk * 4 + 1, [[1, 1]]), st.ap(blk * 4, [[1, 1]]), 1.0 / 1024, EPS, ALU.mult, ALU.add,
                [("st", blk)], [("st", blk)])
        self.tt("pool", st.ap(blk * 4 + 2, [[1, 1]]), st.ap(blk * 4 + 1, [[1, 1]]), self.cm05.ap(0, [[1, 1]]), ALU.pow,
                [("st", blk), "cm05"], [("st", blk)])

    def norm_fm(self, gidx):
        x, NT = self.xt, self.NT
        for blk in range(self.NB):
            self.stats(blk)
        for blk in range(self.NB):
            xs = self.xs[blk % 2]
            self.act(xs[:], x.ap(blk * 1024, [[1, 1024]]), AF.Copy, [("x", self.xpar, blk), ("st", blk)], [("xs", blk % 2)],
                     scale=self.st.ap(blk * 4 + 2, [[1, 1]]))
            bank = blk % 2
            for kc in range(8):
                self.tr(self.psb(bank, kc * 128, 128), xs.ap(kc * 128, [[1, 128]]), self.identb[:],
                        [("xs", blk % 2), "identb"], [self.pk(bank)])
            self.tt("dve", self.hn.ap(blk * 128, [[NT, 8], [1, 128]]),
                    self.psb(bank).rearrange("p (k t) -> p k t", k=8),
                    self.G.ap(gidx * 8, [[1, 8], [0, 128]]), ALU.mult, [self.pk(bank), "G"], [("hn", blk // 4)])

    def ffn(self, f, gidx):
        S, x, NT = self.S, self.xt, self.NT
        NH = self.NB // 4
        self.norm_fm(gidx)
        c = 0
        for fc in range(NFC):
            s = self.cnt_gu % self.NGU
            self.cnt_gu += 1
            gut = self.gu[s]
            self.load(gut.ap(0, [[1, 2048]]), S["Wgu"].ap((f * 22 + fc) * 128 * 2048, [[2048, 128], [1, 2048]]),
                      ["Wgu"], [("gu", s)], f"gu{s}")
            for th in range(NH):
                st_ = c % 2
                c += 1
                gb, ub = 2 * st_, 2 * st_ + 1
                for kc in range(8):
                    self.mm(self.ps(gb), gut.ap(kc * 128, [[1, 128]]), self.hn.ap(kc * NT + th * 512, [[1, 512]]), kc == 0, kc == 7,
                            [("gu", s), ("hn", th)], [self.pk(gb)])
                for kc in range(8):
                    self.mm(self.ps(ub), gut.ap(1024 + kc * 128, [[1, 128]]), self.hn.ap(kc * NT + th * 512, [[1, 512]]), kc == 0, kc == 7,
                            [("gu", s), ("hn", th)], [self.pk(ub)])
                self.act(self.sg[st_][:], self.ps(gb), AF.Silu, [self.pk(gb)], [("sg", st_)])
                self.tt("dve", self.h.ap(fc * NT + th * 512, [[1, 512]]), self.ps(ub), self.sg[st_][:], ALU.mult,
                        [self.pk(ub), ("sg", st_)], [("h", fc, th)])
        for half in range(2):
            slots = []
            for fc in range(NFC):
                s = self.cnt_d % self.ND
                self.cnt_d += 1
                slots.append(s)
                self.load(self.dr[s][:], S["Wd"].ap(((f * 2 + half) * 22 + fc) * 128 * 512, [[512, 128], [1, 512]]),
                          ["Wd"], [("dr", s)], f"dr{s % 6}")
            for pg in range(self.NB // 2):
                bk = (4, 5) if self.dgrp % 2 == 0 else (6, 7)
                self.dgrp += 1
                for fc in range(NFC):
                    s = slots[fc]
                    for j in range(2):
                        blk = pg * 2 + j
                        self.mm(self.ps(bk[j]), self.h.ap(fc * NT + blk * 128, [[1, 128]]), self.dr[s][:],
                                fc == 0, fc == NFC - 1, [("h", fc, blk // 4), ("dr", s)], [self.pk(bk[j])])
                for j in range(2):
                    blk = pg * 2 + j
                    xa = x.ap(blk * 1024 + half * 512, [[1, 512]])
                    self.stt(xa, self.ps(bk[j]), 0.5, xa, ALU.mult, ALU.add, [self.pk(bk[j]), ("x", self.xpar, blk)], [("x", self.xpar, blk)])

    def pw_load(self, wname, slab):
        s = self.cnt_pw % self.NPW
        self.cnt_pw += 1
        self.load(self.pw[s].ap(0, [[1, 4096]]), self.S[wname].ap(slab * 128 * 4096, [[4096, 128], [1, 4096]]),
                  [wname], [("pw", s)], f"pw{s}")
        return s

    def proj_fm(self, s, ots, dst_name, dst_tile0, tok, dkeys):
        NT = self.NT
        LM = self.LMAX
        for c0 in range(0, len(ots), 2):
            grp = ots[c0:c0 + 2]
            g = self.cnt_stg % 2
            self.cnt_stg += 1
            stg = self.stg[g]
            for i, ot in enumerate(grp):
                for th in range(NT // 512):
                    bank = self.nbank() % 4
                    for kc in range(8):
                        self.mm(self.ps(bank), self.pw[s].ap(kc * 512 + ot * 128, [[1, 128]]), self.hn.ap(kc * NT + th * 512, [[1, 512]]),
                                kc == 0, kc == 7, [("pw", s), ("hn", th)], [self.pk(bank)])
                    self.copy(self.ew(), stg.ap(i * NT + th * 512, [[1, 512]]), self.ps(bank), [self.pk(bank)], [("stg", g)])
            n = len(grp)
            self.store(self.S[dst_name].ap((dst_tile0 + c0) * 128 * LM + tok, [[LM, 128], [128 * LM, n], [1, NT]]),
                       stg.ap(0, [[NT, n], [1, NT]]), [("stg", g)], dkeys, f"stg{g}")

    def proj_tm_v(self, s, col0, nh, vst, vkey, dst_name, rowlen, tok, dkeys):
        ncol = nh * 64
        NB, NT = self.NB, self.NT
        for blk in range(NB):
            bank = 4 + blk % 4
            for kc in range(8):
                self.mm(self.ps(bank, 0, ncol), self.hn.ap(kc * NT + blk * 128, [[1, 128]]),
                        self.pw[s].ap(kc * 512 + col0, [[1, ncol]]), kc == 0, kc == 7, [("pw", s), ("hn", blk // 4)], [self.pk(bank)])
            self.copy(self.ew(), vst.ap(blk * nh * 66, [[66, nh], [1, 64]]), self.psv(bank, 0, [[64, nh], [1, 64]]),
                      [self.pk(bank)], [vkey])
        self.store(self.S[dst_name].ap(tok * rowlen, [[rowlen, 128], [128 * rowlen, NB], [1, rowlen]]),
                   vst.ap(0, [[rowlen, NB], [1, rowlen]]), [vkey], dkeys, "vst")

    def xkeys(self, row0):
        return [("xscr", t) for t in range(row0 // 512, (row0 + self.NT) // 512)]

    def x_load(self, src, row0, par):
        NB = self.NB
        keys = [("x", par, b) for b in range(NB)]
        self.load(self.xts[par].ap(0, [[1024, NB], [1, 1024]]), src.ap(row0 * D, [[D, 128], [128 * D, NB], [1, D]]),
                  self.xkeys(row0) if src is self.S["x"] else [], keys, f"xld{par}")

    def x_store(self, dst, row0, key):
        NB = self.NB
        keys = [("x", self.xpar, b) for b in range(NB)]
        w = self.xkeys(row0) if dst is self.S["x"] else []
        self.store(dst.ap(row0 * D, [[D, 128], [128 * D, NB], [1, D]]), self.xt.ap(0, [[1024, NB], [1, 1024]]),
                   keys, w, key + str(self.xpar))

    def skeys(self, nm, tok):
        return [(nm, t) for t in range(tok // 512, (tok + self.NT) // 512)]

    def phaseA(self, tok0, L, NB):
        self.alloc_ffn(NB, "v")
        NT = self.NT
        nt_ = L // NT
        self.x_load(self.W["x_in"], tok0, 0)
        for t in range(nt_):
            tok = NT * t
            self.set_par(t % 2)
            if t + 1 < nt_:
                self.x_load(self.W["x_in"], tok0 + tok + NT, (t + 1) % 2)
            self.ffn(0, 0)
            self.norm_fm(1)
            for slab, nm in enumerate(["u", "q", "k"]):
                s = self.pw_load("WinE", slab)
                self.proj_fm(s, [0, 1, 2, 3], nm, 0, tok, self.skeys(nm + "scr", tok))
            s = self.pw_load("WinE", 3)
            self.proj_tm_v(s, 0, 8, self.vst, "vst", "v", 528, tok, self.skeys("vscr", tok))
            self.x_store(self.S["x"], tok, "xst")

    def attn_block(self, heads, np_, kfn, qfn, vfn, tabfn, bslot, sink):
        yb = self.yb[bslot]

        DEP = 3 if np_ > 4 else 4
        slot_w = 640 if np_ > 4 else 512

        def stage1(hi, h):
            ss = hi % DEP
            c0 = ss * slot_w
            banks = sorted(set([self.pk(c0 // 512), self.pk((c0 + np_ * 128 - 1) // 512)]))
            for i in range(np_):
                self.mm(self.psum.ap(c0 + i * 128, [[1, 128]]), kfn(h, i), qfn(h), True, True, [("kT", self.xpar), ("qT", self.xpar)], banks)
            self.act(self.E[ss].ap(0, [[1, np_ * 128]]), self.psum.ap(c0, [[1, np_ * 128]]), AF.Exp, banks, [("E", ss)], scale=0.125)
            self.tt("dve", self.Pm[ss].ap(0, [[1, np_ * 128]]), self.E[ss].ap(0, [[1, np_ * 128]]), tabfn(h), ALU.mult,
                    [("E", ss), "TAB"], [("Pm", ss)])

        def stage2(hi, h):
            ss = hi % DEP
            ob = 4 + hi // 4
            for i in range(np_):
                self.mm(self.ps(ob, (hi % 4) * 65, 65), self.Pm[ss].ap(i * 128, [[1, 128]]), vfn(h, i), i == 0, i == np_ - 1,
                        [("Pm", ss), ("vT", self.xpar)], [self.pk(ob)])

        nhd = len(heads)
        for hi, h in enumerate(heads):
            stage1(hi, h)
            if hi >= DEP - 1:
                stage2(hi - DEP + 1, heads[hi - DEP + 1])
        for hi in range(max(0, nhd - DEP + 1), nhd):
            stage2(hi, heads[hi])
        for hb in range((len(heads) + 3) // 4):
            ob = 4 + hb
            hs = heads[hb * 4:(hb + 1) * 4]
            nh = len(hs)
            den = self.psv(ob, 64, [[65, nh]])
            rc = self.rec.ap(hb * 4, [[1, nh]])
            if sink:
                dsa = self.dsum.ap(hb * 4, [[1, nh]])
                self.tt("dve", dsa, den, self.es.ap(hs[0], [[1, nh]]), ALU.add, [self.pk(ob), "es"], ["dsum"])
                self.P.add("dve", lambda e, rc=rc, dsa=dsa: e.reciprocal(out=rc, in_=dsa), r=["dsum"], w=["rec"])
            else:
                self.P.add("dve", lambda e, rc=rc, den=den: e.reciprocal(out=rc, in_=den), r=[self.pk(ob)], w=["rec"])
            self.tt("dve", yb.ap(hs[0] * 64, [[64, nh], [1, 64]]), self.psv(ob, 0, [[65, nh], [1, 64]]),
                    self.rec.ap(hb * 4, [[1, nh], [0, 64]]), ALU.mult, [self.pk(ob), "rec"], [("yb", bslot)])

    def out_proj(self, wname, lhs_fn, rkeys):
        for half in range(2):
            s = self.pw_load(wname, half)
            for blk in range(4):
                bank = 4 + blk
                for kc in range(8):
                    self.mm(self.ps(bank), lhs_fn(kc, blk), self.pw[s].ap(kc * 512, [[1, 512]]), kc == 0, kc == 7,
                            rkeys + [("pw", s)], [self.pk(bank)])
                xa = self.xt.ap(blk * 1024 + half * 512, [[1, 512]])
                self.stt(xa, self.ps(bank), 1.0, xa, ALU.mult, ALU.add, [self.pk(bank), ("x", self.xpar, blk)], [("x", self.xpar, blk)])

    def phaseC1(self, L):
        S = self.S
        LM = self.LMAX
        NB_ = L // 128
        self.alloc_attn("na")
        def loads(t, par):
            tok = 512 * t
            self.x_load(S["x"], tok, par)
            lo, hi = max(0, tok - 256), min(L, tok + 768)
            kw = hi - lo
            tl = list(range(lo // 512, (hi + 511) // 512))
            for e_ in range(2):
                self.load(self.qTs[par].ap(e_ * 512, [[1024, 4], [1, 512]], p0=64 * e_, pn=64),
                          S["q"].ap(64 * e_ * LM + tok, [[LM, 64], [128 * LM, 4], [1, 512]]), [("qscr", t)], [("qT", par)], f"qT{e_}{par}")
            self.load(self.kTs[par].ap(0, [[1024, 4], [1, kw]]), S["k"].ap(lo, [[LM, 128], [128 * LM, 4], [1, kw]]),
                      [("kscr", i) for i in tl], [("kT", par)], f"kT{par}")
            self.load(self.vTs[par].ap(0, [[528, kw // 128], [1, 528]]), S["v"].ap(lo * 528, [[528, 128], [128 * 528, kw // 128], [1, 528]]),
                      [("vscr", i) for i in tl], [("vT", par)], f"vT{par}")
            self.load(self.yaTs[par].ap(0, [[512, 4], [1, 512]]), S["ya"].ap(tok, [[LM, 128], [128 * LM, 4], [1, 512]]),
                      [("yascr", t)], [("yaT", par)], f"yaT{par}")

        nt_ = L // 512
        loads(0, 0)
        for t in range(nt_):
            tok = 512 * t
            self.set_par(t % 2)
            if t + 1 < nt_:
                loads(t + 1, (t + 1) % 2)
            lo = max(0, tok - 256)
            for b in range(4):
                n = 4 * t + b
                cls = "first" if n == 0 else "second" if n == 1 else "last" if n == NB_ - 1 else "slast" if n == NB_ - 2 else "interior"
                offs = NA_OFFS[cls]
                pb = NA_PBASE[cls]
                np_ = len(offs)
                kb = [((n + o) * 128 - lo) for o in offs]
                bslot = b % 2
                self.attn_block(
                    list(range(8)), np_,
                    lambda h, i: self.kT.ap((h // 2) * 1024 + kb[i], [[1, 128]]),
                    lambda h: self.qT.ap((h // 2) * 1024 + (h % 2) * 512 + b * 128, [[1, 128]]),
                    lambda h, i: self.vT.ap((kb[i] // 128) * 528 + h * 66, [[1, 65]]),
                    lambda h: self.TAB.ap((h * 21 + pb) * 128, [[1, np_ * 128]]),
                    bslot, False)
                for kc in range(4):
                    self.tr(self.psb(6, kc * 128, 128), self.yb[bslot].ap(kc * 128, [[1, 128]]), self.identb[:],
                            [("yb", bslot), "identb"], [self.pk(6)])
                self.copy(self.ew(), self.mix.ap(4 * 512 + b * 128, [[512, 4], [1, 128]]),
                          self.psb(6, 0, 512).rearrange("p (k t) -> p k t", k=4), [self.pk(6)], ["mix"])
            self.out_proj("WoutE", lambda kc, blk: (self.yaT.ap(kc * 512 + blk * 128, [[1, 128]]) if kc < 4
                                                      else self.mix.ap(kc * 512 + blk * 128, [[1, 128]])), [("yaT", self.xpar), "mix"])
            self.x_store(S["x"], tok, "xst")

    def phaseC2(self, L, NB):
        self.alloc_ffn(NB, "v2")
        NT = self.NT
        nt_ = L // NT
        self.x_load(self.S["x"], 0, 0)
        for t in range(nt_):
            tok = NT * t
            self.set_par(t % 2)
            if t + 1 < nt_:
                self.x_load(self.S["x"], tok + NT, (t + 1) % 2)
            self.ffn(1, 2)
            self.ffn(2, 3)
            self.norm_fm(4)
            for slab in range(2):
                s = self.pw_load("WinO", slab)
                self.proj_fm(s, [0, 1, 2, 3], "q2", slab * 4, tok, self.skeys("q2scr", tok))
            s = self.pw_load("WinO", 2)
            self.proj_fm(s, [0, 1], "k2", 0, tok, self.skeys("k2scr", tok))
            self.proj_tm_v(s, 256, 4, self.v2st, "v2st", "v2", 264, tok, self.skeys("v2scr", tok))
            self.x_store(self.S["x"], tok, "xst")

    def phaseD1(self, L):
        S = self.S
        LM = self.LMAX
        NB_ = L // 128
        hmap = {}
        for tq, (ha, hb_) in enumerate(GQA_PAIRS):
            hmap[ha] = (tq, 0)
            hmap[hb_] = (tq, 1)
        self.alloc_attn("gqa")
        def loads(t, par):
            tok = 512 * t
            self.x_load(S["x"], tok, par)
            lo, hi = max(0, tok - 128), min(L, tok + 640)
            kw = hi - lo
            tl = list(range(lo // 512, (hi + 511) // 512))
            self.load(self.qTs[par].ap(0, [[512, 8], [1, 512]]), S["q2"].ap(tok, [[LM, 128], [128 * LM, 8], [1, 512]]),
                      [("q2scr", t)], [("qT", par)], f"qT0{par}")
            for e_ in range(2):
                self.load(self.kTs[par].ap(e_ * 1024, [[2048, 2], [1, kw]], p0=64 * e_, pn=64),
                          S["k2"].ap(64 * e_ * LM + lo, [[LM, 64], [128 * LM, 2], [1, kw]]), [("k2scr", i) for i in tl], [("kT", par)], f"kT{e_}{par}")
            self.load(self.vTs[par].ap(0, [[528, kw // 128], [1, 264]]), S["v2"].ap(lo * 264, [[264, 128], [128 * 264, kw // 128], [1, 264]]),
                      [("v2scr", i) for i in tl], [("vT", par)], f"vT{par}")

        nt_ = L // 512
        loads(0, 0)
        for t in range(nt_):
            tok = 512 * t
            self.set_par(t % 2)
            if t + 1 < nt_:
                loads(t + 1, (t + 1) % 2)
            lo = max(0, tok - 128)
            for b in range(4):
                n = 4 * t + b
                offs = [o for o in (-1, 0, 1) if 0 <= n + o < NB_]
                np_ = len(offs)
                kb = [((n + o) * 128 - lo) for o in offs]
                o0 = offs[0] + 1
                bslot = b % 2
                for hh in range(2):
                    self.attn_block(
                        list(range(hh * 8, hh * 8 + 8)), np_,
                        lambda h, i: self.kT.ap(((h // 4) // 2) * 2048 + hmap[h][1] * 1024 + kb[i], [[1, 128]]),
                        lambda h: self.qT.ap(hmap[h][0] * 512 + b * 128, [[1, 128]]),
                        lambda h, i: self.vT.ap((kb[i] // 128) * 528 + (h // 4) * 66, [[1, 65]]),
                        lambda h: self.TAB.ap((h * 3 + o0) * 128, [[1, np_ * 128]]),
                        bslot, True)
                for kc in range(8):
                    self.tr(self.psb(6, kc * 128, 128), self.yb[bslot].ap(kc * 128, [[1, 128]]), self.identb[:],
                            [("yb", bslot), "identb"], [self.pk(6)])
                self.copy(self.ew(), self.mix.ap(b * 128, [[512, 8], [1, 128]]),
                          self.psb(6).rearrange("p (k t) -> p k t", k=8), [self.pk(6)], ["mix"])
            self.out_proj("WoutO", lambda kc, blk: self.mix.ap(kc * 512 + blk * 128, [[1, 128]]), ["mix"])
            self.x_store(S["x"], tok, "xst")

    def phaseD2(self, tok0, L, NB):
        self.alloc_ffn(NB)
        NT = self.NT
        nt_ = L // NT
        self.x_load(self.S["x"], 0, 0)
        for t in range(nt_):
            tok = NT * t
            self.set_par(t % 2)
            if t + 1 < nt_:
                self.x_load(self.S["x"], tok + NT, (t + 1) % 2)
            self.ffn(3, 5)
            x = self.xt
            for blk in range(self.NB):
                self.stats(blk)
                xa = x.ap(blk * 1024, [[1, 1024]])
                self.stt(xa, xa, self.st.ap(blk * 4 + 2, [[1, 1]]), self.Gfin[:], ALU.mult, ALU.mult,
                         [("x", self.xpar, blk), ("st", blk), "Gfin"], [("x", self.xpar, blk)])
            self.x_store(self.y_out, tok0 + tok, "yst")


    def setup_tables(self):
        sb = self.sb
        sb.ptr = self.arena0
        a = sb.alloc
        W = self.W
        P = self.P
        self.TNA = a("TNA", [128, 8, 21, 128], BF16)
        self.TGQ = a("TGQ", [128, 16, 3, 128], BF16)
        rpn = a("rpn", [120, 31], F32)
        self.load(rpn[:], W["na_rpb"].ap(0, [[31, 120], [1, 31]]), [], ["rpn"], "c2")
        self.mm(self.ps(0, 0, 120, pn=31), rpn[:], self.identf.ap(0, [[1, 120]], pn=120), True, True, ["rpn", "identf"], [self.pk(0)])
        erb = a("erb", [31, 120], BF16)
        self.act(erb[:], self.ps(0, 0, 120, pn=31), AF.Exp, [self.pk(0)], ["erb"])
        ohc = a("ohc", [31, 64 * 128], BF16)
        self.load(ohc[:], W["c_ohc"].ap(0, [[64 * 128, 31], [1, 64 * 128]]), [], ["ohc"], "c2")
        esub = a("esub", [128, 8, 15, 64], BF16)
        for qb in range(16):
            bank = 1 + qb % 3
            for q4 in range(4):
                qc = qb * 4 + q4
                self.mm(self.ps(bank, q4 * 120, 120), ohc.ap(qc * 128, [[1, 128]], pn=31), erb.ap(0, [[1, 120]], pn=31),
                        True, True, ["ohc", "erb"], [self.pk(bank)])
            self.copy(self.ew(), esub.ap(qb * 4, [[1, 4], [15 * 64, 8], [64, 15]]), self.psv(bank, 0, [[120, 4], [15, 8], [1, 15]]),
                      [self.pk(bank)], ["esub"])
            self.pump(1)
        k = 0
        for cls in NA_CLASSES:
            for i, blk in enumerate(na_pattern(cls)):
                pat = NA_PBASE[cls] + i
                for kr in range(2):
                    for qr in range(2):
                        dst = self.TNA.ap(pat * 128 + qr * 64, [[21 * 128, 8], [1, 64]], p0=64 * kr, pn=64)
                        rr = blk[kr][qr]
                        k += 1
                        if rr is None:
                            P.add("pool", lambda e, dst=dst: e.memset(dst, 0.0), w=["TAB"])
                        else:
                            self.copy(["dve", "pool", "act"][k % 3], dst,
                                      esub.ap(rr * 64, [[15 * 64, 8], [1, 64]], p0=64 * kr, pn=64), ["esub"], ["TAB"])
        t5n = a("t5n", [32, 16], F32)
        self.load(t5n[:], W["t5_table"].ap(0, [[16, 32], [1, 16]]), [], ["t5n"], "c2")
        etb = a("etb", [32, 16], BF16)
        self.act(etb[:], t5n[:], AF.Exp, ["t5n"], ["etb"])
        ohg = a("ohg", [32, 512], BF16)
        self.load(ohg[:], W["c_ohg"].ap(0, [[512, 32], [1, 512]]), [], ["ohg"], "c2")
        for oi in range(3):
            for qb in range(4):
                bank = 4 + (oi * 4 + qb) % 4
                for ql in range(32):
                    q = qb * 32 + ql
                    s = oi * 128 + 128 - q
                    self.mm(self.ps(bank, ql * 16, 16), ohg.ap(s, [[1, 128]], pn=32), etb.ap(0, [[1, 16]], pn=32), True, True,
                            ["ohg", "etb"], [self.pk(bank)])
                self.copy(self.ew(), self.TGQ.ap(oi * 128 + qb * 32, [[1, 32], [3 * 128, 16]]), self.psv(bank, 0, [[16, 32], [1, 16]]),
                          [self.pk(bank)], ["TAB"])
                self.pump(1)

        n1, n2 = 8 * 21 * 128, 16 * 3 * 128
        self.store(self.S["TNA"].ap(0, [[n1, 128], [1, n1]]), self.TNA.ap(0, [[1, n1]]), ["TAB"], ["TABscr"], "tabst")
        self.store(self.S["TGQ"].ap(0, [[n2, 128], [1, n2]]), self.TGQ.ap(0, [[1, n2]]), ["TAB"], ["TABscr"], "tabst")

    def setup_s5(self):
        sb = self.sb
        sb.ptr = self.arena0
        a = sb.alloc
        W, S, P = self.W, self.S, self.P
        V = "dve"
        nat = a("nat", [64, 128], F32)
        self.load(nat.ap(0, [[1, 64]], pn=64), W["s5_lam_re"].ap(0, [[64, 64], [1, 64]]), [], ["nat"], "c3")
        self.load(nat.ap(64, [[1, 64]], pn=64), W["s5_lam_im"].ap(0, [[64, 64], [1, 64]]), [], ["nat"], "c3")
        for i in range(2):
            self.mm(self.ps(0, i * 64, 64, pn=64), nat.ap(i * 64, [[1, 64]], pn=64), self.identf.ap(0, [[1, 64]], pn=64), True, True,
                    ["nat", "identf"], [self.pk(0)])
        f64 = lambda nm: a(nm, [64, 64], F32)
        lre, lim, ldt, zr, th, den, nr, cr, ci, t64 = [f64(n) for n in ["lre", "lim", "ldt", "zr", "th", "den", "nr", "cr", "ci", "t64"]]
        K = "s5s"
        self.copy(V, lre[:], self.ps(0, 0, 64, pn=64), [self.pk(0)], [K])
        self.copy(V, lim[:], self.ps(0, 64, 64, pn=64), [self.pk(0)], [K])
        self.load(ldt[:], W["s5_log_dt"].ap(0, [[0, 64], [1, 64]]), [], [K], "c3")
        self.act(ldt[:], ldt[:], AF.Exp, [K], [K])
        self.tt(V, zr[:], lre[:], ldt[:], ALU.mult, [K], [K])
        self.tt(V, th[:], lim[:], ldt[:], ALU.mult, [K], [K])
        if getattr(self, "s5_stop", 99) <= 1:
            return
        expo = a("expo", [64, 3, 2, 32], F32)
        self.load(expo[:], W["c_expo"].ap(0, [[192, 64], [1, 192]]), [], [K], "c3")
        big = lambda nm: a(nm, [64, 2, 32, 32], F32)
        PWre = [big(f"pwre{i}") for i in range(3)]
        PWim = [big(f"pwim{i}") for i in range(3)]
        m16 = lambda nm: a(nm, [64, 64, 16], F32)
        bre, bim, Bre, Bim, tb = m16("bre"), m16("bim"), m16("Bre"), m16("Bim"), m16("tb")
        cn = a("cn", [128, 8, 64], F32)
        ct_tiles = [a(f"ct{ci_}", [64, 1024], F32) for ci_ in range(2)]
        a1re, a1im = f64("a1re"), f64("a1im")
        drep = a("drep", [32, 128], F32)
        dcol = a("dcol", [128, 32], F32)
        msk = a("msk", [128, 2, 128], F32)
        mark_T = sb.ptr
        T1, T2, T3, T4 = big("T1"), big("T2"), big("T3"), big("T4")
        fl = [[1, 2048]]
        d3 = [[1024, 2], [32, 32], [1, 32]]
        TWO_PI = 2.0 * math.pi
        C1 = 6.28125
        C2 = float(np.float32(TWO_PI - C1))
        C3 = float(TWO_PI - C1 - C2)
        MAGIC = 12582912.0
        for tab in range(3):
            ex_b = expo.ap(tab * 64, [[32, 2], [0, 32], [1, 32]])
            self.tt(V, T1.ap(0, d3), th.ap(0, [[32, 2], [1, 32], [0, 32]]), ex_b, ALU.mult, [K], [K])
            self.tt(V, T2.ap(0, d3), zr.ap(0, [[32, 2], [1, 32], [0, 32]]), ex_b, ALU.mult, [K], [K])
            self.act(T2.ap(0, fl), T2.ap(0, fl), AF.Exp, [K], [K])
            self.ts(V, T3.ap(0, fl), T1.ap(0, fl), 1.0 / TWO_PI, None, ALU.mult, None, [K], [K])
            self.ts(V, T3.ap(0, fl), T3.ap(0, fl), MAGIC, None, ALU.add, None, [K], [K])
            self.ts(V, T3.ap(0, fl), T3.ap(0, fl), -MAGIC, None, ALU.add, None, [K], [K])
            self.stt(T4.ap(0, fl), T3.ap(0, fl), -C1, T1.ap(0, fl), ALU.mult, ALU.add, [K], [K])
            self.stt(T4.ap(0, fl), T3.ap(0, fl), -C2, T4.ap(0, fl), ALU.mult, ALU.add, [K], [K])
            self.stt(T4.ap(0, fl), T3.ap(0, fl), -C3, T4.ap(0, fl), ALU.mult, ALU.add, [K], [K])
            self.ts(V, T3.ap(0, fl), T4.ap(0, fl), -0.5, None, ALU.mult, None, [K], [K])
            self.stt(T3.ap(0, fl), T4.ap(0, fl), 0.5, T3.ap(0, fl), ALU.mult, ALU.max, [K], [K])
            self.act(T1.ap(0, fl), T3.ap(0, fl), AF.Sin, [K], [K], scale=-1.0, bias=self.halfpi.ap(0, [[1, 1]], pn=64))
            self.act(T3.ap(0, fl), T4.ap(0, fl), AF.Sin, [K], [K], scale=0.5)
            self.stt(PWim[tab].ap(0, fl), T3.ap(0, fl), 2.0, T1.ap(0, fl), ALU.mult, ALU.mult, [K], [K])
            self.tt(V, T4.ap(0, fl), T3.ap(0, fl), T3.ap(0, fl), ALU.mult, [K], [K])
            self.ts(V, PWre[tab].ap(0, fl), T4.ap(0, fl), -2.0, 1.0, ALU.mult, ALU.add, [K], [K])
            self.tt(V, PWre[tab].ap(0, fl), PWre[tab].ap(0, fl), T2.ap(0, fl), ALU.mult, [K], [K])
            self.tt(V, PWim[tab].ap(0, fl), PWim[tab].ap(0, fl), T2.ap(0, fl), ALU.mult, [K], [K])
            self.pump(3)
        if getattr(self, "s5_stop", 99) <= 2:
            return
        for ri, PWt in enumerate([PWre[2], PWim[2]]):
            self.copy(V, self.A32.ap(ri * 64, [[1, 32]], pn=64), PWt.ap(31, [[32, 32]]), [K], ["A32"])
            self.copy(V, self.A32.ap(ri * 64 + 32, [[1, 32]], pn=64), PWt.ap(1024, [[32, 32]]), [K], ["A32"])
            dst = a1re if ri == 0 else a1im
            self.copy(V, dst.ap(0, [[1, 32]]), PWt.ap(0, [[32, 32]]), [K], [K])
            self.copy(V, dst.ap(32, [[1, 32]]), PWt.ap(1024 + 31, [[32, 32]]), [K], [K])
        self.ts(V, self.Ac2.ap(0, [[1, 64]], pn=64), self.A32.ap(64, [[1, 64]], pn=64), -1.0, None, ALU.mult, None, ["A32"], ["A32"])
        self.copy(V, self.Ac2.ap(64, [[1, 64]], pn=64), self.A32.ap(64, [[1, 64]], pn=64), ["A32"], ["A32"])
        self.tt(V, den[:], lre[:], lre[:], ALU.mult, [K], [K])
        self.tt(V, t64[:], lim[:], lim[:], ALU.mult, [K], [K])
        self.tt(V, den[:], den[:], t64[:], ALU.add, [K], [K])
        P.add(V, lambda e: e.reciprocal(out=den[:], in_=den[:]), r=[K], w=[K])
        self.ts(V, nr[:], a1re[:], -1.0, None, ALU.add, None, [K], [K])
        self.tt(V, cr[:], nr[:], lre[:], ALU.mult, [K], [K])
        self.tt(V, t64[:], a1im[:], lim[:], ALU.mult, [K], [K])
        self.tt(V, cr[:], cr[:], t64[:], ALU.add, [K], [K])
        self.tt(V, cr[:], cr[:], den[:], ALU.mult, [K], [K])
        self.tt(V, ci[:], a1im[:], lre[:], ALU.mult, [K], [K])
        self.tt(V, t64[:], nr[:], lim[:], ALU.mult, [K], [K])
        self.tt(V, ci[:], ci[:], t64[:], ALU.subtract, [K], [K])
        self.tt(V, ci[:], ci[:], den[:], ALU.mult, [K], [K])
        if getattr(self, "s5_stop", 99) <= 3:
            return
        for nm, dst in (("s5_b_re", bre), ("s5_b_im", bim)):
            for q in range(4):
                self.load(dst.ap(q * 256, [[16, 16], [1, 16]], pn=64),
                          W[nm].ap(q * 16 * 1024, [[16, 64], [1024, 16], [1, 16]]), [], [K], "c3")
        f16 = [[16, 64], [1, 16]]
        crb, cib = cr.ap(0, [[1, 64], [0, 16]]), ci.ap(0, [[1, 64], [0, 16]])
        self.tt(V, Bre.ap(0, f16), bre.ap(0, f16), crb, ALU.mult, [K], [K])
        self.tt(V, tb.ap(0, f16), bim.ap(0, f16), cib, ALU.mult, [K], [K])
        self.tt(V, Bre.ap(0, f16), Bre.ap(0, f16), tb.ap(0, f16), ALU.subtract, [K], [K])
        self.tt(V, Bim.ap(0, f16), bim.ap(0, f16), crb, ALU.mult, [K], [K])
        self.tt(V, tb.ap(0, f16), bre.ap(0, f16), cib, ALU.mult, [K], [K])
        self.tt(V, Bim.ap(0, f16), Bim.ap(0, f16), tb.ap(0, f16), ALU.add, [K], [K])
        if getattr(self, "s5_stop", 99) <= 4:
            return
        CT = []
        for ci_, nm in enumerate(["s5_c_re", "s5_c_im"]):
            self.load(cn.ap(0, [[64, 8], [1, 64]]), W[nm].ap(0, [[64, 128], [128 * 64, 8], [1, 64]]), [], ["cn"], "c3")
            ct = ct_tiles[ci_]
            for t in range(8):
                bank = 1 + t // 4
                self.mm(self.ps(bank, (t % 4) * 128, 128, pn=64), cn.ap(t * 64, [[1, 64]]), self.identf[:], True, True,
                        ["cn", "identf"], [self.pk(bank)])
            for hb in range(2):
                self.copy(V, ct.ap(hb * 512, [[1, 512]]), self.ps(1 + hb, 0, 512, pn=64), [self.pk(1 + hb)], [K])
            CT.append(ct)
        self.load(drep.ap(0, [[16, 8], [1, 16]], pn=32), W["s5_d"].ap(0, [[16, 32], [0, 8], [1, 16]]), [], ["drep"], "c3")
        self.mm(self.ps(3, 0, 32), drep.ap(0, [[1, 128]], pn=32), self.identf.ap(0, [[1, 32]], pn=32), True, True,
                ["drep", "identf"], [self.pk(3)])
        self.copy(V, dcol[:], self.ps(3, 0, 32), [self.pk(3)], [K])
        self.load(msk.ap(0, [[1, 256]]), W["c_mask"].ap(0, [[256, 128], [1, 256]]), [], [K], "c3")
        if getattr(self, "s5_stop", 99) <= 5:
            return
        GS = 2
        sb.ptr = mark_T
        bshape = [64, 2, 2, GS, 512]
        WTb, WPb, QTb = a("WTb", bshape, BF16), a("WPb", bshape, BF16), a("QTb", bshape, BF16)
        t1 = a("bt1", [64, GS, 32, 16], F32)
        t2 = a("bt2", [64, GS, 32, 16], F32)
        Wst = a("Wst", [128, 2, 512], BF16)
        Mst = a("Mst", [128, 7, 128], BF16)
        mt1, mt2 = a("mt1", [128, 128], F32), a("mt2", [128, 128], F32)
        od = [[512, GS], [16, 32], [1, 16]]
        full = [[1, GS * 512]]
        for bi in range(32 // GS):
            g0 = bi * GS
            KB = "s5b"
            self.pump(3)
            for dr_ in range(2):
                bb_re = Bre.ap((dr_ * 32 + g0) * 16, [[16, GS], [0, 32], [1, 16]])
                bb_im = Bim.ap((dr_ * 32 + g0) * 16, [[16, GS], [0, 32], [1, 16]])
                c_re = CT[0].ap((dr_ * 32 + g0) * 16, [[16, GS], [0, 32], [1, 16]])
                c_im = CT[1].ap((dr_ * 32 + g0) * 16, [[16, GS], [0, 32], [1, 16]])
                for tab, outt in ((0, WTb), (1, WPb)):
                    pre = PWre[tab].ap(dr_ * 1024 + g0 * 32, [[32, GS], [1, 32], [0, 16]])
                    pim = PWim[tab].ap(dr_ * 1024 + g0 * 32, [[32, GS], [1, 32], [0, 16]])
                    o_re = outt.ap(((0 * 2 + dr_) * GS) * 512, od, pn=64)
                    o_im = outt.ap(((1 * 2 + dr_) * GS) * 512, od, pn=64)
                    self.tt(V, t1[:], pre, bb_re, ALU.mult, [K], [KB])
                    self.tt(V, t2[:], pim, bb_im, ALU.mult, [K], [KB + "p"])
                    self.tt(V, o_re, t1[:], t2[:], ALU.subtract, [KB, KB + "p"], [KB])
                    self.tt(V, t1[:], pre, bb_im, ALU.mult, [K, KB], [KB])
                    self.tt(V, t2[:], pim, bb_re, ALU.mult, [K, KB], [KB + "p"])
                    self.tt(V, o_im, t1[:], t2[:], ALU.add, [KB, KB + "p"], [KB])
                pre = PWre[2].ap(dr_ * 1024 + g0 * 32, [[32, GS], [1, 32], [0, 16]])
                pim = PWim[2].ap(dr_ * 1024 + g0 * 32, [[32, GS], [1, 32], [0, 16]])
                o_re = QTb.ap(((0 * 2 + dr_) * GS) * 512, od, pn=64)
                o_im = QTb.ap(((1 * 2 + dr_) * GS) * 512, od, pn=64)
                self.tt(V, t1[:], c_re, pre, ALU.mult, [K, KB], [KB])
                self.tt(V, t2[:], c_im, pim, ALU.mult, [K, KB], [KB + "p"])
                self.tt(V, o_re, t1[:], t2[:], ALU.subtract, [KB, KB + "p"], [KB])
                self.tt(V, t1[:], c_re, pim, ALU.mult, [K, KB], [KB])
                self.tt(V, t2[:], c_im, pre, ALU.mult, [K, KB], [KB + "p"])
                self.stt(o_im, t1[:], -1.0, t2[:], ALU.mult, ALU.subtract, [KB, KB + "p"], [KB])
                if getattr(self, "s5_stop", 99) <= 6:
                    continue
                for g in range(GS):
                    self.store(S["QT"].ap((dr_ * 32 + g0 + g) * 64 * 1024, [[1024, 64], [512, 2], [1, 512]]),
                               QTb.ap((dr_ * GS + g) * 512, [[2 * GS * 512, 2], [1, 512]], pn=64), [KB], ["QTscr"], f"s5q{g}")
                if getattr(self, "s5_stop", 99) <= 7:
                    continue
                for g in range(GS):
                    bank = 6 + g % 2
                    for Jp in range(4):
                        for slot in range(3):
                            ri = slot % 2
                            self.tr(self.psb(bank, Jp * 192 + slot * 64, 64),
                                    WTb.ap(((ri * 2 + dr_) * GS + g) * 512 + Jp * 128, [[1, 128]], pn=64),
                                    self.identb.ap(0, [[1, 64]], pn=64), [KB, "identb"], [self.pk(bank)])
                    self.copy(self.ew(), Wst.ap(0, [[1, 768]]), self.psb(bank, 0, 768), [self.pk(bank)], ["Wst"])
                    self.store(S["Ws5"].ap((dr_ * 32 + g0 + g) * 128 * 768, [[768, 128], [1, 768]]),
                               Wst.ap(0, [[1, 768]]), ["Wst"], ["Wscr"], "s5w")
            if getattr(self, "s5_stop", 99) <= 8:
                continue
            for g in range(GS):
                for dl in range(4):
                    for ri in range(2):
                        self.mm(self.ps(4, dl * 128, 128), WPb.ap(((ri * 2 + 0) * GS + g) * 512, [[1, 128]], pn=64),
                                QTb.ap(((ri * 2 + 0) * GS + g) * 512 + dl * 128, [[1, 128]], pn=64), ri == 0, ri == 1,
                                [KB], [self.pk(4)])
                    for ri in range(2):
                        self.mm(self.ps(5, dl * 128, 128), WPb.ap(((ri * 2 + 1) * GS + g) * 512 + dl * 128, [[1, 128]], pn=64),
                                QTb.ap(((ri * 2 + 1) * GS + g) * 512, [[1, 128]], pn=64), ri == 0, ri == 1,
                                [KB], [self.pk(5)])
                if getattr(self, "s5_stop", 99) <= 9:
                    continue
                self.copy("act", Mst.ap(4 * 128, [[1, 384]]), self.ps(4, 128, 384), [self.pk(4)], ["Mst"])
                for dl in range(1, 4):
                    self.copy("act", Mst.ap((3 - dl) * 128, [[1, 128]]), self.ps(5, dl * 128, 128), [self.pk(5)], ["Mst"])
                if getattr(self, "s5_stop", 99) <= 10:
                    continue
                import os
                self.copy("act", mt1[:], self.ps(4, 0, 128), [self.pk(4)], ["mta"])
                self.copy("act", mt2[:], self.ps(5, 0, 128), [self.pk(5)], ["mta"])
                if os.environ.get("S5SUB") == "0":
                    continue
                self.tt(V, mt1[:], mt1[:], msk.ap(0, [[1, 128]]), ALU.mult, ["mta", K], ["mt"])
                self.tt(V, mt2[:], mt2[:], msk.ap(128, [[1, 128]]), ALU.mult, ["mta", K], ["mt"])
                if os.environ.get("S5SUB") == "1":
                    continue
                self.tt(V, mt1[:], mt1[:], mt2[:], ALU.add, ["mt"], ["mt"])
                if os.environ.get("S5SUB") == "2":
                    continue
                self.stt(Mst.ap(3 * 128, [[1, 128]]), self.identf[:], dcol.ap(g0 + g, [[1, 1]]), mt1[:], ALU.mult, ALU.add,
                         ["mt", "identf", K], ["Mst"])
                if getattr(self, "s5_stop", 99) <= 11:
                    continue
                self.store(S["M"].ap((g0 + g) * 128 * 896, [[896, 128], [1, 896]]), Mst.ap(0, [[1, 896]]), ["Mst"], ["Mscr"], "s5m")


    def alloc_B(self, L):
        sb = self.sb
        sb.ptr = self.arena0
        a = sb.alloc
        Kc = L // 32
        self.gfm = a("gfm", [128, 4, L], BF16)
        self.ufm = a("ufm", [128, L], BF16)
        self.U32 = a("U32", [128, 8, 4, Kc], BF16)
        self.Sst = a("Sst", [64, 2, 2, 8, Kc], F32)
        self.Hbf = a("Hbf", [128, 2, 2, 8, Kc], BF16)
        self.G32 = a("G32", [128, 8, 4, Kc], BF16)
        self.sel = a("sel", [128, 64, 128], BF16)
        self.selT = a("selT", [128, 64, 128], BF16)
        self.wglu = a("wglu", [128, 4, 512], BF16)
        self.wr = [a(f"wr{i}", [128, 4, 192], BF16) for i in range(2)]
        self.qr = [a(f"qr{i}", [128, 2, 512], BF16) for i in range(4)]
        self.mr = [a(f"mr{i}", [128, 7, 128], BF16) for i in range(2)]
        self.gt1 = [a(f"gt1{i}", [128, 512], F32) for i in range(2)]
        self.gt2 = [a(f"gt2{i}", [128, 512], F32) for i in range(2)]
        self.yst = [a(f"yst{i}", [128, 512], BF16) for i in range(2)]
        self.Tt = [a(f"Tt{i}", [64, 2, 2, 8], F32) for i in range(3)]

    def phaseB(self, L):
        S, W, P = self.S, self.W, self.P
        LM = self.LMAX
        Kc = L // 32
        self.alloc_B(L)
        V = "dve"
        nt = L // 512
        self.load(self.sel.ap(0, [[1, 8192]]), W["c_sel"].ap(0, [[8192, 128], [1, 8192]]), [], ["sel"], "bsel")
        self.load(self.selT.ap(0, [[1, 8192]]), W["c_selT"].ap(0, [[8192, 128], [1, 8192]]), [], ["selT"], "bsel")
        self.load(self.wglu.ap(0, [[1, 2048]]), S["Wglu"].ap(0, [[2048, 128], [1, 2048]]), ["Wglu"], ["wglu"], "bsel")
        cw = cq = cm = 0
        hz = self.Hbf.ap(0, [[1, 32 * Kc]], p0=64, pn=64)
        P.add("pool", lambda e, hz=hz: e.memset(hz, 0.0), w=["Hbf"])
        for i_ in range(4):
            qz = self.qr[i_].ap(0, [[1, 1024]], p0=64, pn=64)
            P.add("pool", lambda e, qz=qz: e.memset(qz, 0.0), w=[("qr", i_)])
        ristr, dstr, gstr = 2 * 8 * Kc, 8 * Kc, Kc
        for i in range(4):
            self.load(self.ufm.ap(0, [[1, L]]), S["u"].ap(i * 128 * LM, [[LM, 128], [1, L]]),
                      [("uscr", t) for t in range(nt)], ["ufm"], "bu")
            for g8 in range(8):
                bank = g8 % 4
                for Jp in range(4):
                    for jl in range(8):
                        self.mm(self.ps(bank, Jp * Kc, Kc), self.sel.ap((g8 * 8 + jl) * 128, [[1, 128]]),
                                self.ufm.ap(8 * Jp + jl, [[32, Kc]]), jl == 0, jl == 7, ["sel", "ufm"], [self.pk(bank)])
                self.copy(self.ew(), self.U32.ap(g8 * 4 * Kc, [[1, 4 * Kc]]), self.ps(bank, 0, 4 * Kc), [self.pk(bank)], ["U32"])
            for g8 in range(8):
                g = 8 * i + g8
                for dr_ in range(2):
                    s = cw % 2
                    cw += 1
                    self.load(self.wr[s].ap(0, [[1, 768]]), S["Ws5"].ap((dr_ * 32 + g) * 128 * 768, [[768, 128], [1, 768]]),
                              ["Wscr"], [("wr", s)], f"wr{s}")
                    bank = 4 + (g8 * 2 + dr_) % 2
                    for ri in range(2):
                        for Jp in range(4):
                            self.mm(self.ps(bank, ri * Kc, Kc), self.wr[s].ap(Jp * 192 + ri * 64, [[1, 128]]),
                                    self.U32.ap((g8 * 4 + Jp) * Kc, [[1, Kc]]), Jp == 0, Jp == 3, [("wr", s), "U32"], [self.pk(bank)])
                    self.copy(self.ew(), self.Sst.ap(dr_ * dstr + g8 * gstr, [[ristr, 2], [1, Kc]]),
                              self.psv(bank, 0, [[Kc, 2], [1, Kc]], pn=64), [self.pk(bank)], ["Sst"])
            c1 = self.A32.ap(8 * i, [[0, 2], [32, 2], [1, 8]])
            c2 = self.Ac2.ap(8 * i, [[64, 2], [32, 2], [1, 8]])
            T0, T1, T2 = self.Tt
            for k in range(1, Kc):
                dcur = dstr + (Kc - 1 - 2 * k)
                dprv = dstr + (Kc + 1 - 2 * k)
                cur2 = self.Sst.ap(k, [[ristr, 2], [dcur, 2], [gstr, 8]])
                prv2 = self.Sst.ap(k - 1, [[ristr, 2], [dprv, 2], [gstr, 8]])
                prvs = self.Sst.ap(ristr + k - 1, [[-ristr, 2], [dprv, 2], [gstr, 8]])
                self.tt(V, T0[:], prv2, c1, ALU.mult, ["Sst", "A32"], ["Tt0"])
                self.tt(V, T1[:], prvs, c2, ALU.mult, ["Sst", "A32"], ["Tt1"])
                self.tt(V, cur2, cur2, T0[:], ALU.add, ["Tt0", "Sst"], ["Sst"])
                self.tt(V, cur2, cur2, T1[:], ALU.add, ["Tt1", "Sst"], ["Sst"])
            for dr_ in range(2):
                zcol = 0 if dr_ == 0 else Kc - 1
                hz2 = self.Hbf.ap(dr_ * dstr + zcol, [[ristr, 2], [gstr, 8], [1, 1]], pn=64)
                P.add("pool", lambda e, hz2=hz2: e.memset(hz2, 0.0), w=["Hbf"])
                so, do = (0, 1) if dr_ == 0 else (1, 0)
                self.copy("act" if dr_ else "dve", self.Hbf.ap(dr_ * dstr + do, [[ristr, 2], [gstr, 8], [1, Kc - 1]], pn=64),
                          self.Sst.ap(dr_ * dstr + so, [[ristr, 2], [gstr, 8], [1, Kc - 1]]), ["Sst"], ["Hbf"])
            for g8 in range(8):
                g = 8 * i + g8
                sm = cm % 2
                cm += 1
                self.load(self.mr[sm].ap(0, [[1, 896]]), S["M"].ap(g * 128 * 896, [[896, 128], [1, 896]]), ["Mscr"], [("mr", sm)], f"mr{sm}")
                sq = []
                for dr_ in range(2):
                    s = cq % 4
                    cq += 1
                    self.load(self.qr[s].ap(0, [[1, 1024]], pn=64), S["QT"].ap((dr_ * 32 + g) * 64 * 1024, [[1024, 64], [1, 1024]]),
                              ["QTscr"], [("qr", s)], f"qr{s}")
                    sq.append(s)
                bank = g8 % 4
                for J in range(4):
                    first = True
                    for Jp in range(4):
                        self.mm(self.ps(bank, J * Kc, Kc), self.mr[sm].ap((J - Jp + 3) * 128, [[1, 128]]),
                                self.U32.ap((g8 * 4 + Jp) * Kc, [[1, Kc]]), first, False, [("mr", sm), "U32"], [self.pk(bank)])
                        first = False
                    for dr_ in range(2):
                        for ri in range(2):
                            last = (dr_ == 1 and ri == 1)
                            self.mm(self.ps(bank, J * Kc, Kc), self.qr[sq[dr_]].ap(ri * 512 + J * 128, [[1, 128]]),
                                    self.Hbf.ap(ri * ristr + dr_ * dstr + g8 * gstr, [[1, Kc]]), False, last,
                                    [("qr", sq[dr_]), "Hbf"], [self.pk(bank)])
                n = 4 * Kc
                y = self.ps(bank, 0, n)
                gs_ = g8 % 2
                ta, tb = self.gt1[gs_].ap(0, [[1, n]]), self.gt2[gs_].ap(0, [[1, n]])
                self.act(ta, y, AF.Square, [self.pk(bank)], [("gt1", gs_)])
                self.ts(V, ta, ta, 0.044715, 1.0, ALU.mult, ALU.add, [("gt1", gs_)], [("gt1", gs_)])
                self.tt(V, tb, ta, y, ALU.mult, [("gt1", gs_), self.pk(bank)], [("gt2", gs_)])
                self.act(tb, tb, AF.Sigmoid, [("gt2", gs_)], [("gt2", gs_)], scale=1.5957691216057308)
                self.tt(V, self.G32.ap(g8 * 4 * Kc, [[1, n]]), tb, y, ALU.mult, [("gt2", gs_), self.pk(bank)], ["G32"])
            for J in range(4):
                for jh in range(2):
                    bank = 4 + (J * 2 + jh) % 4
                    for j4 in range(4):
                        jl = jh * 4 + j4
                        for g8 in range(8):
                            self.mm(self.ps(bank, j4 * Kc, Kc), self.selT.ap((g8 * 8 + jl) * 128, [[1, 128]]),
                                    self.G32.ap((g8 * 4 + J) * Kc, [[1, Kc]]), g8 == 0, g8 == 7, ["selT", "G32"], [self.pk(bank)])
                    self.copy(self.ew(), self.gfm.ap(i * L + 8 * J + jh * 4, [[1, 4], [32, Kc]]),
                              self.psv(bank, 0, [[Kc, 4], [1, Kc]]), [self.pk(bank)], ["gfm"])
        if self.dbg:
            for i in range(4):
                self.store(S["g"].ap(i * 128 * LM, [[LM, 128], [1, L]]), self.gfm.ap(i * L, [[1, L]]), ["gfm"], ["gscr"], "gdbg")
        c = 0
        for t in range(nt):
            for co in range(4):
                bank = c % 4
                ys = c % 2
                c += 1
                for kc in range(4):
                    self.mm(self.ps(bank), self.wglu.ap(kc * 512 + co * 128, [[1, 128]]), self.gfm.ap(kc * L + t * 512, [[1, 512]]),
                            kc == 0, kc == 3, ["wglu", "gfm"], [self.pk(bank)])
                sg = self.gt1[ys].ap(0, [[1, 512]])
                self.act(sg, self.ps(bank), AF.Sigmoid, [self.pk(bank), "bglu"], [("gt1", ys)], bias=self.bglu.ap(co, [[1, 1]]))
                self.tt(V, self.yst[ys][:], sg, self.gfm.ap(co * L + t * 512, [[1, 512]]), ALU.mult, [("gt1", ys), "gfm"], [("yst", ys)])
                self.store(S["ya"].ap(co * 128 * LM + t * 512, [[LM, 128], [1, 512]]), self.yst[ys][:], [("yst", ys)],
                           [("yascr", t)], f"yst{ys}")

    def build(self):
        P = self.P
        stages = getattr(self, "stages", "ctsv")
        self.setup_consts()
        P.add("pool", lambda e: e.memset(self.halfpi[:], math.pi / 2), w=["halfpi"])
        self.conv_q = []
        if "v" in stages:
            self.conv_weights()
        if "t" in stages:
            self.setup_tables()
            P.barrier()
        if "s" in stages:
            self.setup_s5()
        self.pump(len(self.conv_q))
        P.barrier()
        tok0 = 0
        for L in self.seq_lens:
            NB = 8 if L % 1024 == 0 else 4
            if "A" in self.phases:
                self.phaseA(tok0, L, NB)
                P.barrier()
            if "B" in self.phases:
                self.phaseB(L)
                P.barrier()
            if "C" in self.phases:
                self.phaseC1(L)
                P.barrier()
                self.phaseC2(L, NB)
                P.barrier()
            if "D" in self.phases:
                self.phaseD1(L)
                P.barrier()
                self.phaseD2(tok0, L, NB)
                P.barrier()
            tok0 += L
        P.emit(final_wait_keys=["yst", "xst", "stg0", "stg1", "vst", "yst0", "yst1", "s5m", "s5w", "s5q0", "s5q1", "gdbg", "tabst"])
        return self.nc


SEQ_LENS = [4096, 2048, 2048, 2048, 2048]
_CACHE = {}


def kernel(**inputs):
    xp = np.asarray(inputs["x_prompt"], dtype=np.float32)
    xs = np.asarray(inputs["x_sample"], dtype=np.float32)
    consts = host_constants()
    shared = {nm: np.ascontiguousarray(np.asarray(inputs[nm], dtype=np.float32)) for nm, _ in WEIGHT_SPECS}
    shared.update(consts)
    if "nc" not in _CACHE:
        _CACHE["nc"] = MK(SEQ_LENS).build()
    nc = _CACHE["nc"]
    in_maps = []
    for c in range(8):
        xin = np.concatenate([xp[c].reshape(4096, D), xs[4 * c:4 * c + 4].reshape(4 * 2048, D)], axis=0)
        m = dict(shared)
        m["x_in"] = np.ascontiguousarray(xin)
        in_maps.append(m)
    res = run_bass_kernel_spmd(nc, in_maps, core_ids=list(range(8)))
    yp = np.zeros((8, 4096, D), np.float32)
    ys = np.zeros((32, 2048, D), np.float32)
    for c in range(8):
        y = np.asarray(res.results[c]["y_out"]).reshape(-1, D)
        yp[c] = y[:4096]
        ys[4 * c:4 * c + 4] = y[4096:].reshape(4, 2048, D)
    return (yp, ys)
```

```python
import math
from contextlib import ExitStack
import numpy as np
import ml_dtypes
import concourse.bass as bass
import concourse.mybir as mybir
from concourse.bass_utils import run_bass_kernel_spmd

F32 = mybir.dt.float32
BF16 = mybir.dt.bfloat16
AF = mybir.ActivationFunctionType
ALU = mybir.AluOpType
NPBF = ml_dtypes.bfloat16

D = 1024
DFF = 2816
NFC = 22
EPS = 1e-6
EPOCH_MAX = 12000


def AP(t, off, dims):
    return bass.AP(t, off, [list(d) for d in dims])


class Prog:
    def __init__(self, nc):
        self.nc = nc
        self.ops = []
        self.barriers = []

    def add(self, eng, fn, r=(), w=(), dma=None):
        self.ops.append((eng, fn, tuple(r), tuple(w), dma))

    def barrier(self):
        self.barriers.append(len(self.ops))

    def emit(self, final_wait_keys=()):
        nc = self.nc
        ops = self.ops
        n = len(ops)
        engs = ("pe", "act", "dve", "pool", "sp")
        lastw, lastr = {}, {}
        deps = [None] * n
        bar_set = set(self.barriers)
        last_on_eng = {}
        last_dma = {}
        pending_bar = {}
        prev_same = {}
        for i, (eng, fn, r, w, dma) in enumerate(ops):
            if i in bar_set:
                allprev = list(last_on_eng.values()) + list(last_dma.values())
                for e in engs:
                    pending_bar[e] = list(allprev)
            me = ("dma", dma) if dma else eng
            d = set()
            for k in r:
                d.update(lastw.get(k, {}).values())
            for k in w:
                d.update(lastw.get(k, {}).values())
                d.update(lastr.get(k, {}).values())
            if eng in pending_bar:
                d.update(pending_bar.pop(eng))
            if dma and dma in last_dma:
                d.add(last_dma[dma])
            dd = []
            for j in d:
                ej, _, _, _, dj = ops[j]
                if dj is None and dma is None and ej == eng and (eng == "pe" or j == prev_same.get(eng, -1) and False):
                    continue
                dd.append(j)
            deps[i] = dd
            for k in r:
                lastr.setdefault(k, {})[me] = i
            for k in w:
                lastw.setdefault(k, {})[me] = i
            if dma:
                last_dma[dma] = i
            else:
                last_on_eng[eng] = i
        signalled = [False] * n
        for i in range(n):
            for j in deps[i]:
                signalled[j] = True
        cnt = {e: 0 for e in engs}
        epoch = {e: 0 for e in engs}
        dcnt = {}
        semof = [None] * n
        valof = [0] * n
        for i, (eng, fn, r, w, dma) in enumerate(ops):
            if dma:
                dcnt[dma] = dcnt.get(dma, 0) + 16
                semof[i] = ("dma", dma)
                valof[i] = dcnt[dma]
            elif signalled[i]:
                if cnt[eng] >= EPOCH_MAX:
                    epoch[eng] += 1
                    cnt[eng] = 0
                cnt[eng] += 1
                semof[i] = (eng, epoch[eng])
                valof[i] = cnt[eng]
        semnames = sorted(set(s for s in semof if s is not None), key=str)
        self.n_sems = len(semnames)
        with ExitStack() as es:
            sems = {}
            for sname in semnames:
                nm = "s_" + "_".join(str(x) for x in sname)
                sems[sname] = es.enter_context(nc.semaphore(nm))
            block = es.enter_context(nc.Block())
            per_eng = {e: [] for e in engs}
            for i, op in enumerate(ops):
                per_eng[op[0]].append(i)

            def run_engine(e_name, eobj):
                waited = {}
                for i in per_eng[e_name]:
                    eng, fn, r, w, dma = ops[i]
                    need = {}
                    for j in deps[i]:
                        sj = semof[j]
                        if valof[j] > need.get(sj, 0):
                            need[sj] = valof[j]
                    for sj, v in need.items():
                        if waited.get(sj, 0) >= v:
                            continue
                        waited[sj] = v
                        eobj.wait_ge(sems[sj], v)
                    ins = fn(eobj)
                    if dma:
                        ins.then_inc(sems[semof[i]], 16)
                    elif signalled[i]:
                        ins.then_inc(sems[semof[i]], 1)
                if e_name == "sp":
                    for k in final_wait_keys:
                        if ("dma", k) in sems and dcnt.get(k, 0) > 0:
                            eobj.wait_ge(sems[("dma", k)], dcnt[k])

            @block.tensor
            def _(e):
                run_engine("pe", e)

            @block.scalar
            def _(e):
                run_engine("act", e)

            @block.vector
            def _(e):
                run_engine("dve", e)

            @block.gpsimd
            def _(e):
                run_engine("pool", e)

            @block.sync
            def _(e):
                run_engine("sp", e)


GQA_PAIRS = [(0, 4), (1, 5), (2, 6), (3, 7), (8, 12), (9, 13), (10, 14), (11, 15)]
NA_CLASSES = ["first", "second", "interior", "slast", "last"]
NA_OFFS = {"first": [0, 1, 2, 3], "second": [-1, 0, 1, 2], "interior": [-2, -1, 0, 1, 2],
           "slast": [-2, -1, 0, 1], "last": [-3, -2, -1, 0]}
NA_PBASE = {"first": 0, "second": 4, "interior": 8, "slast": 13, "last": 17}


def _t5_bucket_np(rel):
    half, max_exact = 16, 8
    ret = np.where(rel > 0, half, 0)
    n = np.abs(rel)
    nf = np.maximum(n, 1).astype(np.float32)
    large = max_exact + (np.log(nf / np.float32(max_exact)) / np.float32(math.log(128 / max_exact))
                         * np.float32(half - max_exact)).astype(np.int32)
    large = np.minimum(large, half - 1)
    return ret + np.where(n < max_exact, n, large)


def na_pattern(cls):
    R = 64
    n = {"first": 0, "second": 1, "interior": 10, "slast": 30, "last": 31}[cls]
    out = []
    for off in NA_OFFS[cls]:
        m = n + off
        blk = [[None, None], [None, None]]
        for kr in range(2):
            for qr in range(2):
                r = 2 * n + qr
                krow = 2 * m + kr
                rs = min(max(r - 4, 0), R - 8)
                if rs <= krow < rs + 8:
                    blk[kr][qr] = krow - r + 7
        out.append(blk)
    return out


def host_constants():
    c = {}
    c["c_ident_bf"] = np.eye(128, dtype=np.float32).astype(NPBF)
    c["c_ident_f"] = np.eye(128, dtype=np.float32)
    sel = np.zeros((128, 64, 128), np.float32)
    selT = np.zeros((128, 64, 128), np.float32)
    for g8 in range(8):
        for jl in range(8):
            for cc in range(16):
                sel[16 * g8 + cc, g8 * 8 + jl, 16 * jl + cc] = 1.0
                selT[16 * jl + cc, g8 * 8 + jl, 16 * g8 + cc] = 1.0
    c["c_sel"] = sel.astype(NPBF)
    c["c_selT"] = selT.astype(NPBF)
    jl = np.arange(128) // 16
    msk = np.zeros((128, 2, 128), np.float32)
    msk[:, 0, :] = (jl[:, None] <= jl[None, :])
    msk[:, 1, :] = (jl[:, None] >= jl[None, :])
    c["c_mask"] = msk
    j = np.arange(32, dtype=np.float32)
    ex = np.zeros((64, 3, 2, 32), np.float32)
    ex[:, 0, 0, :] = 31 - j
    ex[:, 0, 1, :] = j
    ex[:, 1, 0, :] = -1 - j
    ex[:, 1, 1, :] = j - 32
    ex[:, 2, 0, :] = j + 1
    ex[:, 2, 1, :] = 32 - j
    c["c_expo"] = ex
    oh = np.zeros((31, 64, 128), np.float32)
    for qc in range(64):
        cs = min(max(qc - 8, 0), 48)
        for kc in range(cs, cs + 16):
            rc = kc - qc + 15
            oh[rc, qc, kc] = 1.0
            oh[rc, qc, 64 + kc] = 1.0
    c["c_ohc"] = oh.astype(NPBF)
    rp = np.arange(512) - 256
    bk = _t5_bucket_np(rp)
    og = np.zeros((32, 512), np.float32)
    for i in range(512):
        if abs(rp[i]) <= 128:
            og[bk[i], i] = 1.0
    c["c_ohg"] = og.astype(NPBF)
    return c


WEIGHT_SPECS = [
    ("norm_ffn", [2, 2, 1024]), ("w_ffn_gate", [2, 2, 1024, 2816]), ("w_ffn_up", [2, 2, 1024, 2816]),
    ("w_ffn_down", [2, 2, 2816, 1024]), ("norm_mix", [2, 1024]), ("w_in_even", [1, 1024, 2048]),
    ("s5_lam_re", [1, 2, 32, 64]), ("s5_lam_im", [1, 2, 32, 64]), ("s5_log_dt", [1, 2, 32]),
    ("s5_b_re", [1, 2, 32, 64, 16]), ("s5_b_im", [1, 2, 32, 64, 16]), ("s5_c_re", [1, 2, 32, 16, 64]),
    ("s5_c_im", [1, 2, 32, 16, 64]), ("s5_d", [1, 512]), ("s5_w_glu", [1, 512, 512]), ("s5_b_glu", [1, 512]),
    ("na_rpb", [1, 8, 15, 31]), ("w_out_even", [1, 1024, 1024]), ("w_in_odd", [1, 1024, 1536]),
    ("gqa_sink", [1, 16]), ("w_out_odd", [1, 1024, 1024]), ("t5_table", [32, 16]), ("norm_final", [1024]),
]
CONST_SPECS = [("c_ident_bf", [128, 128], BF16), ("c_ident_f", [128, 128], F32), ("c_sel", [128, 64, 128], BF16),
               ("c_selT", [128, 64, 128], BF16), ("c_mask", [128, 2, 128], F32), ("c_expo", [64, 3, 2, 32], F32),
               ("c_ohc", [31, 64, 128], BF16), ("c_ohg", [32, 512], BF16)]

DTSIZE = {F32: 4, BF16: 2}


class TT:
    def __init__(self, h, shape, dtype):
        self.h = h
        self.shape = list(shape)
        self.dtype = dtype
        self.row = int(np.prod(shape[1:]))
        st = []
        s = 1
        for d in reversed(shape[1:]):
            st.append(s)
            s *= d
        self.strides = list(reversed(st))

    def ap(self, off, dims, p0=0, pn=None):
        if pn is None:
            pn = self.shape[0] - p0
        return AP(self.h, p0 * self.row + off, [[self.row, pn]] + [list(d) for d in dims])

    def __getitem__(self, idx):
        return self.h[idx]


class DT:
    def __init__(self, h, shape):
        self.h = h
        self.shape = list(shape)

    def ap(self, off, dims):
        return AP(self.h, off, [list(d) for d in dims])


class SBAlloc:
    def __init__(self, nc, base=16512, limit=229376 - 64):
        self.nc = nc
        self.ptr = base
        self.limit = limit
        self.n = 0
        self.hi = base

    def alloc(self, name, shape, dtype):
        size = int(np.prod(shape[1:])) * DTSIZE[dtype]
        size = (size + 63) // 64 * 64
        assert self.ptr + size <= self.limit, f"SBUF overflow at {name}: {self.ptr}+{size}"
        self.n += 1
        h = self.nc.alloc_sbuf_tensor_at(f"{name}_{self.n}", list(shape), dtype, offset=self.ptr)
        self.ptr += size
        self.hi = max(self.hi, self.ptr)
        return TT(h, shape, dtype)


class MK:
    def __init__(self, seq_lens, dbg=False, phases="ABCD"):
        self.seq_lens = list(seq_lens)
        self.NTOK = sum(seq_lens)
        self.LMAX = max(seq_lens)
        self.dbg = dbg
        self.phases = phases
        self.nc = nc = bass.Bass("TRN2", target_bir_lowering=False)
        self.P = Prog(nc)
        self.rr = 0
        self.bankc = 0
        self.out_keys = []
        self.store_eng = "sp"
        self.io()
        self.alloc()

    def dram(self, name, shape, dtype, kind="Internal"):
        h = self.nc.dram_tensor(name, list(shape), dtype, kind=kind)
        return DT(h, shape)

    def io(self):
        W = {}
        W["x_in"] = self.dram("x_in", [self.NTOK, D], F32, "ExternalInput")
        self.y_out = self.dram("y_out", [self.NTOK, D], F32, "ExternalOutput")
        for nm, shp in WEIGHT_SPECS:
            W[nm] = self.dram(nm, shp, F32, "ExternalInput")
        for nm, shp, dt_ in CONST_SPECS:
            W[nm] = self.dram(nm, shp, dt_, "ExternalInput")
        self.W = W
        L = self.LMAX
        kd = "ExternalOutput" if self.dbg else "Internal"
        S = {}
        S["Wgu"] = self.dram("s_Wgu", [4, 22, 128, 2048], BF16)
        S["Wd"] = self.dram("s_Wd", [4, 2, 22, 128, 512], BF16)
        S["WinE"] = self.dram("s_WinE", [4, 128, 8, 512], BF16)
        S["WoutE"] = self.dram("s_WoutE", [2, 128, 8, 512], BF16)
        S["WinO"] = self.dram("s_WinO", [3, 128, 8, 512], BF16)
        S["WoutO"] = self.dram("s_WoutO", [2, 128, 8, 512], BF16)
        S["Wglu"] = self.dram("s_Wglu", [128, 4, 512], BF16)
        S["Ws5"] = self.dram("s_Ws5", [2, 32, 128, 4, 192], BF16, kd)
        S["QT"] = self.dram("s_QT", [2, 32, 64, 2, 512], BF16, kd)
        S["M"] = self.dram("s_M", [32, 128, 7, 128], BF16, kd)
        S["TNA"] = self.dram("s_TNA", [128, 8 * 21 * 128], BF16)
        S["TGQ"] = self.dram("s_TGQ", [128, 16 * 3 * 128], BF16)
        S["x"] = self.dram("s_x", [L, D], F32, kd)
        S["u"] = self.dram("s_u", [4, 128, L], BF16, kd)
        S["q"] = self.dram("s_q", [4, 128, L], BF16, kd)
        S["k"] = self.dram("s_k", [4, 128, L], BF16, kd)
        S["v"] = self.dram("s_v", [L, 528], BF16, kd)
        S["ya"] = self.dram("s_ya", [4, 128, L], BF16, kd)
        S["g"] = self.dram("s_g", [4, 128, L], BF16, kd)
        S["q2"] = self.dram("s_q2", [8, 128, L], BF16, kd)
        S["k2"] = self.dram("s_k2", [2, 128, L], BF16, kd)
        S["v2"] = self.dram("s_v2", [L, 264], BF16, kd)
        self.S = S

    def alloc(self):
        nc = self.nc
        sb = self.sb = SBAlloc(nc)
        a = sb.alloc
        self.identb = a("identb", [128, 128], BF16)
        self.identf = a("identf", [128, 128], F32)
        self.G = a("G", [128, 6, 8], F32)
        self.Gfin = a("Gfin", [128, 1024], F32)
        self.es = a("es", [128, 16], F32)
        self.cm05 = a("cm05", [128, 2], F32)
        self.st = a("st", [128, 8, 4], F32)
        self.A32 = a("A32", [64, 2, 2, 32], F32)
        self.Ac2 = a("Ac2", [64, 2, 2, 32], F32)
        self.bglu = a("bglu", [128, 4], F32)
        self.rec = a("rec", [128, 16], F32)
        self.halfpi = a("halfpi", [128, 2], F32)
        self.psum = TT(nc.alloc_psum_tensor("psum", [128, 4096], F32), [128, 4096], F32)
        self.arena0 = sb.ptr

    def ps(self, bank, off=0, n=512, p0=0, pn=128):
        return self.psum.ap(bank * 512 + off, [[1, n]], p0=p0, pn=pn)

    def psv(self, bank, off, dims, p0=0, pn=128):
        return self.psum.ap(bank * 512 + off, dims, p0=p0, pn=pn)

    def psb(self, bank, off=0, n=1024, p0=0, pn=128):
        full = self.psum.ap(bank * 512, [[1, 512]], p0=p0, pn=pn).bitcast(BF16)
        return full[:, off:off + n]

    def pk(self, bank):
        return ("ps", bank)

    def nbank(self):
        b = self.bankc % 8
        self.bankc += 1
        return b

    def ew(self):
        self.rr += 1
        return "act" if self.rr % 2 else "dve"

    def copy(self, eng, out, in_, r, w):
        if eng == "act":
            self.P.add("act", lambda e: e.copy(out=out, in_=in_), r=r, w=w)
        elif eng == "dve":
            self.P.add("dve", lambda e: e.tensor_copy(out=out, in_=in_), r=r, w=w)
        else:
            self.P.add("pool", lambda e: e.tensor_copy(out=out, in_=in_), r=r, w=w)

    def load(self, out, in_, r, w, key):
        self.P.add("sp", lambda e: e.dma_start(out=out, in_=in_), r=r, w=w, dma=key)

    def store(self, out, in_, r, w, key):
        self.P.add(self.store_eng, lambda e: e.dma_start(out=out, in_=in_), r=r, w=w, dma=key)

    def mm(self, out, lhsT, rhs, start, stop, r, w):
        self.P.add("pe", lambda e: e.matmul(out, lhsT=lhsT, rhs=rhs, start=start, stop=stop), r=r, w=w)

    def tr(self, out, in_, ident, r, w):
        self.P.add("pe", lambda e: e.transpose(out=out, in_=in_, identity=ident), r=r, w=w)

    def tt(self, eng, out, in0, in1, op, r, w):
        self.P.add(eng, lambda e: e.tensor_tensor(out=out, in0=in0, in1=in1, op=op), r=r, w=w)

    def ts(self, eng, out, in0, s1, s2, op0, op1, r, w):
        if s2 is None:
            self.P.add(eng, lambda e: e.tensor_scalar(out=out, in0=in0, scalar1=s1, scalar2=None, op0=op0), r=r, w=w)
        else:
            self.P.add(eng, lambda e: e.tensor_scalar(out=out, in0=in0, scalar1=s1, scalar2=s2, op0=op0, op1=op1), r=r, w=w)

    def stt(self, out, in0, scalar, in1, op0, op1, r, w):
        self.P.add("dve", lambda e: e.scalar_tensor_tensor(out=out, in0=in0, scalar=scalar, in1=in1, op0=op0, op1=op1), r=r, w=w)

    def act(self, out, in_, func, r, w, scale=1.0, bias=0.0, accum=None):
        if accum is not None:
            self.P.add("act", lambda e: e.activation(out=out, in_=in_, func=func, scale=scale, bias=bias, accum_out=accum), r=r, w=w)
        else:
            self.P.add("act", lambda e: e.activation(out=out, in_=in_, func=func, scale=scale, bias=bias), r=r, w=w)

    def setup_consts(self):
        W = self.W
        P = self.P
        self.load(self.identb[:], W["c_ident_bf"].h.ap(), [], ["identb"], "c0")
        self.load(self.identf[:], W["c_ident_f"].h.ap(), [], ["identf"], "c0")
        P.add("pool", lambda e: e.memset(self.cm05[:, 0:1], -0.5), w=["cm05"])
        a = self.sb.alloc
        gt = a("gtmp", [52, 128], F32)
        self.load(gt.ap(0, [[1, 128]], p0=48, pn=4), W["s5_b_glu"].ap(0, [[128, 4], [1, 128]]), [], ["gt"], "c1")
        self.load(gt.ap(0, [[1, 128]], p0=0, pn=32), W["norm_ffn"].ap(0, [[128, 32], [1, 128]]), [], ["gt"], "c1")
        self.load(gt.ap(0, [[1, 128]], p0=32, pn=16), W["norm_mix"].ap(0, [[128, 16], [1, 128]]), [], ["gt"], "c1")
        P.add("pe", lambda e: e.matmul(self.ps(0, 0, 52), lhsT=gt.ap(0, [[1, 128]], pn=52), rhs=self.identf.ap(0, [[1, 52]], pn=52),
                                       start=True, stop=True), r=["gt", "identf"], w=[self.pk(0)])
        src_row = {0: 0, 2: 8, 3: 16, 5: 24, 1: 32, 4: 40}
        for gi, r0 in src_row.items():
            self.copy("dve", self.G.ap(gi * 8, [[1, 8]]), self.ps(0, r0, 8), [self.pk(0)], ["G"])
        self.load(self.Gfin[:], W["norm_final"].ap(0, [[0, 128], [1, 1024]]), [], ["Gfin"], "c1")
        self.load(self.es[:], W["gqa_sink"].ap(0, [[0, 128], [1, 16]]), [], ["es"], "c1")
        self.act(self.es[:], self.es[:], AF.Exp, ["es"], ["es"])
        self.copy("dve", self.bglu.ap(0, [[1, 4]]), self.ps(0, 48, 4), [self.pk(0)], ["bglu"])

    def conv_weights(self):
        sb = self.sb
        mark = sb.ptr
        NCS = 2
        sb.ptr = sb.limit - NCS * 24576 - 512
        cf = [sb.alloc(f"cf{i}", [128, 4096], F32) for i in range(NCS)]
        cb = [sb.alloc(f"cb{i}", [128, 4096], BF16) for i in range(NCS)]
        self.cj = 0
        self.conv_q = []

        def job(src_pieces, dst_pieces, n, key):
            self.conv_q.append(lambda: job_emit(src_pieces, dst_pieces, n, key))

        def job_emit(src_pieces, dst_pieces, n, key):
            s = self.cj % NCS
            self.cj += 1
            for pi_, (sap, off, dims) in enumerate(src_pieces):
                self.P.add("pool", lambda e, o_=cf[s].ap(off, dims), i_=sap: e.dma_start(out=o_, in_=i_), r=[], w=[("cf", s)], dma=f"cf{s}p{pi_}")
            fl = [[1, n]]
            eng = "act"
            self.copy(eng, cb[s].ap(0, fl), cf[s].ap(0, fl), [("cf", s)], [("cb", s)])
            for pi_, (dap, off, dims) in enumerate(dst_pieces):
                self.P.add("pool", lambda e, dap=dap, off=off, dims=dims: e.dma_start(out=dap, in_=cb[s].ap(off, dims)),
                           r=[("cb", s)], w=[key], dma=f"cs{s}p{pi_}")

        W, S = self.W, self.S
        for f in range(4):
            for gu, nm in enumerate(["w_ffn_gate", "w_ffn_up"]):
                base = f * 1024 * 2816
                for fc0 in range(0, 22, 4):
                    nf = min(4, 22 - fc0)
                    sp, dp = [], []
                    for j in range(nf):
                        fc = fc0 + j
                        sp.append((W[nm].ap(base + fc * 128, [[2816, 128], [128 * 2816, 8], [1, 128]]), j * 1024, [[128, 8], [1, 128]]))
                        dp.append((S["Wgu"].ap(((f * 22 + fc) * 128) * 2048 + gu * 1024, [[2048, 128], [1, 1024]]), j * 1024, [[1, 1024]]))
                    job(sp, dp, nf * 1024, "Wgu")
            base = f * 2816 * 1024
            for half in range(2):
                for fc0 in range(0, 22, 8):
                    nf = min(8, 22 - fc0)
                    sap = W["w_ffn_down"].ap(base + fc0 * 128 * 1024 + half * 512, [[1024, 128], [128 * 1024, nf], [1, 512]])
                    dst = S["Wd"].ap((((f * 2 + half) * 22 + fc0) * 128) * 512, [[512, 128], [128 * 512, nf], [1, 512]])
                    job([(sap, 0, [[512, nf], [1, 512]])], [(dst, 0, [[512, nf], [1, 512]])], nf * 512, "Wd")

        def generic(nm, ncols, slabs, dstname):
            for si, c0 in enumerate(slabs):
                sap = W[nm].ap(c0, [[ncols, 128], [128 * ncols, 8], [1, 512]])
                dst = S[dstname].ap(si * 128 * 4096, [[4096, 128], [1, 4096]])
                job([(sap, 0, [[512, 8], [1, 512]])], [(dst, 0, [[1, 4096]])], 4096, dstname)

        generic("w_in_even", 2048, [0, 512, 1024, 1536], "WinE")
        generic("w_out_even", 1024, [0, 512], "WoutE")
        generic("w_out_odd", 1024, [0, 512], "WoutO")
        for si in range(2):
            pieces = []
            for t_ in range(4):
                for e_ in range(2):
                    h = GQA_PAIRS[si * 4 + t_][e_]
                    sap = W["w_in_odd"].ap(h * 64, [[1536, 128], [128 * 1536, 8], [1, 64]])
                    pieces.append((sap, t_ * 128 + e_ * 64, [[512, 8], [1, 64]]))
            dst = S["WinO"].ap(si * 128 * 4096, [[4096, 128], [1, 4096]])
            job(pieces, [(dst, 0, [[1, 4096]])], 4096, "WinO")
        sap = W["w_in_odd"].ap(1024, [[1536, 128], [128 * 1536, 8], [1, 512]])
        dst = S["WinO"].ap(2 * 128 * 4096, [[4096, 128], [1, 4096]])
        job([(sap, 0, [[512, 8], [1, 512]])], [(dst, 0, [[1, 4096]])], 4096, "WinO")
        sap = W["s5_w_glu"].ap(0, [[512, 128], [128 * 512, 4], [1, 512]])
        dst = S["Wglu"].ap(0, [[2048, 128], [1, 2048]])
        job([(sap, 0, [[512, 4], [1, 512]])], [(dst, 0, [[1, 2048]])], 2048, "Wglu")
        sb.ptr = mark

    def pump(self, n):
        for _ in range(n):
            if self.conv_q:
                self.conv_q.pop(0)()


    def alloc_ffn(self, NB, vkind=None):
        sb = self.sb
        sb.ptr = self.arena0
        a = sb.alloc
        self.NB = NB
        NT = NB * 128
        self.NT = NT
        self.xts = [a(f"xt{i}", [128, NB, 1024], F32) for i in range(2)]
        self.xt, self.xpar = self.xts[0], 0
        self.xs = [a(f"xs{i}", [128, 1024], BF16) for i in range(2)]
        self.junk = a("junk", [128, 1024], BF16)
        self.hn = a("hn", [128, 8, NT], BF16)
        self.h = a("h", [128, 22, NT], BF16)
        self.sg = [a(f"sg{i}", [128, 512], BF16) for i in range(2)]
        self.NGU = 3
        self.gu = [a(f"gu{i}", [128, 2, 8, 128], BF16) for i in range(self.NGU)]
        self.ND = 24
        self.dr = [a(f"dr{i}", [128, 512], BF16) for i in range(self.ND)]
        self.NPW = 2
        self.pw = [a(f"pw{i}", [128, 8, 512], BF16) for i in range(self.NPW)]
        self.stg = [a(f"stg{i}", [128, 2, NT], BF16) for i in range(2)]
        if vkind == "v":
            self.vst = a("vst", [128, NB, 8, 66], BF16)
        if vkind == "v2":
            self.v2st = a("v2st", [128, NB, 4, 66], BF16)
        self.cnt_gu = self.cnt_d = self.cnt_pw = self.cnt_stg = 0
        self.dgrp = 0
        P = self.P
        if vkind == "v":
            va = self.vst.ap(64, [[66, NB * 8], [1, 2]])
            P.add("pool", lambda e, va=va: e.memset(va, 1.0), w=["vst"])
        if vkind == "v2":
            vb = self.v2st.ap(64, [[66, NB * 4], [1, 2]])
            P.add("pool", lambda e, vb=vb: e.memset(vb, 1.0), w=["v2st"])

    def alloc_attn(self, kind):
        sb = self.sb
        sb.ptr = self.arena0
        a = sb.alloc
        self.NB = 4
        self.NT = 512
        self.xts = [a(f"xt{i}", [128, 4, 1024], F32) for i in range(2)]
        self.qTs = [a(f"qT{i}", [128, 8, 512], BF16) for i in range(2)]
        self.kTs = [a(f"kT{i}", [128, 4, 1024], BF16) for i in range(2)]
        self.vTs = [a(f"vT{i}", [128, 8, 528], BF16) for i in range(2)]
        self.yaTs = [a(f"yaT{i}", [128, 4, 512], BF16) for i in range(2)]
        self.set_par(0)
        self.mix = a("mix", [128, 8, 512], BF16)
        self.E = [a(f"E{i}", [128, 640], F32) for i in range(4)]
        self.Pm = [a(f"Pm{i}", [128, 640], BF16) for i in range(4)]
        self.yb = [a(f"yb{i}", [128, 1024], BF16) for i in range(2)]
        self.dsum = a("dsum", [128, 16], F32)
        self.NPW = 2
        self.pw = [a(f"pw{i}", [128, 8, 512], BF16) for i in range(self.NPW)]
        self.cnt_pw = 0
        n = 8 * 21 * 128 if kind == "na" else 16 * 3 * 128
        self.TAB = a("TAB", [128, n], BF16)
        self.load(self.TAB.ap(0, [[1, n]]), self.S["TNA" if kind == "na" else "TGQ"].ap(0, [[n, 128], [1, n]]),
                  ["TABscr"], ["TAB"], "tabld")
        for par in range(2):
            zt = self.qTs[par] if kind == "na" else self.kTs[par]
            za = zt.ap(0, [[1, 4096]])
            self.P.add("pool", lambda e, za=za: e.memset(za, 0.0), w=[("qT", par), ("kT", par)])

    def set_par(self, par):
        self.xpar = par
        self.xt = self.xts[par]
        if hasattr(self, "qTs"):
            self.qT, self.kT, self.vT, self.yaT = self.qTs[par], self.kTs[par], self.vTs[par], self.yaTs[par]

    def stats(self, blk):
        st, x = self.st, self.xt
        self.act(self.junk[:], x.ap(blk * 1024, [[1, 1024]]), AF.Square, [("x", self.xpar, blk)], ["junk", ("st", blk)],
                 accum=st.ap(blk * 4, [[1, 1]]))
        self.ts("dve", st.ap(blk * 4 + 1, [[1, 1]]), st.ap(blk * 4, [[1, 1]]), 1.0 / 1024, EPS, ALU.mult, ALU.add,
                [("st", blk)], [("st", blk)])
        self.tt("pool", st.ap(blk * 4 + 2, [[1, 1]]), st.ap(blk * 4 + 1, [[1, 1]]), self.cm05.ap(0, [[1, 1]]), ALU.pow,
                [("st", blk), "cm05"], [("st", blk)])

    def norm_fm(self, gidx):
        x, NT = self.xt, self.NT
        for blk in range(self.NB):
            self.stats(blk)
        for blk in range(self.NB):
            xs = self.xs[blk % 2]
            self.act(xs[:], x.ap(blk * 1024, [[1, 1024]]), AF.Copy, [("x", self.xpar, blk), ("st", blk)], [("xs", blk % 2)],
                     scale=self.st.ap(blk * 4 + 2, [[1, 1]]))
            bank = blk % 2
            for kc in range(8):
                self.tr(self.psb(bank, kc * 128, 128), xs.ap(kc * 128, [[1, 128]]), self.identb[:],
                        [("xs", blk % 2), "identb"], [self.pk(bank)])
            self.tt("dve", self.hn.ap(blk * 128, [[NT, 8], [1, 128]]),
                    self.psb(bank).rearrange("p (k t) -> p k t", k=8),
                    self.G.ap(gidx * 8, [[1, 8], [0, 128]]), ALU.mult, [self.pk(bank), "G"], [("hn", blk // 4)])

    def ffn(self, f, gidx):
        S, x, NT = self.S, self.xt, self.NT
        NH = self.NB // 4
        self.norm_fm(gidx)
        c = 0
        for fc in range(NFC):
            s = self.cnt_gu % self.NGU
            self.cnt_gu += 1
            gut = self.gu[s]
            self.load(gut.ap(0, [[1, 2048]]), S["Wgu"].ap((f * 22 + fc) * 128 * 2048, [[2048, 128], [1, 2048]]),
                      ["Wgu"], [("gu", s)], f"gu{s}")
            for th in range(NH):
                st_ = c % 2
                c += 1
                gb, ub = 2 * st_, 2 * st_ + 1
                for kc in range(8):
                    self.mm(self.ps(gb), gut.ap(kc * 128, [[1, 128]]), self.hn.ap(kc * NT + th * 512, [[1, 512]]), kc == 0, kc == 7,
                            [("gu", s), ("hn", th)], [self.pk(gb)])
                for kc in range(8):
                    self.mm(self.ps(ub), gut.ap(1024 + kc * 128, [[1, 128]]), self.hn.ap(kc * NT + th * 512, [[1, 512]]), kc == 0, kc == 7,
                            [("gu", s), ("hn", th)], [self.pk(ub)])
                self.act(self.sg[st_][:], self.ps(gb), AF.Silu, [self.pk(gb)], [("sg", st_)])
                self.tt("dve", self.h.ap(fc * NT + th * 512, [[1, 512]]), self.ps(ub), self.sg[st_][:], ALU.mult,
                        [self.pk(ub), ("sg", st_)], [("h", fc, th)])
        for half in range(2):
            slots = []
            for fc in range(NFC):
                s = self.cnt_d % self.ND
                self.cnt_d += 1
                slots.append(s)
                self.load(self.dr[s][:], S["Wd"].ap(((f * 2 + half) * 22 + fc) * 128 * 512, [[512, 128], [1, 512]]),
                          ["Wd"], [("dr", s)], f"dr{s % 6}")
            for pg in range(self.NB // 2):
                bk = (4, 5) if self.dgrp % 2 == 0 else (6, 7)
                self.dgrp += 1
                for fc in range(NFC):
                    s = slots[fc]
                    for j in range(2):
                        blk = pg * 2 + j
                        self.mm(self.ps(bk[j]), self.h.ap(fc * NT + blk * 128, [[1, 128]]), self.dr[s][:],
                                fc == 0, fc == NFC - 1, [("h", fc, blk // 4), ("dr", s)], [self.pk(bk[j])])
                for j in range(2):
                    blk = pg * 2 + j
                    xa = x.ap(blk * 1024 + half * 512, [[1, 512]])
                    self.stt(xa, self.ps(bk[j]), 0.5, xa, ALU.mult, ALU.add, [self.pk(bk[j]), ("x", self.xpar, blk)], [("x", self.xpar, blk)])

    def pw_load(self, wname, slab):
        s = self.cnt_pw % self.NPW
        self.cnt_pw += 1
        self.load(self.pw[s].ap(0, [[1, 4096]]), self.S[wname].ap(slab * 128 * 4096, [[4096, 128], [1, 4096]]),
                  [wname], [("pw", s)], f"pw{s}")
        return s

    def proj_fm(self, s, ots, dst_name, dst_tile0, tok, dkeys):
        NT = self.NT
        LM = self.LMAX
        for c0 in range(0, len(ots), 2):
            grp = ots[c0:c0 + 2]
            g = self.cnt_stg % 2
            self.cnt_stg += 1
            stg = self.stg[g]
            for i, ot in enumerate(grp):
                for th in range(NT // 512):
                    bank = self.nbank() % 4
                    for kc in range(8):
                        self.mm(self.ps(bank), self.pw[s].ap(kc * 512 + ot * 128, [[1, 128]]), self.hn.ap(kc * NT + th * 512, [[1, 512]]),
                                kc == 0, kc == 7, [("pw", s), ("hn", th)], [self.pk(bank)])
                    self.copy(self.ew(), stg.ap(i * NT + th * 512, [[1, 512]]), self.ps(bank), [self.pk(bank)], [("stg", g)])
            n = len(grp)
            self.store(self.S[dst_name].ap((dst_tile0 + c0) * 128 * LM + tok, [[LM, 128], [128 * LM, n], [1, NT]]),
                       stg.ap(0, [[NT, n], [1, NT]]), [("stg", g)], dkeys, f"stg{g}")

    def proj_tm_v(self, s, col0, nh, vst, vkey, dst_name, rowlen, tok, dkeys):
        ncol = nh * 64
        NB, NT = self.NB, self.NT
        for blk in range(NB):
            bank = 4 + blk % 4
            for kc in range(8):
                self.mm(self.ps(bank, 0, ncol), self.hn.ap(kc * NT + blk * 128, [[1, 128]]),
                        self.pw[s].ap(kc * 512 + col0, [[1, ncol]]), kc == 0, kc == 7, [("pw", s), ("hn", blk // 4)], [self.pk(bank)])
            self.copy(self.ew(), vst.ap(blk * nh * 66, [[66, nh], [1, 64]]), self.psv(bank, 0, [[64, nh], [1, 64]]),
                      [self.pk(bank)], [vkey])
        self.store(self.S[dst_name].ap(tok * rowlen, [[rowlen, 128], [128 * rowlen, NB], [1, rowlen]]),
                   vst.ap(0, [[rowlen, NB], [1, rowlen]]), [vkey], dkeys, "vst")

    def xkeys(self, row0):
        return [("xscr", t) for t in range(row0 // 512, (row0 + self.NT) // 512)]

    def x_load(self, src, row0, par):
        NB = self.NB
        keys = [("x", par, b) for b in range(NB)]
        self.load(self.xts[par].ap(0, [[1024, NB], [1, 1024]]), src.ap(row0 * D, [[D, 128], [128 * D, NB], [1, D]]),
                  self.xkeys(row0) if src is self.S["x"] else [], keys, f"xld{par}")

    def x_store(self, dst, row0, key):
        NB = self.NB
        keys = [("x", self.xpar, b) for b in range(NB)]
        w = self.xkeys(row0) if dst is self.S["x"] else []
        self.store(dst.ap(row0 * D, [[D, 128], [128 * D, NB], [1, D]]), self.xt.ap(0, [[1024, NB], [1, 1024]]),
                   keys, w, key + str(self.xpar))

    def skeys(self, nm, tok):
        return [(nm, t) for t in range(tok // 512, (tok + self.NT) // 512)]

    def phaseA(self, tok0, L, NB):
        self.alloc_ffn(NB, "v")
        NT = self.NT
        nt_ = L // NT
        self.x_load(self.W["x_in"], tok0, 0)
        for t in range(nt_):
            tok = NT * t
            self.set_par(t % 2)
            if t + 1 < nt_:
                self.x_load(self.W["x_in"], tok0 + tok + NT, (t + 1) % 2)
            self.ffn(0, 0)
            self.norm_fm(1)
            for slab, nm in enumerate(["u", "q", "k"]):
                s = self.pw_load("WinE", slab)
                self.proj_fm(s, [0, 1, 2, 3], nm, 0, tok, self.skeys(nm + "scr", tok))
            s = self.pw_load("WinE", 3)
            self.proj_tm_v(s, 0, 8, self.vst, "vst", "v", 528, tok, self.skeys("vscr", tok))
            self.x_store(self.S["x"], tok, "xst")

    def attn_block(self, heads, np_, kfn, qfn, vfn, tabfn, bslot, sink):
        yb = self.yb[bslot]

        DEP = 3 if np_ > 4 else 4
        slot_w = 640 if np_ > 4 else 512

        def stage1(hi, h):
            ss = hi % DEP
            c0 = ss * slot_w
            banks = sorted(set([self.pk(c0 // 512), self.pk((c0 + np_ * 128 - 1) // 512)]))
            for i in range(np_):
                self.mm(self.psum.ap(c0 + i * 128, [[1, 128]]), kfn(h, i), qfn(h), True, True, [("kT", self.xpar), ("qT", self.xpar)], banks)
            self.act(self.E[ss].ap(0, [[1, np_ * 128]]), self.psum.ap(c0, [[1, np_ * 128]]), AF.Exp, banks, [("E", ss)], scale=0.125)
            self.tt("dve", self.Pm[ss].ap(0, [[1, np_ * 128]]), self.E[ss].ap(0, [[1, np_ * 128]]), tabfn(h), ALU.mult,
                    [("E", ss), "TAB"], [("Pm", ss)])

        def stage2(hi, h):
            ss = hi % DEP
            ob = 4 + hi // 4
            for i in range(np_):
                self.mm(self.ps(ob, (hi % 4) * 65, 65), self.Pm[ss].ap(i * 128, [[1, 128]]), vfn(h, i), i == 0, i == np_ - 1,
                        [("Pm", ss), ("vT", self.xpar)], [self.pk(ob)])

        nhd = len(heads)
        for hi, h in enumerate(heads):
            stage1(hi, h)
            if hi >= DEP - 1:
                stage2(hi - DEP + 1, heads[hi - DEP + 1])
        for hi in range(max(0, nhd - DEP + 1), nhd):
            stage2(hi, heads[hi])
        for hb in range((len(heads) + 3) // 4):
            ob = 4 + hb
            hs = heads[hb * 4:(hb + 1) * 4]
            nh = len(hs)
            den = self.psv(ob, 64, [[65, nh]])
            rc = self.rec.ap(hb * 4, [[1, nh]])
            if sink:
                dsa = self.dsum.ap(hb * 4, [[1, nh]])
                self.tt("dve", dsa, den, self.es.ap(hs[0], [[1, nh]]), ALU.add, [self.pk(ob), "es"], ["dsum"])
                self.P.add("dve", lambda e, rc=rc, dsa=dsa: e.reciprocal(out=rc, in_=dsa), r=["dsum"], w=["rec"])
            else:
                self.P.add("dve", lambda e, rc=rc, den=den: e.reciprocal(out=rc, in_=den), r=[self.pk(ob)], w=["rec"])
            self.tt("dve", yb.ap(hs[0] * 64, [[64, nh], [1, 64]]), self.psv(ob, 0, [[65, nh], [1, 64]]),
                    self.rec.ap(hb * 4, [[1, nh], [0, 64]]), ALU.mult, [self.pk(ob), "rec"], [("yb", bslot)])

    def out_proj(self, wname, lhs_fn, rkeys):
        for half in range(2):
            s = self.pw_load(wname, half)
            for blk in range(4):
                bank = 4 + blk
                for kc in range(8):
                    self.mm(self.ps(bank), lhs_fn(kc, blk), self.pw[s].ap(kc * 512, [[1, 512]]), kc == 0, kc == 7,
                            rkeys + [("pw", s)], [self.pk(bank)])
                xa = self.xt.ap(blk * 1024 + half * 512, [[1, 512]])
                self.stt(xa, self.ps(bank), 1.0, xa, ALU.mult, ALU.add, [self.pk(bank), ("x", self.xpar, blk)], [("x", self.xpar, blk)])

    def phaseC1(self, L):
        S = self.S
        LM = self.LMAX
        NB_ = L // 128
        self.alloc_attn("na")
        def loads(t, par):
            tok = 512 * t
            self.x_load(S["x"], tok, par)
            lo, hi = max(0, tok - 256), min(L, tok + 768)
            kw = hi - lo
            tl = list(range(lo // 512, (hi + 511) // 512))
            for e_ in range(2):
                self.load(self.qTs[par].ap(e_ * 512, [[1024, 4], [1, 512]], p0=64 * e_, pn=64),
                          S["q"].ap(64 * e_ * LM + tok, [[LM, 64], [128 * LM, 4], [1, 512]]), [("qscr", t)], [("qT", par)], f"qT{e_}{par}")
            self.load(self.kTs[par].ap(0, [[1024, 4], [1, kw]]), S["k"].ap(lo, [[LM, 128], [128 * LM, 4], [1, kw]]),
                      [("kscr", i) for i in tl], [("kT", par)], f"kT{par}")
            self.load(self.vTs[par].ap(0, [[528, kw // 128], [1, 528]]), S["v"].ap(lo * 528, [[528, 128], [128 * 528, kw // 128], [1, 528]]),
                      [("vscr", i) for i in tl], [("vT", par)], f"vT{par}")
            self.load(self.yaTs[par].ap(0, [[512, 4], [1, 512]]), S["ya"].ap(tok, [[LM, 128], [128 * LM, 4], [1, 512]]),
                      [("yascr", t)], [("yaT", par)], f"yaT{par}")

        nt_ = L // 512
        loads(0, 0)
        for t in range(nt_):
            tok = 512 * t
            self.set_par(t % 2)
            if t + 1 < nt_:
                loads(t + 1, (t + 1) % 2)
            lo = max(0, tok - 256)
            for b in range(4):
                n = 4 * t + b
                cls = "first" if n == 0 else "second" if n == 1 else "last" if n == NB_ - 1 else "slast" if n == NB_ - 2 else "interior"
                offs = NA_OFFS[cls]
                pb = NA_PBASE[cls]
                np_ = len(offs)
                kb = [((n + o) * 128 - lo) for o in offs]
                bslot = b % 2
                self.attn_block(
                    list(range(8)), np_,
                    lambda h, i: self.kT.ap((h // 2) * 1024 + kb[i], [[1, 128]]),
                    lambda h: self.qT.ap((h // 2) * 1024 + (h % 2) * 512 + b * 128, [[1, 128]]),
                    lambda h, i: self.vT.ap((kb[i] // 128) * 528 + h * 66, [[1, 65]]),
                    lambda h: self.TAB.ap((h * 21 + pb) * 128, [[1, np_ * 128]]),
                    bslot, False)
                for kc in range(4):
                    self.tr(self.psb(6, kc * 128, 128), self.yb[bslot].ap(kc * 128, [[1, 128]]), self.identb[:],
                            [("yb", bslot), "identb"], [self.pk(6)])
                self.copy(self.ew(), self.mix.ap(4 * 512 + b * 128, [[512, 4], [1, 128]]),
                          self.psb(6, 0, 512).rearrange("p (k t) -> p k t", k=4), [self.pk(6)], ["mix"])
            self.out_proj("WoutE", lambda kc, blk: (self.yaT.ap(kc * 512 + blk * 128, [[1, 128]]) if kc < 4
                                                      else self.mix.ap(kc * 512 + blk * 128, [[1, 128]])), [("yaT", self.xpar), "mix"])
            self.x_store(S["x"], tok, "xst")

    def phaseC2(self, L, NB):
        self.alloc_ffn(NB, "v2")
        NT = self.NT
        nt_ = L // NT
        self.x_load(self.S["x"], 0, 0)
        for t in range(nt_):
            tok = NT * t
            self.set_par(t % 2)
            if t + 1 < nt_:
                self.x_load(self.S["x"], tok + NT, (t + 1) % 2)
            self.ffn(1, 2)
            self.ffn(2, 3)
            self.norm_fm(4)
            for slab in range(2):
                s = self.pw_load("WinO", slab)
                self.proj_fm(s, [0, 1, 2, 3], "q2", slab * 4, tok, self.skeys("q2scr", tok))
            s = self.pw_load("WinO", 2)
            self.proj_fm(s, [0, 1], "k2", 0, tok, self.skeys("k2scr", tok))
            self.proj_tm_v(s, 256, 4, self.v2st, "v2st", "v2", 264, tok, self.skeys("v2scr", tok))
            self.x_store(self.S["x"], tok, "xst")

    def phaseD1(self, L):
        S = self.S
        LM = self.LMAX
        NB_ = L // 128
        hmap = {}
        for tq, (ha, hb_) in enumerate(GQA_PAIRS):
            hmap[ha] = (tq, 0)
            hmap[hb_] = (tq, 1)
        self.alloc_attn("gqa")
        def loads(t, par):
            tok = 512 * t
            self.x_load(S["x"], tok, par)
            lo, hi = max(0, tok - 128), min(L, tok + 640)
            kw = hi - lo
            tl = list(range(lo // 512, (hi + 511) // 512))
            self.load(self.qTs[par].ap(0, [[512, 8], [1, 512]]), S["q2"].ap(tok, [[LM, 128], [128 * LM, 8], [1, 512]]),
                      [("q2scr", t)], [("qT", par)], f"qT0{par}")
            for e_ in range(2):
                self.load(self.kTs[par].ap(e_ * 1024, [[2048, 2], [1, kw]], p0=64 * e_, pn=64),
                          S["k2"].ap(64 * e_ * LM + lo, [[LM, 64], [128 * LM, 2], [1, kw]]), [("k2scr", i) for i in tl], [("kT", par)], f"kT{e_}{par}")
            self.load(self.vTs[par].ap(0, [[528, kw // 128], [1, 264]]), S["v2"].ap(lo * 264, [[264, 128], [128 * 264, kw // 128], [1, 264]]),
                      [("v2scr", i) for i in tl], [("vT", par)], f"vT{par}")

        nt_ = L // 512
        loads(0, 0)
        for t in range(nt_):
            tok = 512 * t
            self.set_par(t % 2)
            if t + 1 < nt_:
                loads(t + 1, (t + 1) % 2)
            lo = max(0, tok - 128)
            for b in range(4):
                n = 4 * t + b
                offs = [o for o in (-1, 0, 1) if 0 <= n + o < NB_]
                np_ = len(offs)
                kb = [((n + o) * 128 - lo) for o in offs]
                o0 = offs[0] + 1
                bslot = b % 2
                for hh in range(2):
                    self.attn_block(
                        list(range(hh * 8, hh * 8 + 8)), np_,
                        lambda h, i: self.kT.ap(((h // 4) // 2) * 2048 + hmap[h][1] * 1024 + kb[i], [[1, 128]]),
                        lambda h: self.qT.ap(hmap[h][0] * 512 + b * 128, [[1, 128]]),
                        lambda h, i: self.vT.ap((kb[i] // 128) * 528 + (h // 4) * 66, [[1, 65]]),
                        lambda h: self.TAB.ap((h * 3 + o0) * 128, [[1, np_ * 128]]),
                        bslot, True)
                for kc in range(8):
                    self.tr(self.psb(6, kc * 128, 128), self.yb[bslot].ap(kc * 128, [[1, 128]]), self.identb[:],
                            [("yb", bslot), "identb"], [self.pk(6)])
                self.copy(self.ew(), self.mix.ap(b * 128, [[512, 8], [1, 128]]),
                          self.psb(6).rearrange("p (k t) -> p k t", k=8), [self.pk(6)], ["mix"])
            self.out_proj("WoutO", lambda kc, blk: self.mix.ap(kc * 512 + blk * 128, [[1, 128]]), ["mix"])
            self.x_store(S["x"], tok, "xst")

    def phaseD2(self, tok0, L, NB):
        self.alloc_ffn(NB)
        NT = self.NT
        nt_ = L // NT
        self.x_load(self.S["x"], 0, 0)
        for t in range(nt_):
            tok = NT * t
            self.set_par(t % 2)
            if t + 1 < nt_:
                self.x_load(self.S["x"], tok + NT, (t + 1) % 2)
            self.ffn(3, 5)
            x = self.xt
            for blk in range(self.NB):
                self.stats(blk)
                xa = x.ap(blk * 1024, [[1, 1024]])
                self.stt(xa, xa, self.st.ap(blk * 4 + 2, [[1, 1]]), self.Gfin[:], ALU.mult, ALU.mult,
                         [("x", self.xpar, blk), ("st", blk), "Gfin"], [("x", self.xpar, blk)])
            self.x_store(self.y_out, tok0 + tok, "yst")


    def setup_tables(self):
        sb = self.sb
        sb.ptr = self.arena0
        a = sb.alloc
        W = self.W
        P = self.P
        self.TNA = a("TNA", [128, 8, 21, 128], BF16)
        self.TGQ = a("TGQ", [128, 16, 3, 128], BF16)
        rpn = a("rpn", [120, 31], F32)
        self.load(rpn[:], W["na_rpb"].ap(0, [[31, 120], [1, 31]]), [], ["rpn"], "c2")
        self.mm(self.ps(0, 0, 120, pn=31), rpn[:], self.identf.ap(0, [[1, 120]], pn=120), True, True, ["rpn", "identf"], [self.pk(0)])
        erb = a("erb", [31, 120], BF16)
        self.act(erb[:], self.ps(0, 0, 120, pn=31), AF.Exp, [self.pk(0)], ["erb"])
        ohc = a("ohc", [31, 64 * 128], BF16)
        self.load(ohc[:], W["c_ohc"].ap(0, [[64 * 128, 31], [1, 64 * 128]]), [], ["ohc"], "c2")
        esub = a("esub", [128, 8, 15, 64], BF16)
        for qb in range(16):
            bank = 1 + qb % 3
            for q4 in range(4):
                qc = qb * 4 + q4
                self.mm(self.ps(bank, q4 * 120, 120), ohc.ap(qc * 128, [[1, 128]], pn=31), erb.ap(0, [[1, 120]], pn=31),
                        True, True, ["ohc", "erb"], [self.pk(bank)])
            self.copy(self.ew(), esub.ap(qb * 4, [[1, 4], [15 * 64, 8], [64, 15]]), self.psv(bank, 0, [[120, 4], [15, 8], [1, 15]]),
                      [self.pk(bank)], ["esub"])
            self.pump(1)
        k = 0
        for cls in NA_CLASSES:
            for i, blk in enumerate(na_pattern(cls)):
                pat = NA_PBASE[cls] + i
                for kr in range(2):
                    for qr in range(2):
                        dst = self.TNA.ap(pat * 128 + qr * 64, [[21 * 128, 8], [1, 64]], p0=64 * kr, pn=64)
                        rr = blk[kr][qr]
                        k += 1
                        if rr is None:
                            P.add("pool", lambda e, dst=dst: e.memset(dst, 0.0), w=["TAB"])
                        else:
                            self.copy(["dve", "pool", "act"][k % 3], dst,
                                      esub.ap(rr * 64, [[15 * 64, 8], [1, 64]], p0=64 * kr, pn=64), ["esub"], ["TAB"])
        t5n = a("t5n", [32, 16], F32)
        self.load(t5n[:], W["t5_table"].ap(0, [[16, 32], [1, 16]]), [], ["t5n"], "c2")
        etb = a("etb", [32, 16], BF16)
        self.act(etb[:], t5n[:], AF.Exp, ["t5n"], ["etb"])
        ohg = a("ohg", [32, 512], BF16)
        self.load(ohg[:], W["c_ohg"].ap(0, [[512, 32], [1, 512]]), [], ["ohg"], "c2")
        for oi in range(3):
            for qb in range(4):
                bank = 4 + (oi * 4 + qb) % 4
                for ql in range(32):
                    q = qb * 32 + ql
                    s = oi * 128 + 128 - q
                    self.mm(self.ps(bank, ql * 16, 16), ohg.ap(s, [[1, 128]], pn=32), etb.ap(0, [[1, 16]], pn=32), True, True,
                            ["ohg", "etb"], [self.pk(bank)])
                self.copy(self.ew(), self.TGQ.ap(oi * 128 + qb * 32, [[1, 32], [3 * 128, 16]]), self.psv(bank, 0, [[16, 32], [1, 16]]),
                          [self.pk(bank)], ["TAB"])
                self.pump(1)

        n1, n2 = 8 * 21 * 128, 16 * 3 * 128
        self.store(self.S["TNA"].ap(0, [[n1, 128], [1, n1]]), self.TNA.ap(0, [[1, n1]]), ["TAB"], ["TABscr"], "tabst")
        self.store(self.S["TGQ"].ap(0, [[n2, 128], [1, n2]]), self.TGQ.ap(0, [[1, n2]]), ["TAB"], ["TABscr"], "tabst")

    def setup_s5(self):
        sb = self.sb
        sb.ptr = self.arena0
        a = sb.alloc
        W, S, P = self.W, self.S, self.P
        V = "dve"
        nat = a("nat", [64, 128], F32)
        self.load(nat.ap(0, [[1, 64]], pn=64), W["s5_lam_re"].ap(0, [[64, 64], [1, 64]]), [], ["nat"], "c3")
        self.load(nat.ap(64, [[1, 64]], pn=64), W["s5_lam_im"].ap(0, [[64, 64], [1, 64]]), [], ["nat"], "c3")
        for i in range(2):
            self.mm(self.ps(0, i * 64, 64, pn=64), nat.ap(i * 64, [[1, 64]], pn=64), self.identf.ap(0, [[1, 64]], pn=64), True, True,
                    ["nat", "identf"], [self.pk(0)])
        f64 = lambda nm: a(nm, [64, 64], F32)
        lre, lim, ldt, zr, th, den, nr, cr, ci, t64 = [f64(n) for n in ["lre", "lim", "ldt", "zr", "th", "den", "nr", "cr", "ci", "t64"]]
        K = "s5s"
        self.copy(V, lre[:], self.ps(0, 0, 64, pn=64), [self.pk(0)], [K])
        self.copy(V, lim[:], self.ps(0, 64, 64, pn=64), [self.pk(0)], [K])
        self.load(ldt[:], W["s5_log_dt"].ap(0, [[0, 64], [1, 64]]), [], [K], "c3")
        self.act(ldt[:], ldt[:], AF.Exp, [K], [K])
        self.tt(V, zr[:], lre[:], ldt[:], ALU.mult, [K], [K])
        self.tt(V, th[:], lim[:], ldt[:], ALU.mult, [K], [K])
        if getattr(self, "s5_stop", 99) <= 1:
            return
        expo = a("expo", [64, 3, 2, 32], F32)
        self.load(expo[:], W["c_expo"].ap(0, [[192, 64], [1, 192]]), [], [K], "c3")
        big = lambda nm: a(nm, [64, 2, 32, 32], F32)
        PWre = [big(f"pwre{i}") for i in range(3)]
        PWim = [big(f"pwim{i}") for i in range(3)]
        m16 = lambda nm: a(nm, [64, 64, 16], F32)
        bre, bim, Bre, Bim, tb = m16("bre"), m16("bim"), m16("Bre"), m16("Bim"), m16("tb")
        cn = a("cn", [128, 8, 64], F32)
        ct_tiles = [a(f"ct{ci_}", [64, 1024], F32) for ci_ in range(2)]
        a1re, a1im = f64("a1re"), f64("a1im")
        drep = a("drep", [32, 128], F32)
        dcol = a("dcol", [128, 32], F32)
        msk = a("msk", [128, 2, 128], F32)
        mark_T = sb.ptr
        T1, T2, T3, T4 = big("T1"), big("T2"), big("T3"), big("T4")
        fl = [[1, 2048]]
        d3 = [[1024, 2], [32, 32], [1, 32]]
        TWO_PI = 2.0 * math.pi
        C1 = 6.28125
        C2 = float(np.float32(TWO_PI - C1))
        C3 = float(TWO_PI - C1 - C2)
        MAGIC = 12582912.0
        for tab in range(3):
            ex_b = expo.ap(tab * 64, [[32, 2], [0, 32], [1, 32]])
            self.tt(V, T1.ap(0, d3), th.ap(0, [[32, 2], [1, 32], [0, 32]]), ex_b, ALU.mult, [K], [K])
            self.tt(V, T2.ap(0, d3), zr.ap(0, [[32, 2], [1, 32], [0, 32]]), ex_b, ALU.mult, [K], [K])
            self.act(T2.ap(0, fl), T2.ap(0, fl), AF.Exp, [K], [K])
            self.ts(V, T3.ap(0, fl), T1.ap(0, fl), 1.0 / TWO_PI, None, ALU.mult, None, [K], [K])
            self.ts(V, T3.ap(0, fl), T3.ap(0, fl), MAGIC, None, ALU.add, None, [K], [K])
            self.ts(V, T3.ap(0, fl), T3.ap(0, fl), -MAGIC, None, ALU.add, None, [K], [K])
            self.stt(T4.ap(0, fl), T3.ap(0, fl), -C1, T1.ap(0, fl), ALU.mult, ALU.add, [K], [K])
            self.stt(T4.ap(0, fl), T3.ap(0, fl), -C2, T4.ap(0, fl), ALU.mult, ALU.add, [K], [K])
            self.stt(T4.ap(0, fl), T3.ap(0, fl), -C3, T4.ap(0, fl), ALU.mult, ALU.add, [K], [K])
            self.ts(V, T3.ap(0, fl), T4.ap(0, fl), -0.5, None, ALU.mult, None, [K], [K])
            self.stt(T3.ap(0, fl), T4.ap(0, fl), 0.5, T3.ap(0, fl), ALU.mult, ALU.max, [K], [K])
            self.act(T1.ap(0, fl), T3.ap(0, fl), AF.Sin, [K], [K], scale=-1.0, bias=self.halfpi.ap(0, [[1, 1]], pn=64))
            self.act(T3.ap(0, fl), T4.ap(0, fl), AF.Sin, [K], [K], scale=0.5)
            self.stt(PWim[tab].ap(0, fl), T3.ap(0, fl), 2.0, T1.ap(0, fl), ALU.mult, ALU.mult, [K], [K])
            self.tt(V, T4.ap(0, fl), T3.ap(0, fl), T3.ap(0, fl), ALU.mult, [K], [K])
            self.ts(V, PWre[tab].ap(0, fl), T4.ap(0, fl), -2.0, 1.0, ALU.mult, ALU.add, [K], [K])
            self.tt(V, PWre[tab].ap(0, fl), PWre[tab].ap(0, fl), T2.ap(0, fl), ALU.mult, [K], [K])
            self.tt(V, PWim[tab].ap(0, fl), PWim[tab].ap(0, fl), T2.ap(0, fl), ALU.mult, [K], [K])
            self.pump(3)
        if getattr(self, "s5_stop", 99) <= 2:
            return
        for ri, PWt in enumerate([PWre[2], PWim[2]]):
            self.copy(V, self.A32.ap(ri * 64, [[1, 32]], pn=64), PWt.ap(31, [[32, 32]]), [K], ["A32"])
            self.copy(V, self.A32.ap(ri * 64 + 32, [[1, 32]], pn=64), PWt.ap(1024, [[32, 32]]), [K], ["A32"])
            dst = a1re if ri == 0 else a1im
            self.copy(V, dst.ap(0, [[1, 32]]), PWt.ap(0, [[32, 32]]), [K], [K])
            self.copy(V, dst.ap(32, [[1, 32]]), PWt.ap(1024 + 31, [[32, 32]]), [K], [K])
        self.ts(V, self.Ac2.ap(0, [[1, 64]], pn=64), self.A32.ap(64, [[1, 64]], pn=64), -1.0, None, ALU.mult, None, ["A32"], ["A32"])
        self.copy(V, self.Ac2.ap(64, [[1, 64]], pn=64), self.A32.ap(64, [[1, 64]], pn=64), ["A32"], ["A32"])
        self.tt(V, den[:], lre[:], lre[:], ALU.mult, [K], [K])
        self.tt(V, t64[:], lim[:], lim[:], ALU.mult, [K], [K])
        self.tt(V, den[:], den[:], t64[:], ALU.add, [K], [K])
        P.add(V, lambda e: e.reciprocal(out=den[:], in_=den[:]), r=[K], w=[K])
        self.ts(V, nr[:], a1re[:], -1.0, None, ALU.add, None, [K], [K])
        self.tt(V, cr[:], nr[:], lre[:], ALU.mult, [K], [K])
        self.tt(V, t64[:], a1im[:], lim[:], ALU.mult, [K], [K])
        self.tt(V, cr[:], cr[:], t64[:], ALU.add, [K], [K])
        self.tt(V, cr[:], cr[:], den[:], ALU.mult, [K], [K])
        self.tt(V, ci[:], a1im[:], lre[:], ALU.mult, [K], [K])
        self.tt(V, t64[:], nr[:], lim[:], ALU.mult, [K], [K])
        self.tt(V, ci[:], ci[:], t64[:], ALU.subtract, [K], [K])
        self.tt(V, ci[:], ci[:], den[:], ALU.mult, [K], [K])
        if getattr(self, "s5_stop", 99) <= 3:
            return
        for nm, dst in (("s5_b_re", bre), ("s5_b_im", bim)):
            for q in range(4):
                self.load(dst.ap(q * 256, [[16, 16], [1, 16]], pn=64),
                          W[nm].ap(q * 16 * 1024, [[16, 64], [1024, 16], [1, 16]]), [], [K], "c3")
        f16 = [[16, 64], [1, 16]]
        crb, cib = cr.ap(0, [[1, 64], [0, 16]]), ci.ap(0, [[1, 64], [0, 16]])
        self.tt(V, Bre.ap(0, f16), bre.ap(0, f16), crb, ALU.mult, [K], [K])
        self.tt(V, tb.ap(0, f16), bim.ap(0, f16), cib, ALU.mult, [K], [K])
        self.tt(V, Bre.ap(0, f16), Bre.ap(0, f16), tb.ap(0, f16), ALU.subtract, [K], [K])
        self.tt(V, Bim.ap(0, f16), bim.ap(0, f16), crb, ALU.mult, [K], [K])
        self.tt(V, tb.ap(0, f16), bre.ap(0, f16), cib, ALU.mult, [K], [K])
        self.tt(V, Bim.ap(0, f16), Bim.ap(0, f16), tb.ap(0, f16), ALU.add, [K], [K])
        if getattr(self, "s5_stop", 99) <= 4:
            return
        CT = []
        for ci_, nm in enumerate(["s5_c_re", "s5_c_im"]):
            self.load(cn.ap(0, [[64, 8], [1, 64]]), W[nm].ap(0, [[64, 128], [128 * 64, 8], [1, 64]]), [], ["cn"], "c3")
            ct = ct_tiles[ci_]
            for t in range(8):
                bank = 1 + t // 4
                self.mm(self.ps(bank, (t % 4) * 128, 128, pn=64), cn.ap(t * 64, [[1, 64]]), self.identf[:], True, True,
                        ["cn", "identf"], [self.pk(bank)])
            for hb in range(2):
                self.copy(V, ct.ap(hb * 512, [[1, 512]]), self.ps(1 + hb, 0, 512, pn=64), [self.pk(1 + hb)], [K])
            CT.append(ct)
        self.load(drep.ap(0, [[16, 8], [1, 16]], pn=32), W["s5_d"].ap(0, [[16, 32], [0, 8], [1, 16]]), [], ["drep"], "c3")
        self.mm(self.ps(3, 0, 32), drep.ap(0, [[1, 128]], pn=32), self.identf.ap(0, [[1, 32]], pn=32), True, True,
                ["drep", "identf"], [self.pk(3)])
        self.copy(V, dcol[:], self.ps(3, 0, 32), [self.pk(3)], [K])
        self.load(msk.ap(0, [[1, 256]]), W["c_mask"].ap(0, [[256, 128], [1, 256]]), [], [K], "c3")
        if getattr(self, "s5_stop", 99) <= 5:
            return
        GS = 2
        sb.ptr = mark_T
        bshape = [64, 2, 2, GS, 512]
        WTb, WPb, QTb = a("WTb", bshape, BF16), a("WPb", bshape, BF16), a("QTb", bshape, BF16)
        t1 = a("bt1", [64, GS, 32, 16], F32)
        t2 = a("bt2", [64, GS, 32, 16], F32)
        Wst = a("Wst", [128, 2, 512], BF16)
        Mst = a("Mst", [128, 7, 128], BF16)
        mt1, mt2 = a("mt1", [128, 128], F32), a("mt2", [128, 128], F32)
        od = [[512, GS], [16, 32], [1, 16]]
        full = [[1, GS * 512]]
        for bi in range(32 // GS):
            g0 = bi * GS
            KB = "s5b"
            self.pump(3)
            for dr_ in range(2):
                bb_re = Bre.ap((dr_ * 32 + g0) * 16, [[16, GS], [0, 32], [1, 16]])
                bb_im = Bim.ap((dr_ * 32 + g0) * 16, [[16, GS], [0, 32], [1, 16]])
                c_re = CT[0].ap((dr_ * 32 + g0) * 16, [[16, GS], [0, 32], [1, 16]])
                c_im = CT[1].ap((dr_ * 32 + g0) * 16, [[16, GS], [0, 32], [1, 16]])
                for tab, outt in ((0, WTb), (1, WPb)):
                    pre = PWre[tab].ap(dr_ * 1024 + g0 * 32, [[32, GS], [1, 32], [0, 16]])
                    pim = PWim[tab].ap(dr_ * 1024 + g0 * 32, [[32, GS], [1, 32], [0, 16]])
                    o_re = outt.ap(((0 * 2 + dr_) * GS) * 512, od, pn=64)
                    o_im = outt.ap(((1 * 2 + dr_) * GS) * 512, od, pn=64)
                    self.tt(V, t1[:], pre, bb_re, ALU.mult, [K], [KB])
                    self.tt(V, t2[:], pim, bb_im, ALU.mult, [K], [KB + "p"])
                    self.tt(V, o_re, t1[:], t2[:], ALU.subtract, [KB, KB + "p"], [KB])
                    self.tt(V, t1[:], pre, bb_im, ALU.mult, [K, KB], [KB])
                    self.tt(V, t2[:], pim, bb_re, ALU.mult, [K, KB], [KB + "p"])
                    self.tt(V, o_im, t1[:], t2[:], ALU.add, [KB, KB + "p"], [KB])
                pre = PWre[2].ap(dr_ * 1024 + g0 * 32, [[32, GS], [1, 32], [0, 16]])
                pim = PWim[2].ap(dr_ * 1024 + g0 * 32, [[32, GS], [1, 32], [0, 16]])
                o_re = QTb.ap(((0 * 2 + dr_) * GS) * 512, od, pn=64)
                o_im = QTb.ap(((1 * 2 + dr_) * GS) * 512, od, pn=64)
                self.tt(V, t1[:], c_re, pre, ALU.mult, [K, KB], [KB])
                self.tt(V, t2[:], c_im, pim, ALU.mult, [K, KB], [KB + "p"])
                self.tt(V, o_re, t1[:], t2[:], ALU.subtract, [KB, KB + "p"], [KB])
                self.tt(V, t1[:], c_re, pim, ALU.mult, [K, KB], [KB])
                self.tt(V, t2[:], c_im, pre, ALU.mult, [K, KB], [KB + "p"])
                self.stt(o_im, t1[:], -1.0, t2[:], ALU.mult, ALU.subtract, [KB, KB + "p"], [KB])
                if getattr(self, "s5_stop", 99) <= 6:
                    continue
                for g in range(GS):
                    self.store(S["QT"].ap((dr_ * 32 + g0 + g) * 64 * 1024, [[1024, 64], [512, 2], [1, 512]]),
                               QTb.ap((dr_ * GS + g) * 512, [[2 * GS * 512, 2], [1, 512]], pn=64), [KB], ["QTscr"], f"s5q{g}")
                if getattr(self, "s5_stop", 99) <= 7:
                    continue
                for g in range(GS):
                    bank = 6 + g % 2
                    for Jp in range(4):
                        for slot in range(3):
                            ri = slot % 2
                            self.tr(self.psb(bank, Jp * 192 + slot * 64, 64),
                                    WTb.ap(((ri * 2 + dr_) * GS + g) * 512 + Jp * 128, [[1, 128]], pn=64),
                                    self.identb.ap(0, [[1, 64]], pn=64), [KB, "identb"], [self.pk(bank)])
                    self.copy(self.ew(), Wst.ap(0, [[1, 768]]), self.psb(bank, 0, 768), [self.pk(bank)], ["Wst"])
                    self.store(S["Ws5"].ap((dr_ * 32 + g0 + g) * 128 * 768, [[768, 128], [1, 768]]),
                               Wst.ap(0, [[1, 768]]), ["Wst"], ["Wscr"], "s5w")
            if getattr(self, "s5_stop", 99) <= 8:
                continue
            for g in range(GS):
                for dl in range(4):
                    for ri in range(2):
                        self.mm(self.ps(4, dl * 128, 128), WPb.ap(((ri * 2 + 0) * GS + g) * 512, [[1, 128]], pn=64),
                                QTb.ap(((ri * 2 + 0) * GS + g) * 512 + dl * 128, [[1, 128]], pn=64), ri == 0, ri == 1,
                                [KB], [self.pk(4)])
                    for ri in range(2):
                        self.mm(self.ps(5, dl * 128, 128), WPb.ap(((ri * 2 + 1) * GS + g) * 512 + dl * 128, [[1, 128]], pn=64),
                                QTb.ap(((ri * 2 + 1) * GS + g) * 512, [[1, 128]], pn=64), ri == 0, ri == 1,
                                [KB], [self.pk(5)])
                if getattr(self, "s5_stop", 99) <= 9:
                    continue
                self.copy("act", Mst.ap(4 * 128, [[1, 384]]), self.ps(4, 128, 384), [self.pk(4)], ["Mst"])
                for dl in range(1, 4):
                    self.copy("act", Mst.ap((3 - dl) * 128, [[1, 128]]), self.ps(5, dl * 128, 128), [self.pk(5)], ["Mst"])
                if getattr(self, "s5_stop", 99) <= 10:
                    continue
                import os
                self.copy("act", mt1[:], self.ps(4, 0, 128), [self.pk(4)], ["mta"])
                self.copy("act", mt2[:], self.ps(5, 0, 128), [self.pk(5)], ["mta"])
                if os.environ.get("S5SUB") == "0":
                    continue
                self.tt(V, mt1[:], mt1[:], msk.ap(0, [[1, 128]]), ALU.mult, ["mta", K], ["mt"])
                self.tt(V, mt2[:], mt2[:], msk.ap(128, [[1, 128]]), ALU.mult, ["mta", K], ["mt"])
                if os.environ.get("S5SUB") == "1":
                    continue
                self.tt(V, mt1[:], mt1[:], mt2[:], ALU.add, ["mt"], ["mt"])
                if os.environ.get("S5SUB") == "2":
                    continue
                self.stt(Mst.ap(3 * 128, [[1, 128]]), self.identf[:], dcol.ap(g0 + g, [[1, 1]]), mt1[:], ALU.mult, ALU.add,
                         ["mt", "identf", K], ["Mst"])
                if getattr(self, "s5_stop", 99) <= 11:
                    continue
                self.store(S["M"].ap((g0 + g) * 128 * 896, [[896, 128], [1, 896]]), Mst.ap(0, [[1, 896]]), ["Mst"], ["Mscr"], "s5m")


    def alloc_B(self, L):
        sb = self.sb
        sb.ptr = self.arena0
        a = sb.alloc
        Kc = L // 32
        self.gfm = a("gfm", [128, 4, L], BF16)
        self.ufm = a("ufm", [128, L], BF16)
        self.U32 = a("U32", [128, 8, 4, Kc], BF16)
        self.Ssts = [a(f"Sst{i}", [64, 2, 2, 8, Kc], F32) for i in range(4)]
        self.Hbf = a("Hbf", [128, 2, 2, 8, Kc], BF16)
        self.G32 = a("G32", [128, 8, 4, Kc], BF16)
        self.sel = a("sel", [128, 64, 128], BF16)
        self.selT = a("selT", [128, 64, 128], BF16)
        self.wglu = a("wglu", [128, 4, 512], BF16)
        self.wr = [a(f"wr{i}", [128, 4, 192], BF16) for i in range(2)]
        self.qr = [a(f"qr{i}", [128, 2, 512], BF16) for i in range(4)]
        self.mr = [a(f"mr{i}", [128, 7, 128], BF16) for i in range(2)]
        self.gt1 = [a(f"gt1{i}", [128, 512], F32) for i in range(2)]
        self.gt2 = [a(f"gt2{i}", [128, 512], F32) for i in range(2)]
        self.yst = [a(f"yst{i}", [128, 512], BF16) for i in range(2)]
        self.Tt = [a(f"Tt{i}", [64, 2, 2, 8], F32) for i in range(8)]

    def phaseB(self, L):
        S, W, P = self.S, self.W, self.P
        LM = self.LMAX
        Kc = L // 32
        self.alloc_B(L)
        V = "dve"
        nt = L // 512
        self.load(self.sel.ap(0, [[1, 8192]]), W["c_sel"].ap(0, [[8192, 128], [1, 8192]]), [], ["sel"], "bsel")
        self.load(self.selT.ap(0, [[1, 8192]]), W["c_selT"].ap(0, [[8192, 128], [1, 8192]]), [], ["selT"], "bsel")
        self.load(self.wglu.ap(0, [[1, 2048]]), S["Wglu"].ap(0, [[2048, 128], [1, 2048]]), ["Wglu"], ["wglu"], "bsel")
        cw = cq = cm = 0
        hz = self.Hbf.ap(0, [[1, 32 * Kc]], p0=64, pn=64)
        P.add("pool", lambda e, hz=hz: e.memset(hz, 0.0), w=["Hbf"])
        for i_ in range(4):
            qz = self.qr[i_].ap(0, [[1, 1024]], p0=64, pn=64)
            P.add("pool", lambda e, qz=qz: e.memset(qz, 0.0), w=[("qr", i_)])
        ristr, dstr, gstr = 2 * 8 * Kc, 8 * Kc, Kc
        def shuffle(i):
            self.load(self.ufm.ap(0, [[1, L]]), S["u"].ap(i * 128 * LM, [[LM, 128], [1, L]]),
                      [("uscr", t) for t in range(nt)], ["ufm"], "bu")
            for g8 in range(8):
                bank = g8 % 4
                for Jp in range(4):
                    for jl in range(8):
                        self.mm(self.ps(bank, Jp * Kc, Kc), self.sel.ap((g8 * 8 + jl) * 128, [[1, 128]]),
                                self.ufm.ap(8 * Jp + jl, [[32, Kc]]), jl == 0, jl == 7, ["sel", "ufm"], [self.pk(bank)])
                self.copy(self.ew(), self.U32.ap(g8 * 4 * Kc, [[1, 4 * Kc]]), self.ps(bank, 0, 4 * Kc), [self.pk(bank)], ["U32"])

        for i in range(4):
            shuffle(i)
            Sst = self.Ssts[i]
            for g8 in range(8):
                g = 8 * i + g8
                for dr_ in range(2):
                    s = cw % 2
                    cw += 1
                    self.load(self.wr[s].ap(0, [[1, 768]]), S["Ws5"].ap((dr_ * 32 + g) * 128 * 768, [[768, 128], [1, 768]]),
                              ["Wscr"], [("wr", s)], f"wr{s}")
                    bank = 4 + (g8 * 2 + dr_) % 2
                    for ri in range(2):
                        for Jp in range(4):
                            self.mm(self.ps(bank, ri * Kc, Kc), self.wr[s].ap(Jp * 192 + ri * 64, [[1, 128]]),
                                    self.U32.ap((g8 * 4 + Jp) * Kc, [[1, Kc]]), Jp == 0, Jp == 3, [("wr", s), "U32"], [self.pk(bank)])
                    self.copy(self.ew(), Sst.ap(dr_ * dstr + g8 * gstr, [[ristr, 2], [1, Kc]]),
                              self.psv(bank, 0, [[Kc, 2], [1, Kc]], pn=64), [self.pk(bank)], [("Sst", i)])
        for k in range(1, Kc):
            dcur = dstr + (Kc - 1 - 2 * k)
            dprv = dstr + (Kc + 1 - 2 * k)
            for i in range(4):
                Sst = self.Ssts[i]
                c1 = self.A32.ap(8 * i, [[0, 2], [32, 2], [1, 8]])
                c2 = self.Ac2.ap(8 * i, [[64, 2], [32, 2], [1, 8]])
                T0, T1 = self.Tt[2 * i], self.Tt[2 * i + 1]
                cur2 = Sst.ap(k, [[ristr, 2], [dcur, 2], [gstr, 8]])
                prv2 = Sst.ap(k - 1, [[ristr, 2], [dprv, 2], [gstr, 8]])
                prvs = Sst.ap(ristr + k - 1, [[-ristr, 2], [dprv, 2], [gstr, 8]])
                self.tt(V, T0[:], prv2, c1, ALU.mult, [("Sst", i), "A32"], [("Tt0", i)])
                self.tt(V, T1[:], prvs, c2, ALU.mult, [("Sst", i), "A32"], [("Tt1", i)])
            for i in range(4):
                Sst = self.Ssts[i]
                T0, T1 = self.Tt[2 * i], self.Tt[2 * i + 1]
                cur2 = Sst.ap(k, [[ristr, 2], [dcur, 2], [gstr, 8]])
                self.tt(V, cur2, cur2, T0[:], ALU.add, [("Tt0", i), ("Sst", i)], [("Sst", i)])
            for i in range(4):
                Sst = self.Ssts[i]
                T0, T1 = self.Tt[2 * i], self.Tt[2 * i + 1]
                cur2 = Sst.ap(k, [[ristr, 2], [dcur, 2], [gstr, 8]])
                self.tt(V, cur2, cur2, T1[:], ALU.add, [("Tt1", i), ("Sst", i)], [("Sst", i)])

        for i in range(4):
            shuffle(i)
            Sst = self.Ssts[i]
            for dr_ in range(2):
                zcol = 0 if dr_ == 0 else Kc - 1
                hz2 = self.Hbf.ap(dr_ * dstr + zcol, [[ristr, 2], [gstr, 8], [1, 1]], pn=64)
                P.add("pool", lambda e, hz2=hz2: e.memset(hz2, 0.0), w=["Hbf"])
                so, do = (0, 1) if dr_ == 0 else (1, 0)
                self.copy("act" if dr_ else "dve", self.Hbf.ap(dr_ * dstr + do, [[ristr, 2], [gstr, 8], [1, Kc - 1]], pn=64),
                          Sst.ap(dr_ * dstr + so, [[ristr, 2], [gstr, 8], [1, Kc - 1]]), [("Sst", i)], ["Hbf"])
            for g8 in range(8):
                g = 8 * i + g8
                sm = cm % 2
                cm += 1
                self.load(self.mr[sm].ap(0, [[1, 896]]), S["M"].ap(g * 128 * 896, [[896, 128], [1, 896]]), ["Mscr"], [("mr", sm)], f"mr{sm}")
                sq = []
                for dr_ in range(2):
                    s = cq % 4
                    cq += 1
                    self.load(self.qr[s].ap(0, [[1, 1024]], pn=64), S["QT"].ap((dr_ * 32 + g) * 64 * 1024, [[1024, 64], [1, 1024]]),
                              ["QTscr"], [("qr", s)], f"qr{s}")
                    sq.append(s)
                bank = g8 % 4
                for J in range(4):
                    first = True
                    for Jp in range(4):
                        self.mm(self.ps(bank, J * Kc, Kc), self.mr[sm].ap((J - Jp + 3) * 128, [[1, 128]]),
                                self.U32.ap((g8 * 4 + Jp) * Kc, [[1, Kc]]), first, False, [("mr", sm), "U32"], [self.pk(bank)])
                        first = False
                    for dr_ in range(2):
                        for ri in range(2):
                            last = (dr_ == 1 and ri == 1)
                            self.mm(self.ps(bank, J * Kc, Kc), self.qr[sq[dr_]].ap(ri * 512 + J * 128, [[1, 128]]),
                                    self.Hbf.ap(ri * ristr + dr_ * dstr + g8 * gstr, [[1, Kc]]), False, last,
                                    [("qr", sq[dr_]), "Hbf"], [self.pk(bank)])
                n = 4 * Kc
                y = self.ps(bank, 0, n)
                gs_ = g8 % 2
                ta, tb = self.gt1[gs_].ap(0, [[1, n]]), self.gt2[gs_].ap(0, [[1, n]])
                self.act(ta, y, AF.Square, [self.pk(bank)], [("gt1", gs_)])
                self.ts(V, ta, ta, 0.044715, 1.0, ALU.mult, ALU.add, [("gt1", gs_)], [("gt1", gs_)])
                self.tt(V, tb, ta, y, ALU.mult, [("gt1", gs_), self.pk(bank)], [("gt2", gs_)])
                self.act(tb, tb, AF.Sigmoid, [("gt2", gs_)], [("gt2", gs_)], scale=1.5957691216057308)
                self.tt(V, self.G32.ap(g8 * 4 * Kc, [[1, n]]), tb, y, ALU.mult, [("gt2", gs_), self.pk(bank)], ["G32"])
            for J in range(4):
                for jh in range(2):
                    bank = 4 + (J * 2 + jh) % 4
                    for j4 in range(4):
                        jl = jh * 4 + j4
                        for g8 in range(8):
                            self.mm(self.ps(bank, j4 * Kc, Kc), self.selT.ap((g8 * 8 + jl) * 128, [[1, 128]]),
                                    self.G32.ap((g8 * 4 + J) * Kc, [[1, Kc]]), g8 == 0, g8 == 7, ["selT", "G32"], [self.pk(bank)])
                    self.copy(self.ew(), self.gfm.ap(i * L + 8 * J + jh * 4, [[1, 4], [32, Kc]]),
                              self.psv(bank, 0, [[Kc, 4], [1, Kc]]), [self.pk(bank)], ["gfm"])
        if self.dbg:
            for i in range(4):
                self.store(S["g"].ap(i * 128 * LM, [[LM, 128], [1, L]]), self.gfm.ap(i * L, [[1, L]]), ["gfm"], ["gscr"], "gdbg")
        c = 0
        for t in range(nt):
            for co in range(4):
                bank = c % 4
                ys = c % 2
                c += 1
                for kc in range(4):
                    self.mm(self.ps(bank), self.wglu.ap(kc * 512 + co * 128, [[1, 128]]), self.gfm.ap(kc * L + t * 512, [[1, 512]]),
                            kc == 0, kc == 3, ["wglu", "gfm"], [self.pk(bank)])
                sg = self.gt1[ys].ap(0, [[1, 512]])
                self.act(sg, self.ps(bank), AF.Sigmoid, [self.pk(bank), "bglu"], [("gt1", ys)], bias=self.bglu.ap(co, [[1, 1]]))
                self.tt(V, self.yst[ys][:], sg, self.gfm.ap(co * L + t * 512, [[1, 512]]), ALU.mult, [("gt1", ys), "gfm"], [("yst", ys)])
                self.store(S["ya"].ap(co * 128 * LM + t * 512, [[LM, 128], [1, 512]]), self.yst[ys][:], [("yst", ys)],
                           [("yascr", t)], f"yst{ys}")

    def build(self):
        P = self.P
        stages = getattr(self, "stages", "ctsv")
        self.setup_consts()
        P.add("pool", lambda e: e.memset(self.halfpi[:], math.pi / 2), w=["halfpi"])
        self.conv_q = []
        if "v" in stages:
            self.conv_weights()
        if "t" in stages:
            self.setup_tables()
            P.barrier()
        if "s" in stages:
            self.setup_s5()
        self.pump(len(self.conv_q))
        P.barrier()
        tok0 = 0
        for L in self.seq_lens:
            NB = 8 if L % 1024 == 0 else 4
            if "A" in self.phases:
                self.phaseA(tok0, L, NB)
                P.barrier()
            if "B" in self.phases:
                self.phaseB(L)
                P.barrier()
            if "C" in self.phases:
                self.phaseC1(L)
                P.barrier()
                self.phaseC2(L, NB)
                P.barrier()
            if "D" in self.phases:
                self.phaseD1(L)
                P.barrier()
                self.phaseD2(tok0, L, NB)
                P.barrier()
            tok0 += L
        P.emit(final_wait_keys=["yst", "xst", "stg0", "stg1", "vst", "yst0", "yst1", "s5m", "s5w", "s5q0", "s5q1", "gdbg", "tabst"])
        return self.nc


SEQ_LENS = [4096, 2048, 2048, 2048, 2048]
_CACHE = {}


def kernel(**inputs):
    xp = np.asarray(inputs["x_prompt"], dtype=np.float32)
    xs = np.asarray(inputs["x_sample"], dtype=np.float32)
    consts = host_constants()
    shared = {nm: np.ascontiguousarray(np.asarray(inputs[nm], dtype=np.float32)) for nm, _ in WEIGHT_SPECS}
    shared.update(consts)
    if "nc" not in _CACHE:
        _CACHE["nc"] = MK(SEQ_LENS).build()
    nc = _CACHE["nc"]
    in_maps = []
    for c in range(8):
        xin = np.concatenate([xp[c].reshape(4096, D), xs[4 * c:4 * c + 4].reshape(4 * 2048, D)], axis=0)
        m = dict(shared)
        m["x_in"] = np.ascontiguousarray(xin)
        in_maps.append(m)
    res = run_bass_kernel_spmd(nc, in_maps, core_ids=list(range(8)))
    yp = np.zeros((8, 4096, D), np.float32)
    ys = np.zeros((32, 2048, D), np.float32)
    for c in range(8):
        y = np.asarray(res.results[c]["y_out"]).reshape(-1, D)
        yp[c] = y[:4096]
        ys[4 * c:4 * c + 4] = y[4096:].reshape(4, 2048, D)
    return (yp, ys)
```

```python
import math
from contextlib import ExitStack
import numpy as np
import ml_dtypes
import concourse.bass as bass
import concourse.mybir as mybir
from concourse.bass_utils import run_bass_kernel_spmd

F32 = mybir.dt.float32
BF16 = mybir.dt.bfloat16
AF = mybir.ActivationFunctionType
ALU = mybir.AluOpType
NPBF = ml_dtypes.bfloat16

D = 1024
DFF = 2816
NFC = 22
EPS = 1e-6
EPOCH_MAX = 12000


def AP(t, off, dims):
    return bass.AP(t, off, [list(d) for d in dims])


class Prog:
    def __init__(self, nc):
        self.nc = nc
        self.ops = []
        self.barriers = []

    def add(self, eng, fn, r=(), w=(), dma=None):
        self.ops.append((eng, fn, tuple(r), tuple(w), dma))

    def barrier(self):
        self.barriers.append(len(self.ops))

    def emit(self, final_wait_keys=()):
        nc = self.nc
        ops = self.ops
        n = len(ops)
        engs = ("pe", "act", "dve", "pool", "sp")
        lastw, lastr = {}, {}
        deps = [None] * n
        bar_set = set(self.barriers)
        last_on_eng = {}
        last_dma = {}
        pending_bar = {}
        prev_same = {}
        for i, (eng, fn, r, w, dma) in enumerate(ops):
            if i in bar_set:
                allprev = list(last_on_eng.values()) + list(last_dma.values())
                for e in engs:
                    pending_bar[e] = list(allprev)
            me = ("dma", dma) if dma else eng
            d = set()
            for k in r:
                d.update(lastw.get(k, {}).values())
            for k in w:
                d.update(lastw.get(k, {}).values())
                d.update(lastr.get(k, {}).values())
            if eng in pending_bar:
                d.update(pending_bar.pop(eng))
            if dma and dma in last_dma:
                d.add(last_dma[dma])
            dd = []
            for j in d:
                ej, _, _, _, dj = ops[j]
                if dj is None and dma is None and ej == eng and (eng == "pe" or j == prev_same.get(eng, -1) and False):
                    continue
                dd.append(j)
            deps[i] = dd
            for k in r:
                lastr.setdefault(k, {})[me] = i
            for k in w:
                lastw.setdefault(k, {})[me] = i
            if dma:
                last_dma[dma] = i
            else:
                last_on_eng[eng] = i
        signalled = [False] * n
        for i in range(n):
            for j in deps[i]:
                signalled[j] = True
        cnt = {e: 0 for e in engs}
        epoch = {e: 0 for e in engs}
        dcnt = {}
        semof = [None] * n
        valof = [0] * n
        for i, (eng, fn, r, w, dma) in enumerate(ops):
            if dma:
                dcnt[dma] = dcnt.get(dma, 0) + 16
                semof[i] = ("dma", dma)
                valof[i] = dcnt[dma]
            elif signalled[i]:
                if cnt[eng] >= EPOCH_MAX:
                    epoch[eng] += 1
                    cnt[eng] = 0
                cnt[eng] += 1
                semof[i] = (eng, epoch[eng])
                valof[i] = cnt[eng]
        semnames = sorted(set(s for s in semof if s is not None), key=str)
        self.n_sems = len(semnames)
        with ExitStack() as es:
            sems = {}
            for sname in semnames:
                nm = "s_" + "_".join(str(x) for x in sname)
                sems[sname] = es.enter_context(nc.semaphore(nm))
            block = es.enter_context(nc.Block())
            per_eng = {e: [] for e in engs}
            for i, op in enumerate(ops):
                per_eng[op[0]].append(i)

            def run_engine(e_name, eobj):
                waited = {}
                for i in per_eng[e_name]:
                    eng, fn, r, w, dma = ops[i]
                    need = {}
                    for j in deps[i]:
                        sj = semof[j]
                        if valof[j] > need.get(sj, 0):
                            need[sj] = valof[j]
                    for sj, v in need.items():
                        if waited.get(sj, 0) >= v:
                            continue
                        waited[sj] = v
                        eobj.wait_ge(sems[sj], v)
                    ins = fn(eobj)
                    if dma:
                        ins.then_inc(sems[semof[i]], 16)
                    elif signalled[i]:
                        ins.then_inc(sems[semof[i]], 1)
                if e_name == "sp":
                    for k in final_wait_keys:
                        if ("dma", k) in sems and dcnt.get(k, 0) > 0:
                            eobj.wait_ge(sems[("dma", k)], dcnt[k])

            @block.tensor
            def _(e):
                run_engine("pe", e)

            @block.scalar
            def _(e):
                run_engine("act", e)

            @block.vector
            def _(e):
                run_engine("dve", e)

            @block.gpsimd
            def _(e):
                run_engine("pool", e)

            @block.sync
            def _(e):
                run_engine("sp", e)


GQA_PAIRS = [(0, 4), (1, 5), (2, 6), (3, 7), (8, 12), (9, 13), (10, 14), (11, 15)]
NA_CLASSES = ["first", "second", "interior", "slast", "last"]
NA_OFFS = {"first": [0, 1, 2, 3], "second": [-1, 0, 1, 2], "interior": [-2, -1, 0, 1, 2],
           "slast": [-2, -1, 0, 1], "last": [-3, -2, -1, 0]}
NA_PBASE = {"first": 0, "second": 4, "interior": 8, "slast": 13, "last": 17}


def _t5_bucket_np(rel):
    half, max_exact = 16, 8
    ret = np.where(rel > 0, half, 0)
    n = np.abs(rel)
    nf = np.maximum(n, 1).astype(np.float32)
    large = max_exact + (np.log(nf / np.float32(max_exact)) / np.float32(math.log(128 / max_exact))
                         * np.float32(half - max_exact)).astype(np.int32)
    large = np.minimum(large, half - 1)
    return ret + np.where(n < max_exact, n, large)


def na_pattern(cls):
    R = 64
    n = {"first": 0, "second": 1, "interior": 10, "slast": 30, "last": 31}[cls]
    out = []
    for off in NA_OFFS[cls]:
        m = n + off
        blk = [[None, None], [None, None]]
        for kr in range(2):
            for qr in range(2):
                r = 2 * n + qr
                krow = 2 * m + kr
                rs = min(max(r - 4, 0), R - 8)
                if rs <= krow < rs + 8:
                    blk[kr][qr] = krow - r + 7
        out.append(blk)
    return out


def host_constants():
    c = {}
    c["c_ident_bf"] = np.eye(128, dtype=np.float32).astype(NPBF)
    c["c_ident_f"] = np.eye(128, dtype=np.float32)
    sel = np.zeros((128, 64, 128), np.float32)
    selT = np.zeros((128, 64, 128), np.float32)
    for g8 in range(8):
        for jl in range(8):
            for cc in range(16):
                sel[16 * g8 + cc, g8 * 8 + jl, 16 * jl + cc] = 1.0
                selT[16 * jl + cc, g8 * 8 + jl, 16 * g8 + cc] = 1.0
    c["c_sel"] = sel.astype(NPBF)
    c["c_selT"] = selT.astype(NPBF)
    jl = np.arange(128) // 16
    msk = np.zeros((128, 2, 128), np.float32)
    msk[:, 0, :] = (jl[:, None] <= jl[None, :])
    msk[:, 1, :] = (jl[:, None] >= jl[None, :])
    c["c_mask"] = msk
    j = np.arange(32, dtype=np.float32)
    ex = np.zeros((64, 3, 2, 32), np.float32)
    ex[:, 0, 0, :] = 31 - j
    ex[:, 0, 1, :] = j
    ex[:, 1, 0, :] = -1 - j
    ex[:, 1, 1, :] = j - 32
    ex[:, 2, 0, :] = j + 1
    ex[:, 2, 1, :] = 32 - j
    c["c_expo"] = ex
    oh = np.zeros((31, 64, 128), np.float32)
    for qc in range(64):
        cs = min(max(qc - 8, 0), 48)
        for kc in range(cs, cs + 16):
            rc = kc - qc + 15
            oh[rc, qc, kc] = 1.0
            oh[rc, qc, 64 + kc] = 1.0
    c["c_ohc"] = oh.astype(NPBF)
    rp = np.arange(512) - 256
    bk = _t5_bucket_np(rp)
    og = np.zeros((32, 512), np.float32)
    for i in range(512):
        if abs(rp[i]) <= 128:
            og[bk[i], i] = 1.0
    c["c_ohg"] = og.astype(NPBF)
    return c


WEIGHT_SPECS = [
    ("norm_ffn", [2, 2, 1024]), ("w_ffn_gate", [2, 2, 1024, 2816]), ("w_ffn_up", [2, 2, 1024, 2816]),
    ("w_ffn_down", [2, 2, 2816, 1024]), ("norm_mix", [2, 1024]), ("w_in_even", [1, 1024, 2048]),
    ("s5_lam_re", [1, 2, 32, 64]), ("s5_lam_im", [1, 2, 32, 64]), ("s5_log_dt", [1, 2, 32]),
    ("s5_b_re", [1, 2, 32, 64, 16]), ("s5_b_im", [1, 2, 32, 64, 16]), ("s5_c_re", [1, 2, 32, 16, 64]),
    ("s5_c_im", [1, 2, 32, 16, 64]), ("s5_d", [1, 512]), ("s5_w_glu", [1, 512, 512]), ("s5_b_glu", [1, 512]),
    ("na_rpb", [1, 8, 15, 31]), ("w_out_even", [1, 1024, 1024]), ("w_in_odd", [1, 1024, 1536]),
    ("gqa_sink", [1, 16]), ("w_out_odd", [1, 1024, 1024]), ("t5_table", [32, 16]), ("norm_final", [1024]),
]
CONST_SPECS = [("c_ident_bf", [128, 128], BF16), ("c_ident_f", [128, 128], F32), ("c_sel", [128, 64, 128], BF16),
               ("c_selT", [128, 64, 128], BF16), ("c_mask", [128, 2, 128], F32), ("c_expo", [64, 3, 2, 32], F32),
               ("c_ohc", [31, 64, 128], BF16), ("c_ohg", [32, 512], BF16)]

DTSIZE = {F32: 4, BF16: 2}


class TT:
    def __init__(self, h, shape, dtype):
        self.h = h
        self.shape = list(shape)
        self.dtype = dtype
        self.row = int(np.prod(shape[1:]))
        st = []
        s = 1
        for d in reversed(shape[1:]):
            st.append(s)
            s *= d
        self.strides = list(reversed(st))

    def ap(self, off, dims, p0=0, pn=None):
        if pn is None:
            pn = self.shape[0] - p0
        return AP(self.h, p0 * self.row + off, [[self.row, pn]] + [list(d) for d in dims])

    def __getitem__(self, idx):
        return self.h[idx]


class DT:
    def __init__(self, h, shape):
        self.h = h
        self.shape = list(shape)

    def ap(self, off, dims):
        return AP(self.h, off, [list(d) for d in dims])


class SBAlloc:
    def __init__(self, nc, base=16512, limit=229376 - 64):
        self.nc = nc
        self.ptr = base
        self.limit = limit
        self.n = 0
        self.hi = base

    def alloc(self, name, shape, dtype):
        size = int(np.prod(shape[1:])) * DTSIZE[dtype]
        size = (size + 63) // 64 * 64
        assert self.ptr + size <= self.limit, f"SBUF overflow at {name}: {self.ptr}+{size}"
        self.n += 1
        h = self.nc.alloc_sbuf_tensor_at(f"{name}_{self.n}", list(shape), dtype, offset=self.ptr)
        self.ptr += size
        self.hi = max(self.hi, self.ptr)
        return TT(h, shape, dtype)


class MK:
    def __init__(self, seq_lens, dbg=False, phases="ABCD"):
        self.seq_lens = list(seq_lens)
        self.NTOK = sum(seq_lens)
        self.LMAX = max(seq_lens)
        self.dbg = dbg
        self.phases = phases
        self.nc = nc = bass.Bass("TRN2", target_bir_lowering=False)
        self.P = Prog(nc)
        self.rr = 0
        self.bankc = 0
        self.out_keys = []
        self.store_eng = "sp"
        self.io()
        self.alloc()

    def dram(self, name, shape, dtype, kind="Internal"):
        h = self.nc.dram_tensor(name, list(shape), dtype, kind=kind)
        return DT(h, shape)

    def io(self):
        W = {}
        W["x_in"] = self.dram("x_in", [self.NTOK, D], F32, "ExternalInput")
        self.y_out = self.dram("y_out", [self.NTOK, D], F32, "ExternalOutput")
        for nm, shp in WEIGHT_SPECS:
            W[nm] = self.dram(nm, shp, F32, "ExternalInput")
        for nm, shp, dt_ in CONST_SPECS:
            W[nm] = self.dram(nm, shp, dt_, "ExternalInput")
        self.W = W
        L = self.LMAX
        kd = "ExternalOutput" if self.dbg else "Internal"
        S = {}
        S["Wgu"] = self.dram("s_Wgu", [4, 22, 128, 2048], BF16)
        S["Wd"] = self.dram("s_Wd", [4, 2, 22, 128, 512], BF16)
        S["WinE"] = self.dram("s_WinE", [4, 128, 8, 512], BF16)
        S["WoutE"] = self.dram("s_WoutE", [2, 128, 8, 512], BF16)
        S["WinO"] = self.dram("s_WinO", [3, 128, 8, 512], BF16)
        S["WoutO"] = self.dram("s_WoutO", [2, 128, 8, 512], BF16)
        S["Wglu"] = self.dram("s_Wglu", [128, 4, 512], BF16)
        S["Ws5"] = self.dram("s_Ws5", [2, 32, 128, 4, 192], BF16, kd)
        S["QT"] = self.dram("s_QT", [2, 32, 64, 2, 512], BF16, kd)
        S["M"] = self.dram("s_M", [32, 128, 7, 128], BF16, kd)
        S["TNA"] = self.dram("s_TNA", [128, 8 * 21 * 128], BF16)
        S["TGQ"] = self.dram("s_TGQ", [128, 16 * 3 * 128], BF16)
        S["x"] = self.dram("s_x", [L, D], F32, kd)
        S["u"] = self.dram("s_u", [4, 128, L], BF16, kd)
        S["q"] = self.dram("s_q", [4, 128, L], BF16, kd)
        S["k"] = self.dram("s_k", [4, 128, L], BF16, kd)
        S["v"] = self.dram("s_v", [L, 528], BF16, kd)
        S["ya"] = self.dram("s_ya", [4, 128, L], BF16, kd)
        S["g"] = self.dram("s_g", [4, 128, L], BF16, kd)
        S["q2"] = self.dram("s_q2", [8, 128, L], BF16, kd)
        S["k2"] = self.dram("s_k2", [2, 128, L], BF16, kd)
        S["v2"] = self.dram("s_v2", [L, 264], BF16, kd)
        self.S = S

    def alloc(self):
        nc = self.nc
        sb = self.sb = SBAlloc(nc)
        a = sb.alloc
        self.identb = a("identb", [128, 128], BF16)
        self.identf = a("identf", [128, 128], F32)
        self.G = a("G", [128, 6, 8], F32)
        self.Gfin = a("Gfin", [128, 1024], F32)
        self.es = a("es", [128, 16], F32)
        self.cm05 = a("cm05", [128, 2], F32)
        self.st = a("st", [128, 8, 4], F32)
        self.A32 = a("A32", [64, 2, 2, 32], F32)
        self.Ac2 = a("Ac2", [64, 2, 2, 32], F32)
        self.bglu = a("bglu", [128, 4], F32)
        self.rec = a("rec", [128, 16], F32)
        self.halfpi = a("halfpi", [128, 2], F32)
        self.psum = TT(nc.alloc_psum_tensor("psum", [128, 4096], F32), [128, 4096], F32)
        self.arena0 = sb.ptr

    def ps(self, bank, off=0, n=512, p0=0, pn=128):
        return self.psum.ap(bank * 512 + off, [[1, n]], p0=p0, pn=pn)

    def psv(self, bank, off, dims, p0=0, pn=128):
        return self.psum.ap(bank * 512 + off, dims, p0=p0, pn=pn)

    def psb(self, bank, off=0, n=1024, p0=0, pn=128):
        full = self.psum.ap(bank * 512, [[1, 512]], p0=p0, pn=pn).bitcast(BF16)
        return full[:, off:off + n]

    def pk(self, bank):
        return ("ps", bank)

    def nbank(self):
        b = self.bankc % 8
        self.bankc += 1
        return b

    def ew(self):
        self.rr += 1
        return "act" if self.rr % 2 else "dve"

    def copy(self, eng, out, in_, r, w):
        if eng == "act":
            self.P.add("act", lambda e: e.copy(out=out, in_=in_), r=r, w=w)
        elif eng == "dve":
            self.P.add("dve", lambda e: e.tensor_copy(out=out, in_=in_), r=r, w=w)
        else:
            self.P.add("pool", lambda e: e.tensor_copy(out=out, in_=in_), r=r, w=w)

    def load(self, out, in_, r, w, key):
        self.P.add("sp", lambda e: e.dma_start(out=out, in_=in_), r=r, w=w, dma=key)

    def store(self, out, in_, r, w, key):
        self.P.add(self.store_eng, lambda e: e.dma_start(out=out, in_=in_), r=r, w=w, dma=key)

    def mm(self, out, lhsT, rhs, start, stop, r, w):
        self.P.add("pe", lambda e: e.matmul(out, lhsT=lhsT, rhs=rhs, start=start, stop=stop), r=r, w=w)

    def tr(self, out, in_, ident, r, w):
        self.P.add("pe", lambda e: e.transpose(out=out, in_=in_, identity=ident), r=r, w=w)

    def tt(self, eng, out, in0, in1, op, r, w):
        self.P.add(eng, lambda e: e.tensor_tensor(out=out, in0=in0, in1=in1, op=op), r=r, w=w)

    def ts(self, eng, out, in0, s1, s2, op0, op1, r, w):
        if s2 is None:
            self.P.add(eng, lambda e: e.tensor_scalar(out=out, in0=in0, scalar1=s1, scalar2=None, op0=op0), r=r, w=w)
        else:
            self.P.add(eng, lambda e: e.tensor_scalar(out=out, in0=in0, scalar1=s1, scalar2=s2, op0=op0, op1=op1), r=r, w=w)

    def stt(self, out, in0, scalar, in1, op0, op1, r, w):
        self.P.add("dve", lambda e: e.scalar_tensor_tensor(out=out, in0=in0, scalar=scalar, in1=in1, op0=op0, op1=op1), r=r, w=w)

    def act(self, out, in_, func, r, w, scale=1.0, bias=0.0, accum=None):
        if accum is not None:
            self.P.add("act", lambda e: e.activation(out=out, in_=in_, func=func, scale=scale, bias=bias, accum_out=accum), r=r, w=w)
        else:
            self.P.add("act", lambda e: e.activation(out=out, in_=in_, func=func, scale=scale, bias=bias), r=r, w=w)

    def setup_consts(self):
        W = self.W
        P = self.P
        self.load(self.identb[:], W["c_ident_bf"].h.ap(), [], ["identb"], "c0")
        self.load(self.identf[:], W["c_ident_f"].h.ap(), [], ["identf"], "c0")
        P.add("pool", lambda e: e.memset(self.cm05[:, 0:1], -0.5), w=["cm05"])
        a = self.sb.alloc
        gt = a("gtmp", [52, 128], F32)
        self.load(gt.ap(0, [[1, 128]], p0=48, pn=4), W["s5_b_glu"].ap(0, [[128, 4], [1, 128]]), [], ["gt"], "c1")
        self.load(gt.ap(0, [[1, 128]], p0=0, pn=32), W["norm_ffn"].ap(0, [[128, 32], [1, 128]]), [], ["gt"], "c1")
        self.load(gt.ap(0, [[1, 128]], p0=32, pn=16), W["norm_mix"].ap(0, [[128, 16], [1, 128]]), [], ["gt"], "c1")
        P.add("pe", lambda e: e.matmul(self.ps(0, 0, 52), lhsT=gt.ap(0, [[1, 128]], pn=52), rhs=self.identf.ap(0, [[1, 52]], pn=52),
                                       start=True, stop=True), r=["gt", "identf"], w=[self.pk(0)])
        src_row = {0: 0, 2: 8, 3: 16, 5: 24, 1: 32, 4: 40}
        for gi, r0 in src_row.items():
            self.copy("dve", self.G.ap(gi * 8, [[1, 8]]), self.ps(0, r0, 8), [self.pk(0)], ["G"])
        self.load(self.Gfin[:], W["norm_final"].ap(0, [[0, 128], [1, 1024]]), [], ["Gfin"], "c1")
        self.load(self.es[:], W["gqa_sink"].ap(0, [[0, 128], [1, 16]]), [], ["es"], "c1")
        self.act(self.es[:], self.es[:], AF.Exp, ["es"], ["es"])
        self.copy("dve", self.bglu.ap(0, [[1, 4]]), self.ps(0, 48, 4), [self.pk(0)], ["bglu"])

    def conv_weights(self):
        sb = self.sb
        mark = sb.ptr
        sb.ptr = sb.limit - 49152 - 512
        cf = [sb.alloc(f"cf{i}", [128, 4096], F32) for i in range(2)]
        cb = [sb.alloc(f"cb{i}", [128, 4096], BF16) for i in range(2)]
        self.cj = 0
        self.conv_q = []

        def job(src_pieces, dst_pieces, n, key):
            self.conv_q.append(lambda: job_emit(src_pieces, dst_pieces, n, key))

        def job_emit(src_pieces, dst_pieces, n, key):
            s = self.cj % 2
            self.cj += 1
            for pi_, (sap, off, dims) in enumerate(src_pieces):
                self.P.add("pool", lambda e, o_=cf[s].ap(off, dims), i_=sap: e.dma_start(out=o_, in_=i_), r=[], w=[("cf", s)], dma=f"cf{s}p{pi_}")
            fl = [[1, n]]
            eng = "act"
            self.copy(eng, cb[s].ap(0, fl), cf[s].ap(0, fl), [("cf", s)], [("cb", s)])
            for pi_, (dap, off, dims) in enumerate(dst_pieces):
                self.P.add("pool", lambda e, dap=dap, off=off, dims=dims: e.dma_start(out=dap, in_=cb[s].ap(off, dims)),
                           r=[("cb", s)], w=[key], dma=f"cs{s}p{pi_}")

        W, S = self.W, self.S
        for f in range(4):
            for gu, nm in enumerate(["w_ffn_gate", "w_ffn_up"]):
                base = f * 1024 * 2816
                for fc0 in range(0, 22, 4):
                    nf = min(4, 22 - fc0)
                    sp, dp = [], []
                    for j in range(nf):
                        fc = fc0 + j
                        sp.append((W[nm].ap(base + fc * 128, [[2816, 128], [128 * 2816, 8], [1, 128]]), j * 1024, [[128, 8], [1, 128]]))
                        dp.append((S["Wgu"].ap(((f * 22 + fc) * 128) * 2048 + gu * 1024, [[2048, 128], [1, 1024]]), j * 1024, [[1, 1024]]))
                    job(sp, dp, nf * 1024, "Wgu")
            base = f * 2816 * 1024
            for half in range(2):
                for fc0 in range(0, 22, 8):
                    nf = min(8, 22 - fc0)
                    sap = W["w_ffn_down"].ap(base + fc0 * 128 * 1024 + half * 512, [[1024, 128], [128 * 1024, nf], [1, 512]])
                    dst = S["Wd"].ap((((f * 2 + half) * 22 + fc0) * 128) * 512, [[512, 128], [128 * 512, nf], [1, 512]])
                    job([(sap, 0, [[512, nf], [1, 512]])], [(dst, 0, [[512, nf], [1, 512]])], nf * 512, "Wd")

        def generic(nm, ncols, slabs, dstname):
            for si, c0 in enumerate(slabs):
                sap = W[nm].ap(c0, [[ncols, 128], [128 * ncols, 8], [1, 512]])
                dst = S[dstname].ap(si * 128 * 4096, [[4096, 128], [1, 4096]])
                job([(sap, 0, [[512, 8], [1, 512]])], [(dst, 0, [[1, 4096]])], 4096, dstname)

        generic("w_in_even", 2048, [0, 512, 1024, 1536], "WinE")
        generic("w_out_even", 1024, [0, 512], "WoutE")
        generic("w_out_odd", 1024, [0, 512], "WoutO")
        for si in range(2):
            pieces = []
            for t_ in range(4):
                for e_ in range(2):
                    h = GQA_PAIRS[si * 4 + t_][e_]
                    sap = W["w_in_odd"].ap(h * 64, [[1536, 128], [128 * 1536, 8], [1, 64]])
                    pieces.append((sap, t_ * 128 + e_ * 64, [[512, 8], [1, 64]]))
            dst = S["WinO"].ap(si * 128 * 4096, [[4096, 128], [1, 4096]])
            job(pieces, [(dst, 0, [[1, 4096]])], 4096, "WinO")
        sap = W["w_in_odd"].ap(1024, [[1536, 128], [128 * 1536, 8], [1, 512]])
        dst = S["WinO"].ap(2 * 128 * 4096, [[4096, 128], [1, 4096]])
        job([(sap, 0, [[512, 8], [1, 512]])], [(dst, 0, [[1, 4096]])], 4096, "WinO")
        sap = W["s5_w_glu"].ap(0, [[512, 128], [128 * 512, 4], [1, 512]])
        dst = S["Wglu"].ap(0, [[2048, 128], [1, 2048]])
        job([(sap, 0, [[512, 4], [1, 512]])], [(dst, 0, [[1, 2048]])], 2048, "Wglu")
        sb.ptr = mark

    def pump(self, n):
        for _ in range(n):
            if self.conv_q:
                self.conv_q.pop(0)()


    def alloc_ffn(self, NB, vkind=None):
        sb = self.sb
        sb.ptr = self.arena0
        a = sb.alloc
        self.NB = NB
        NT = NB * 128
        self.NT = NT
        self.xts = [a(f"xt{i}", [128, NB, 1024], F32) for i in range(2)]
        self.xt, self.xpar = self.xts[0], 0
        self.xs = [a(f"xs{i}", [128, 1024], BF16) for i in range(2)]
        self.junk = a("junk", [128, 1024], BF16)
        self.hn = a("hn", [128, 8, NT], BF16)
        self.h = a("h", [128, 22, NT], BF16)
        self.sg = [a(f"sg{i}", [128, 512], BF16) for i in range(2)]
        self.NGU = 3
        self.gu = [a(f"gu{i}", [128, 2, 8, 128], BF16) for i in range(self.NGU)]
        self.ND = 24
        self.dr = [a(f"dr{i}", [128, 512], BF16) for i in range(self.ND)]
        self.NPW = 2
        self.pw = [a(f"pw{i}", [128, 8, 512], BF16) for i in range(self.NPW)]
        self.stg = [a(f"stg{i}", [128, 2, NT], BF16) for i in range(2)]
        if vkind == "v":
            self.vst = a("vst", [128, NB, 8, 66], BF16)
        if vkind == "v2":
            self.v2st = a("v2st", [128, NB, 4, 66], BF16)
        self.cnt_gu = self.cnt_d = self.cnt_pw = self.cnt_stg = 0
        self.dgrp = 0
        P = self.P
        if vkind == "v":
            va = self.vst.ap(64, [[66, NB * 8], [1, 2]])
            P.add("pool", lambda e, va=va: e.memset(va, 1.0), w=["vst"])
        if vkind == "v2":
            vb = self.v2st.ap(64, [[66, NB * 4], [1, 2]])
            P.add("pool", lambda e, vb=vb: e.memset(vb, 1.0), w=["v2st"])

    def alloc_attn(self, kind):
        sb = self.sb
        sb.ptr = self.arena0
        a = sb.alloc
        self.NB = 4
        self.NT = 512
        self.xts = [a(f"xt{i}", [128, 4, 1024], F32) for i in range(2)]
        self.qTs = [a(f"qT{i}", [128, 8, 512], BF16) for i in range(2)]
        self.kTs = [a(f"kT{i}", [128, 4, 1024], BF16) for i in range(2)]
        self.vTs = [a(f"vT{i}", [128, 8, 528], BF16) for i in range(2)]
        self.yaTs = [a(f"yaT{i}", [128, 4, 512], BF16) for i in range(2)]
        self.set_par(0)
        self.mix = a("mix", [128, 8, 512], BF16)
        self.E = [a(f"E{i}", [128, 640], F32) for i in range(4)]
        self.Pm = [a(f"Pm{i}", [128, 640], BF16) for i in range(4)]
        self.yb = [a(f"yb{i}", [128, 1024], BF16) for i in range(2)]
        self.dsum = a("dsum", [128, 16], F32)
        self.NPW = 2
        self.pw = [a(f"pw{i}", [128, 8, 512], BF16) for i in range(self.NPW)]
        self.cnt_pw = 0
        n = 8 * 21 * 128 if kind == "na" else 16 * 3 * 128
        self.TAB = a("TAB", [128, n], BF16)
        self.load(self.TAB.ap(0, [[1, n]]), self.S["TNA" if kind == "na" else "TGQ"].ap(0, [[n, 128], [1, n]]),
                  ["TABscr"], ["TAB"], "tabld")
        for par in range(2):
            zt = self.qTs[par] if kind == "na" else self.kTs[par]
            za = zt.ap(0, [[1, 4096]])
            self.P.add("pool", lambda e, za=za: e.memset(za, 0.0), w=[("qT", par), ("kT", par)])

    def set_par(self, par):
        self.xpar = par
        self.xt = self.xts[par]
        if hasattr(self, "qTs"):
            self.qT, self.kT, self.vT, self.yaT = self.qTs[par], self.kTs[par], self.vTs[par], self.yaTs[par]

    def stats(self, blk):
        st, x = self.st, self.xt
        self.act(self.junk[:], x.ap(blk * 1024, [[1, 1024]]), AF.Square, [("x", self.xpar, blk)], ["junk", ("st", blk)],
                 accum=st.ap(blk * 4, [[1, 1]]))
        self.ts("dve", st.ap(blk * 4 + 1, [[1, 1]]), st.ap(blk * 4, [[1, 1]]), 1.0 / 1024, EPS, ALU.mult, ALU.add,
                [("st", blk)], [("st", blk)])
        self.tt("pool", st.ap(blk * 4 + 2, [[1, 1]]), st.ap(blk * 4 + 1, [[1, 1]]), self.cm05.ap(0, [[1, 1]]), ALU.pow,
                [("st", blk), "cm05"], [("st", blk)])

    def norm_fm(self, gidx):
        x, NT = self.xt, self.NT
        for blk in range(self.NB):
            self.stats(blk)
        for blk in range(self.NB):
            xs = self.xs[blk % 2]
            self.act(xs[:], x.ap(blk * 1024, [[1, 1024]]), AF.Copy, [("x", self.xpar, blk), ("st", blk)], [("xs", blk % 2)],
                     scale=self.st.ap(blk * 4 + 2, [[1, 1]]))
            bank = blk % 2
            for kc in range(8):
                self.tr(self.psb(bank, kc * 128, 128), xs.ap(kc * 128, [[1, 128]]), self.identb[:],
                        [("xs", blk % 2), "identb"], [self.pk(bank)])
            self.tt("dve", self.hn.ap(blk * 128, [[NT, 8], [1, 128]]),
                    self.psb(bank).rearrange("p (k t) -> p k t", k=8),
                    self.G.ap(gidx * 8, [[1, 8], [0, 128]]), ALU.mult, [self.pk(bank), "G"], [("hn", blk // 4)])

    def ffn(self, f, gidx):
        S, x, NT = self.S, self.xt, self.NT
        NH = self.NB // 4
        self.norm_fm(gidx)
        c = 0
        for fc in range(NFC):
            s = self.cnt_gu % self.NGU
            self.cnt_gu += 1
            gut = self.gu[s]
            self.load(gut.ap(0, [[1, 2048]]), S["Wgu"].ap((f * 22 + fc) * 128 * 2048, [[2048, 128], [1, 2048]]),
                      ["Wgu"], [("gu", s)], f"gu{s}")
            for th in range(NH):
                st_ = c % 2
                c += 1
                gb, ub = 2 * st_, 2 * st_ + 1
                for kc in range(8):
                    self.mm(self.ps(gb), gut.ap(kc * 128, [[1, 128]]), self.hn.ap(kc * NT + th * 512, [[1, 512]]), kc == 0, kc == 7,
                            [("gu", s), ("hn", th)], [self.pk(gb)])
                for kc in range(8):
                    self.mm(self.ps(ub), gut.ap(1024 + kc * 128, [[1, 128]]), self.hn.ap(kc * NT + th * 512, [[1, 512]]), kc == 0, kc == 7,
                            [("gu", s), ("hn", th)], [self.pk(ub)])
                self.act(self.sg[st_][:], self.ps(gb), AF.Silu, [self.pk(gb)], [("sg", st_)])
                self.tt("dve", self.h.ap(fc * NT + th * 512, [[1, 512]]), self.ps(ub), self.sg[st_][:], ALU.mult,
                        [self.pk(ub), ("sg", st_)], [("h", fc, th)])
        for half in range(2):
            slots = []
            for fc in range(NFC):
                s = self.cnt_d % self.ND
                self.cnt_d += 1
                slots.append(s)
                self.load(self.dr[s][:], S["Wd"].ap(((f * 2 + half) * 22 + fc) * 128 * 512, [[512, 128], [1, 512]]),
                          ["Wd"], [("dr", s)], f"dr{s % 6}")
            for pg in range(self.NB // 2):
                bk = (4, 5) if self.dgrp % 2 == 0 else (6, 7)
                self.dgrp += 1
                for fc in range(NFC):
                    s = slots[fc]
                    for j in range(2):
                        blk = pg * 2 + j
                        self.mm(self.ps(bk[j]), self.h.ap(fc * NT + blk * 128, [[1, 128]]), self.dr[s][:],
                                fc == 0, fc == NFC - 1, [("h", fc, blk // 4), ("dr", s)], [self.pk(bk[j])])
                for j in range(2):
                    blk = pg * 2 + j
                    xa = x.ap(blk * 1024 + half * 512, [[1, 512]])
                    self.stt(xa, self.ps(bk[j]), 0.5, xa, ALU.mult, ALU.add, [self.pk(bk[j]), ("x", self.xpar, blk)], [("x", self.xpar, blk)])

    def pw_load(self, wname, slab):
        s = self.cnt_pw % self.NPW
        self.cnt_pw += 1
        self.load(self.pw[s].ap(0, [[1, 4096]]), self.S[wname].ap(slab * 128 * 4096, [[4096, 128], [1, 4096]]),
                  [wname], [("pw", s)], f"pw{s}")
        return s

    def proj_fm(self, s, ots, dst_name, dst_tile0, tok, dkeys):
        NT = self.NT
        LM = self.LMAX
        for c0 in range(0, len(ots), 2):
            grp = ots[c0:c0 + 2]
            g = self.cnt_stg % 2
            self.cnt_stg += 1
            stg = self.stg[g]
            for i, ot in enumerate(grp):
                for th in range(NT // 512):
                    bank = self.nbank() % 4
                    for kc in range(8):
                        self.mm(self.ps(bank), self.pw[s].ap(kc * 512 + ot * 128, [[1, 128]]), self.hn.ap(kc * NT + th * 512, [[1, 512]]),
                                kc == 0, kc == 7, [("pw", s), ("hn", th)], [self.pk(bank)])
                    self.copy(self.ew(), stg.ap(i * NT + th * 512, [[1, 512]]), self.ps(bank), [self.pk(bank)], [("stg", g)])
            n = len(grp)
            self.store(self.S[dst_name].ap((dst_tile0 + c0) * 128 * LM + tok, [[LM, 128], [128 * LM, n], [1, NT]]),
                       stg.ap(0, [[NT, n], [1, NT]]), [("stg", g)], dkeys, f"stg{g}")

    def proj_tm_v(self, s, col0, nh, vst, vkey, dst_name, rowlen, tok, dkeys):
        ncol = nh * 64
        NB, NT = self.NB, self.NT
        for blk in range(NB):
            bank = 4 + blk % 4
            for kc in range(8):
                self.mm(self.ps(bank, 0, ncol), self.hn.ap(kc * NT + blk * 128, [[1, 128]]),
                        self.pw[s].ap(kc * 512 + col0, [[1, ncol]]), kc == 0, kc == 7, [("pw", s), ("hn", blk // 4)], [self.pk(bank)])
            self.copy(self.ew(), vst.ap(blk * nh * 66, [[66, nh], [1, 64]]), self.psv(bank, 0, [[64, nh], [1, 64]]),
                      [self.pk(bank)], [vkey])
        self.store(self.S[dst_name].ap(tok * rowlen, [[rowlen, 128], [128 * rowlen, NB], [1, rowlen]]),
                   vst.ap(0, [[rowlen, NB], [1, rowlen]]), [vkey], dkeys, "vst")

    def xkeys(self, row0):
        return [("xscr", t) for t in range(row0 // 512, (row0 + self.NT) // 512)]

    def x_load(self, src, row0, par):
        NB = self.NB
        keys = [("x", par, b) for b in range(NB)]
        self.load(self.xts[par].ap(0, [[1024, NB], [1, 1024]]), src.ap(row0 * D, [[D, 128], [128 * D, NB], [1, D]]),
                  self.xkeys(row0) if src is self.S["x"] else [], keys, f"xld{par}")

    def x_store(self, dst, row0, key):
        NB = self.NB
        keys = [("x", self.xpar, b) for b in range(NB)]
        w = self.xkeys(row0) if dst is self.S["x"] else []
        self.store(dst.ap(row0 * D, [[D, 128], [128 * D, NB], [1, D]]), self.xt.ap(0, [[1024, NB], [1, 1024]]),
                   keys, w, key + str(self.xpar))

    def skeys(self, nm, tok):
        return [(nm, t) for t in range(tok // 512, (tok + self.NT) // 512)]

    def phaseA(self, tok0, L, NB):
        self.alloc_ffn(NB, "v")
        NT = self.NT
        nt_ = L // NT
        self.x_load(self.W["x_in"], tok0, 0)
        for t in range(nt_):
            tok = NT * t
            self.set_par(t % 2)
            if t + 1 < nt_:
                self.x_load(self.W["x_in"], tok0 + tok + NT, (t + 1) % 2)
            self.ffn(0, 0)
            self.norm_fm(1)
            for slab, nm in enumerate(["u", "q", "k"]):
                s = self.pw_load("WinE", slab)
                self.proj_fm(s, [0, 1, 2, 3], nm, 0, tok, self.skeys(nm + "scr", tok))
            s = self.pw_load("WinE", 3)
            self.proj_tm_v(s, 0, 8, self.vst, "vst", "v", 528, tok, self.skeys("vscr", tok))
            self.x_store(self.S["x"], tok, "xst")

    def attn_block(self, heads, np_, kfn, qfn, vfn, tabfn, bslot, sink):
        yb = self.yb[bslot]

        DEP = 3 if np_ > 4 else 4
        slot_w = 640 if np_ > 4 else 512

        def stage1(hi, h):
            ss = hi % DEP
            c0 = ss * slot_w
            banks = sorted(set([self.pk(c0 // 512), self.pk((c0 + np_ * 128 - 1) // 512)]))
            for i in range(np_):
                self.mm(self.psum.ap(c0 + i * 128, [[1, 128]]), kfn(h, i), qfn(h), True, True, [("kT", self.xpar), ("qT", self.xpar)], banks)
            self.act(self.E[ss].ap(0, [[1, np_ * 128]]), self.psum.ap(c0, [[1, np_ * 128]]), AF.Exp, banks, [("E", ss)], scale=0.125)
            self.tt("dve", self.Pm[ss].ap(0, [[1, np_ * 128]]), self.E[ss].ap(0, [[1, np_ * 128]]), tabfn(h), ALU.mult,
                    [("E", ss), "TAB"], [("Pm", ss)])

        def stage2(hi, h):
            ss = hi % DEP
            ob = 4 + hi // 4
            for i in range(np_):
                self.mm(self.ps(ob, (hi % 4) * 65, 65), self.Pm[ss].ap(i * 128, [[1, 128]]), vfn(h, i), i == 0, i == np_ - 1,
                        [("Pm", ss), ("vT", self.xpar)], [self.pk(ob)])

        nhd = len(heads)
        for hi, h in enumerate(heads):
            stage1(hi, h)
            if hi >= DEP - 1:
                stage2(hi - DEP + 1, heads[hi - DEP + 1])
        for hi in range(max(0, nhd - DEP + 1), nhd):
            stage2(hi, heads[hi])
        for hb in range((len(heads) + 3) // 4):
            ob = 4 + hb
            hs = heads[hb * 4:(hb + 1) * 4]
            nh = len(hs)
            den = self.psv(ob, 64, [[65, nh]])
            rc = self.rec.ap(hb * 4, [[1, nh]])
            if sink:
                dsa = self.dsum.ap(hb * 4, [[1, nh]])
                self.tt("dve", dsa, den, self.es.ap(hs[0], [[1, nh]]), ALU.add, [self.pk(ob), "es"], ["dsum"])
                self.P.add("dve", lambda e, rc=rc, dsa=dsa: e.reciprocal(out=rc, in_=dsa), r=["dsum"], w=["rec"])
            else:
                self.P.add("dve", lambda e, rc=rc, den=den: e.reciprocal(out=rc, in_=den), r=[self.pk(ob)], w=["rec"])
            self.tt("dve", yb.ap(hs[0] * 64, [[64, nh], [1, 64]]), self.psv(ob, 0, [[65, nh], [1, 64]]),
                    self.rec.ap(hb * 4, [[1, nh], [0, 64]]), ALU.mult, [self.pk(ob), "rec"], [("yb", bslot)])

    def out_proj(self, wname, lhs_fn, rkeys):
        for half in range(2):
            s = self.pw_load(wname, half)
            for blk in range(4):
                bank = 4 + blk
                for kc in range(8):
                    self.mm(self.ps(bank), lhs_fn(kc, blk), self.pw[s].ap(kc * 512, [[1, 512]]), kc == 0, kc == 7,
                            rkeys + [("pw", s)], [self.pk(bank)])
                xa = self.xt.ap(blk * 1024 + half * 512, [[1, 512]])
                self.stt(xa, self.ps(bank), 1.0, xa, ALU.mult, ALU.add, [self.pk(bank), ("x", self.xpar, blk)], [("x", self.xpar, blk)])

    def phaseC1(self, L):
        S = self.S
        LM = self.LMAX
        NB_ = L // 128
        self.alloc_attn("na")
        def loads(t, par):
            tok = 512 * t
            self.x_load(S["x"], tok, par)
            lo, hi = max(0, tok - 256), min(L, tok + 768)
            kw = hi - lo
            tl = list(range(lo // 512, (hi + 511) // 512))
            for e_ in range(2):
                self.load(self.qTs[par].ap(e_ * 512, [[1024, 4], [1, 512]], p0=64 * e_, pn=64),
                          S["q"].ap(64 * e_ * LM + tok, [[LM, 64], [128 * LM, 4], [1, 512]]), [("qscr", t)], [("qT", par)], f"qT{e_}{par}")
            self.load(self.kTs[par].ap(0, [[1024, 4], [1, kw]]), S["k"].ap(lo, [[LM, 128], [128 * LM, 4], [1, kw]]),
                      [("kscr", i) for i in tl], [("kT", par)], f"kT{par}")
            self.load(self.vTs[par].ap(0, [[528, kw // 128], [1, 528]]), S["v"].ap(lo * 528, [[528, 128], [128 * 528, kw // 128], [1, 528]]),
                      [("vscr", i) for i in tl], [("vT", par)], f"vT{par}")
            self.load(self.yaTs[par].ap(0, [[512, 4], [1, 512]]), S["ya"].ap(tok, [[LM, 128], [128 * LM, 4], [1, 512]]),
                      [("yascr", t)], [("yaT", par)], f"yaT{par}")

        nt_ = L // 512
        loads(0, 0)
        for t in range(nt_):
            tok = 512 * t
            self.set_par(t % 2)
            if t + 1 < nt_:
                loads(t + 1, (t + 1) % 2)
            lo = max(0, tok - 256)
            for b in range(4):
                n = 4 * t + b
                cls = "first" if n == 0 else "second" if n == 1 else "last" if n == NB_ - 1 else "slast" if n == NB_ - 2 else "interior"
                offs = NA_OFFS[cls]
                pb = NA_PBASE[cls]
                np_ = len(offs)
                kb = [((n + o) * 128 - lo) for o in offs]
                bslot = b % 2
                self.attn_block(
                    list(range(8)), np_,
                    lambda h, i: self.kT.ap((h // 2) * 1024 + kb[i], [[1, 128]]),
                    lambda h: self.qT.ap((h // 2) * 1024 + (h % 2) * 512 + b * 128, [[1, 128]]),
                    lambda h, i: self.vT.ap((kb[i] // 128) * 528 + h * 66, [[1, 65]]),
                    lambda h: self.TAB.ap((h * 21 + pb) * 128, [[1, np_ * 128]]),
                    bslot, False)
                for kc in range(4):
                    self.tr(self.psb(6, kc * 128, 128), self.yb[bslot].ap(kc * 128, [[1, 128]]), self.identb[:],
                            [("yb", bslot), "identb"], [self.pk(6)])
                self.copy(self.ew(), self.mix.ap(4 * 512 + b * 128, [[512, 4], [1, 128]]),
                          self.psb(6, 0, 512).rearrange("p (k t) -> p k t", k=4), [self.pk(6)], ["mix"])
            self.out_proj("WoutE", lambda kc, blk: (self.yaT.ap(kc * 512 + blk * 128, [[1, 128]]) if kc < 4
                                                      else self.mix.ap(kc * 512 + blk * 128, [[1, 128]])), [("yaT", self.xpar), "mix"])
            self.x_store(S["x"], tok, "xst")

    def phaseC2(self, L, NB):
        self.alloc_ffn(NB, "v2")
        NT = self.NT
        nt_ = L // NT
        self.x_load(self.S["x"], 0, 0)
        for t in range(nt_):
            tok = NT * t
            self.set_par(t % 2)
            if t + 1 < nt_:
                self.x_load(self.S["x"], tok + NT, (t + 1) % 2)
            self.ffn(1, 2)
            self.ffn(2, 3)
            self.norm_fm(4)
            for slab in range(2):
                s = self.pw_load("WinO", slab)
                self.proj_fm(s, [0, 1, 2, 3], "q2", slab * 4, tok, self.skeys("q2scr", tok))
            s = self.pw_load("WinO", 2)
            self.proj_fm(s, [0, 1], "k2", 0, tok, self.skeys("k2scr", tok))
            self.proj_tm_v(s, 256, 4, self.v2st, "v2st", "v2", 264, tok, self.skeys("v2scr", tok))
            self.x_store(self.S["x"], tok, "xst")

    def phaseD1(self, L):
        S = self.S
        LM = self.LMAX
        NB_ = L // 128
        hmap = {}
        for tq, (ha, hb_) in enumerate(GQA_PAIRS):
            hmap[ha] = (tq, 0)
            hmap[hb_] = (tq, 1)
        self.alloc_attn("gqa")
        def loads(t, par):
            tok = 512 * t
            self.x_load(S["x"], tok, par)
            lo, hi = max(0, tok - 128), min(L, tok + 640)
            kw = hi - lo
            tl = list(range(lo // 512, (hi + 511) // 512))
            self.load(self.qTs[par].ap(0, [[512, 8], [1, 512]]), S["q2"].ap(tok, [[LM, 128], [128 * LM, 8], [1, 512]]),
                      [("q2scr", t)], [("qT", par)], f"qT0{par}")
            for e_ in range(2):
                self.load(self.kTs[par].ap(e_ * 1024, [[2048, 2], [1, kw]], p0=64 * e_, pn=64),
                          S["k2"].ap(64 * e_ * LM + lo, [[LM, 64], [128 * LM, 2], [1, kw]]), [("k2scr", i) for i in tl], [("kT", par)], f"kT{e_}{par}")
            self.load(self.vTs[par].ap(0, [[528, kw // 128], [1, 264]]), S["v2"].ap(lo * 264, [[264, 128], [128 * 264, kw // 128], [1, 264]]),
                      [("v2scr", i) for i in tl], [("vT", par)], f"vT{par}")

        nt_ = L // 512
        loads(0, 0)
        for t in range(nt_):
            tok = 512 * t
            self.set_par(t % 2)
            if t + 1 < nt_:
                loads(t + 1, (t + 1) % 2)
            lo = max(0, tok - 128)
            for b in range(4):
                n = 4 * t + b
                offs = [o for o in (-1, 0, 1) if 0 <= n + o < NB_]
                np_ = len(offs)
                kb = [((n + o) * 128 - lo) for o in offs]
                o0 = offs[0] + 1
                bslot = b % 2
                for hh in range(2):
                    self.attn_block(
                        list(range(hh * 8, hh * 8 + 8)), np_,
                        lambda h, i: self.kT.ap(((h // 4) // 2) * 2048 + hmap[h][1] * 1024 + kb[i], [[1, 128]]),
                        lambda h: self.qT.ap(hmap[h][0] * 512 + b * 128, [[1, 128]]),
                        lambda h, i: self.vT.ap((kb[i] // 128) * 528 + (h // 4) * 66, [[1, 65]]),
                        lambda h: self.TAB.ap((h * 3 + o0) * 128, [[1, np_ * 128]]),
                        bslot, True)
                for kc in range(8):
                    self.tr(self.psb(6, kc * 128, 128), self.yb[bslot].ap(kc * 128, [[1, 128]]), self.identb[:],
                            [("yb", bslot), "identb"], [self.pk(6)])
                self.copy(self.ew(), self.mix.ap(b * 128, [[512, 8], [1, 128]]),
                          self.psb(6).rearrange("p (k t) -> p k t", k=8), [self.pk(6)], ["mix"])
            self.out_proj("WoutO", lambda kc, blk: self.mix.ap(kc * 512 + blk * 128, [[1, 128]]), ["mix"])
            self.x_store(S["x"], tok, "xst")

    def phaseD2(self, tok0, L, NB):
        self.alloc_ffn(NB)
        NT = self.NT
        nt_ = L // NT
        self.x_load(self.S["x"], 0, 0)
        for t in range(nt_):
            tok = NT * t
            self.set_par(t % 2)
            if t + 1 < nt_:
                self.x_load(self.S["x"], tok + NT, (t + 1) % 2)
            self.ffn(3, 5)
            x = self.xt
            for blk in range(self.NB):
                self.stats(blk)
                xa = x.ap(blk * 1024, [[1, 1024]])
                self.stt(xa, xa, self.st.ap(blk * 4 + 2, [[1, 1]]), self.Gfin[:], ALU.mult, ALU.mult,
                         [("x", self.xpar, blk), ("st", blk), "Gfin"], [("x", self.xpar, blk)])
            self.x_store(self.y_out, tok0 + tok, "yst")


    def setup_tables(self):
        sb = self.sb
        sb.ptr = self.arena0
        a = sb.alloc
        W = self.W
        P = self.P
        self.TNA = a("TNA", [128, 8, 21, 128], BF16)
        self.TGQ = a("TGQ", [128, 16, 3, 128], BF16)
        rpn = a("rpn", [120, 31], F32)
        self.load(rpn[:], W["na_rpb"].ap(0, [[31, 120], [1, 31]]), [], ["rpn"], "c2")
        self.mm(self.ps(0, 0, 120, pn=31), rpn[:], self.identf.ap(0, [[1, 120]], pn=120), True, True, ["rpn", "identf"], [self.pk(0)])
        erb = a("erb", [31, 120], BF16)
        self.act(erb[:], self.ps(0, 0, 120, pn=31), AF.Exp, [self.pk(0)], ["erb"])
        ohc = a("ohc", [31, 64 * 128], BF16)
        self.load(ohc[:], W["c_ohc"].ap(0, [[64 * 128, 31], [1, 64 * 128]]), [], ["ohc"], "c2")
        esub = a("esub", [128, 8, 15, 64], BF16)
        for qb in range(16):
            bank = 1 + qb % 3
            for q4 in range(4):
                qc = qb * 4 + q4
                self.mm(self.ps(bank, q4 * 120, 120), ohc.ap(qc * 128, [[1, 128]], pn=31), erb.ap(0, [[1, 120]], pn=31),
                        True, True, ["ohc", "erb"], [self.pk(bank)])
            self.copy(self.ew(), esub.ap(qb * 4, [[1, 4], [15 * 64, 8], [64, 15]]), self.psv(bank, 0, [[120, 4], [15, 8], [1, 15]]),
                      [self.pk(bank)], ["esub"])
            self.pump(1)
        k = 0
        for cls in NA_CLASSES:
            for i, blk in enumerate(na_pattern(cls)):
                pat = NA_PBASE[cls] + i
                for kr in range(2):
                    for qr in range(2):
                        dst = self.TNA.ap(pat * 128 + qr * 64, [[21 * 128, 8], [1, 64]], p0=64 * kr, pn=64)
                        rr = blk[kr][qr]
                        k += 1
                        if rr is None:
                            P.add("dve", lambda e, dst=dst: e.memset(dst, 0.0), w=["TAB"])
                        else:
                            self.copy(["dve", "act"][k % 2], dst,
                                      esub.ap(rr * 64, [[15 * 64, 8], [1, 64]], p0=64 * kr, pn=64), ["esub"], ["TAB"])
        t5n = a("t5n", [32, 16], F32)
        self.load(t5n[:], W["t5_table"].ap(0, [[16, 32], [1, 16]]), [], ["t5n"], "c2")
        etb = a("etb", [32, 16], BF16)
        self.act(etb[:], t5n[:], AF.Exp, ["t5n"], ["etb"])
        ohg = a("ohg", [32, 512], BF16)
        self.load(ohg[:], W["c_ohg"].ap(0, [[512, 32], [1, 512]]), [], ["ohg"], "c2")
        for oi in range(3):
            for qb in range(4):
                bank = 4 + (oi * 4 + qb) % 4
                for ql in range(32):
                    q = qb * 32 + ql
                    s = oi * 128 + 128 - q
                    self.mm(self.ps(bank, ql * 16, 16), ohg.ap(s, [[1, 128]], pn=32), etb.ap(0, [[1, 16]], pn=32), True, True,
                            ["ohg", "etb"], [self.pk(bank)])
                self.copy(self.ew(), self.TGQ.ap(oi * 128 + qb * 32, [[1, 32], [3 * 128, 16]]), self.psv(bank, 0, [[16, 32], [1, 16]]),
                          [self.pk(bank)], ["TAB"])
                self.pump(1)

        n1, n2 = 8 * 21 * 128, 16 * 3 * 128
        self.store(self.S["TNA"].ap(0, [[n1, 128], [1, n1]]), self.TNA.ap(0, [[1, n1]]), ["TAB"], ["TABscr"], "tabst")
        self.store(self.S["TGQ"].ap(0, [[n2, 128], [1, n2]]), self.TGQ.ap(0, [[1, n2]]), ["TAB"], ["TABscr"], "tabst")

    def setup_s5(self):
        sb = self.sb
        sb.ptr = self.arena0
        a = sb.alloc
        W, S, P = self.W, self.S, self.P
        V = "dve"
        nat = a("nat", [64, 128], F32)
        self.load(nat.ap(0, [[1, 64]], pn=64), W["s5_lam_re"].ap(0, [[64, 64], [1, 64]]), [], ["nat"], "c3")
        self.load(nat.ap(64, [[1, 64]], pn=64), W["s5_lam_im"].ap(0, [[64, 64], [1, 64]]), [], ["nat"], "c3")
        for i in range(2):
            self.mm(self.ps(0, i * 64, 64, pn=64), nat.ap(i * 64, [[1, 64]], pn=64), self.identf.ap(0, [[1, 64]], pn=64), True, True,
                    ["nat", "identf"], [self.pk(0)])
        f64 = lambda nm: a(nm, [64, 64], F32)
        lre, lim, ldt, zr, th, den, nr, cr, ci, t64 = [f64(n) for n in ["lre", "lim", "ldt", "zr", "th", "den", "nr", "cr", "ci", "t64"]]
        K = "s5s"
        self.copy(V, lre[:], self.ps(0, 0, 64, pn=64), [self.pk(0)], [K])
        self.copy(V, lim[:], self.ps(0, 64, 64, pn=64), [self.pk(0)], [K])
        self.load(ldt[:], W["s5_log_dt"].ap(0, [[0, 64], [1, 64]]), [], [K], "c3")
        self.act(ldt[:], ldt[:], AF.Exp, [K], [K])
        self.tt(V, zr[:], lre[:], ldt[:], ALU.mult, [K], [K])
        self.tt(V, th[:], lim[:], ldt[:], ALU.mult, [K], [K])
        if getattr(self, "s5_stop", 99) <= 1:
            return
        expo = a("expo", [64, 3, 2, 32], F32)
        self.load(expo[:], W["c_expo"].ap(0, [[192, 64], [1, 192]]), [], [K], "c3")
        big = lambda nm: a(nm, [64, 2, 32, 32], F32)
        PWre = [big(f"pwre{i}") for i in range(3)]
        PWim = [big(f"pwim{i}") for i in range(3)]
        m16 = lambda nm: a(nm, [64, 64, 16], F32)
        bre, bim, Bre, Bim, tb = m16("bre"), m16("bim"), m16("Bre"), m16("Bim"), m16("tb")
        cn = a("cn", [128, 8, 64], F32)
        ct_tiles = [a(f"ct{ci_}", [64, 1024], F32) for ci_ in range(2)]
        a1re, a1im = f64("a1re"), f64("a1im")
        drep = a("drep", [32, 128], F32)
        dcol = a("dcol", [128, 32], F32)
        msk = a("msk", [128, 2, 128], F32)
        mark_T = sb.ptr
        T1, T2, T3, T4 = big("T1"), big("T2"), big("T3"), big("T4")
        fl = [[1, 2048]]
        d3 = [[1024, 2], [32, 32], [1, 32]]
        TWO_PI = 2.0 * math.pi
        C1 = 6.28125
        C2 = float(np.float32(TWO_PI - C1))
        C3 = float(TWO_PI - C1 - C2)
        MAGIC = 12582912.0
        for tab in range(3):
            ex_b = expo.ap(tab * 64, [[32, 2], [0, 32], [1, 32]])
            self.tt(V, T1.ap(0, d3), th.ap(0, [[32, 2], [1, 32], [0, 32]]), ex_b, ALU.mult, [K], [K])
            self.tt(V, T2.ap(0, d3), zr.ap(0, [[32, 2], [1, 32], [0, 32]]), ex_b, ALU.mult, [K], [K])
            self.act(T2.ap(0, fl), T2.ap(0, fl), AF.Exp, [K], [K])
            self.ts(V, T3.ap(0, fl), T1.ap(0, fl), 1.0 / TWO_PI, None, ALU.mult, None, [K], [K])
            self.ts(V, T3.ap(0, fl), T3.ap(0, fl), MAGIC, None, ALU.add, None, [K], [K])
            self.ts(V, T3.ap(0, fl), T3.ap(0, fl), -MAGIC, None, ALU.add, None, [K], [K])
            self.stt(T4.ap(0, fl), T3.ap(0, fl), -C1, T1.ap(0, fl), ALU.mult, ALU.add, [K], [K])
            self.stt(T4.ap(0, fl), T3.ap(0, fl), -C2, T4.ap(0, fl), ALU.mult, ALU.add, [K], [K])
            self.stt(T4.ap(0, fl), T3.ap(0, fl), -C3, T4.ap(0, fl), ALU.mult, ALU.add, [K], [K])
            self.ts(V, T3.ap(0, fl), T4.ap(0, fl), -0.5, None, ALU.mult, None, [K], [K])
            self.stt(T3.ap(0, fl), T4.ap(0, fl), 0.5, T3.ap(0, fl), ALU.mult, ALU.max, [K], [K])
            self.act(T1.ap(0, fl), T3.ap(0, fl), AF.Sin, [K], [K], scale=-1.0, bias=self.halfpi.ap(0, [[1, 1]], pn=64))
            self.act(T3.ap(0, fl), T4.ap(0, fl), AF.Sin, [K], [K], scale=0.5)
            self.stt(PWim[tab].ap(0, fl), T3.ap(0, fl), 2.0, T1.ap(0, fl), ALU.mult, ALU.mult, [K], [K])
            self.tt(V, T4.ap(0, fl), T3.ap(0, fl), T3.ap(0, fl), ALU.mult, [K], [K])
            self.ts(V, PWre[tab].ap(0, fl), T4.ap(0, fl), -2.0, 1.0, ALU.mult, ALU.add, [K], [K])
            self.tt(V, PWre[tab].ap(0, fl), PWre[tab].ap(0, fl), T2.ap(0, fl), ALU.mult, [K], [K])
            self.tt(V, PWim[tab].ap(0, fl), PWim[tab].ap(0, fl), T2.ap(0, fl), ALU.mult, [K], [K])
            self.pump(3)
        if getattr(self, "s5_stop", 99) <= 2:
            return
        for ri, PWt in enumerate([PWre[2], PWim[2]]):
            self.copy(V, self.A32.ap(ri * 64, [[1, 32]], pn=64), PWt.ap(31, [[32, 32]]), [K], ["A32"])
            self.copy(V, self.A32.ap(ri * 64 + 32, [[1, 32]], pn=64), PWt.ap(1024, [[32, 32]]), [K], ["A32"])
            dst = a1re if ri == 0 else a1im
            self.copy(V, dst.ap(0, [[1, 32]]), PWt.ap(0, [[32, 32]]), [K], [K])
            self.copy(V, dst.ap(32, [[1, 32]]), PWt.ap(1024 + 31, [[32, 32]]), [K], [K])
        self.ts(V, self.Ac2.ap(0, [[1, 64]], pn=64), self.A32.ap(64, [[1, 64]], pn=64), -1.0, None, ALU.mult, None, ["A32"], ["A32"])
        self.copy(V, self.Ac2.ap(64, [[1, 64]], pn=64), self.A32.ap(64, [[1, 64]], pn=64), ["A32"], ["A32"])
        self.tt(V, den[:], lre[:], lre[:], ALU.mult, [K], [K])
        self.tt(V, t64[:], lim[:], lim[:], ALU.mult, [K], [K])
        self.tt(V, den[:], den[:], t64[:], ALU.add, [K], [K])
        P.add(V, lambda e: e.reciprocal(out=den[:], in_=den[:]), r=[K], w=[K])
        self.ts(V, nr[:], a1re[:], -1.0, None, ALU.add, None, [K], [K])
        self.tt(V, cr[:], nr[:], lre[:], ALU.mult, [K], [K])
        self.tt(V, t64[:], a1im[:], lim[:], ALU.mult, [K], [K])
        self.tt(V, cr[:], cr[:], t64[:], ALU.add, [K], [K])
        self.tt(V, cr[:], cr[:], den[:], ALU.mult, [K], [K])
        self.tt(V, ci[:], a1im[:], lre[:], ALU.mult, [K], [K])
        self.tt(V, t64[:], nr[:], lim[:], ALU.mult, [K], [K])
        self.tt(V, ci[:], ci[:], t64[:], ALU.subtract, [K], [K])
        self.tt(V, ci[:], ci[:], den[:], ALU.mult, [K], [K])
        if getattr(self, "s5_stop", 99) <= 3:
            return
        for nm, dst in (("s5_b_re", bre), ("s5_b_im", bim)):
            for q in range(4):
                self.load(dst.ap(q * 256, [[16, 16], [1, 16]], pn=64),
                          W[nm].ap(q * 16 * 1024, [[16, 64], [1024, 16], [1, 16]]), [], [K], "c3")
        f16 = [[16, 64], [1, 16]]
        crb, cib = cr.ap(0, [[1, 64], [0, 16]]), ci.ap(0, [[1, 64], [0, 16]])
        self.tt(V, Bre.ap(0, f16), bre.ap(0, f16), crb, ALU.mult, [K], [K])
        self.tt(V, tb.ap(0, f16), bim.ap(0, f16), cib, ALU.mult, [K], [K])
        self.tt(V, Bre.ap(0, f16), Bre.ap(0, f16), tb.ap(0, f16), ALU.subtract, [K], [K])
        self.tt(V, Bim.ap(0, f16), bim.ap(0, f16), crb, ALU.mult, [K], [K])
        self.tt(V, tb.ap(0, f16), bre.ap(0, f16), cib, ALU.mult, [K], [K])
        self.tt(V, Bim.ap(0, f16), Bim.ap(0, f16), tb.ap(0, f16), ALU.add, [K], [K])
        if getattr(self, "s5_stop", 99) <= 4:
            return
        CT = []
        for ci_, nm in enumerate(["s5_c_re", "s5_c_im"]):
            self.load(cn.ap(0, [[64, 8], [1, 64]]), W[nm].ap(0, [[64, 128], [128 * 64, 8], [1, 64]]), [], ["cn"], "c3")
            ct = ct_tiles[ci_]
            for t in range(8):
                bank = 1 + t // 4
                self.mm(self.ps(bank, (t % 4) * 128, 128, pn=64), cn.ap(t * 64, [[1, 64]]), self.identf[:], True, True,
                        ["cn", "identf"], [self.pk(bank)])
            for hb in range(2):
                self.copy(V, ct.ap(hb * 512, [[1, 512]]), self.ps(1 + hb, 0, 512, pn=64), [self.pk(1 + hb)], [K])
            CT.append(ct)
        self.load(drep.ap(0, [[16, 8], [1, 16]], pn=32), W["s5_d"].ap(0, [[16, 32], [0, 8], [1, 16]]), [], ["drep"], "c3")
        self.mm(self.ps(3, 0, 32), drep.ap(0, [[1, 128]], pn=32), self.identf.ap(0, [[1, 32]], pn=32), True, True,
                ["drep", "identf"], [self.pk(3)])
        self.copy(V, dcol[:], self.ps(3, 0, 32), [self.pk(3)], [K])
        self.load(msk.ap(0, [[1, 256]]), W["c_mask"].ap(0, [[256, 128], [1, 256]]), [], [K], "c3")
        if getattr(self, "s5_stop", 99) <= 5:
            return
        GS = 2
        sb.ptr = mark_T
        bshape = [64, 2, 2, GS, 512]
        WTb, WPb, QTb = a("WTb", bshape, BF16), a("WPb", bshape, BF16), a("QTb", bshape, BF16)
        t1 = a("bt1", [64, GS, 32, 16], F32)
        t2 = a("bt2", [64, GS, 32, 16], F32)
        Wst = a("Wst", [128, 2, 512], BF16)
        Mst = a("Mst", [128, 7, 128], BF16)
        mt1, mt2 = a("mt1", [128, 128], F32), a("mt2", [128, 128], F32)
        od = [[512, GS], [16, 32], [1, 16]]
        full = [[1, GS * 512]]
        for bi in range(32 // GS):
            g0 = bi * GS
            KB = "s5b"
            self.pump(3)
            for dr_ in range(2):
                bb_re = Bre.ap((dr_ * 32 + g0) * 16, [[16, GS], [0, 32], [1, 16]])
                bb_im = Bim.ap((dr_ * 32 + g0) * 16, [[16, GS], [0, 32], [1, 16]])
                c_re = CT[0].ap((dr_ * 32 + g0) * 16, [[16, GS], [0, 32], [1, 16]])
                c_im = CT[1].ap((dr_ * 32 + g0) * 16, [[16, GS], [0, 32], [1, 16]])
                for tab, outt in ((0, WTb), (1, WPb)):
                    pre = PWre[tab].ap(dr_ * 1024 + g0 * 32, [[32, GS], [1, 32], [0, 16]])
                    pim = PWim[tab].ap(dr_ * 1024 + g0 * 32, [[32, GS], [1, 32], [0, 16]])
                    o_re = outt.ap(((0 * 2 + dr_) * GS) * 512, od, pn=64)
                    o_im = outt.ap(((1 * 2 + dr_) * GS) * 512, od, pn=64)
                    self.tt(V, t1[:], pre, bb_re, ALU.mult, [K], [KB])
                    self.tt(V, t2[:], pim, bb_im, ALU.mult, [K], [KB + "p"])
                    self.tt(V, o_re, t1[:], t2[:], ALU.subtract, [KB, KB + "p"], [KB])
                    self.tt(V, t1[:], pre, bb_im, ALU.mult, [K, KB], [KB])
                    self.tt(V, t2[:], pim, bb_re, ALU.mult, [K, KB], [KB + "p"])
                    self.tt(V, o_im, t1[:], t2[:], ALU.add, [KB, KB + "p"], [KB])
                pre = PWre[2].ap(dr_ * 1024 + g0 * 32, [[32, GS], [1, 32], [0, 16]])
                pim = PWim[2].ap(dr_ * 1024 + g0 * 32, [[32, GS], [1, 32], [0, 16]])
                o_re = QTb.ap(((0 * 2 + dr_) * GS) * 512, od, pn=64)
                o_im = QTb.ap(((1 * 2 + dr_) * GS) * 512, od, pn=64)
                self.tt(V, t1[:], c_re, pre, ALU.mult, [K, KB], [KB])
                self.tt(V, t2[:], c_im, pim, ALU.mult, [K, KB], [KB + "p"])
                self.tt(V, o_re, t1[:], t2[:], ALU.subtract, [KB, KB + "p"], [KB])
                self.tt(V, t1[:], c_re, pim, ALU.mult, [K, KB], [KB])
                self.tt(V, t2[:], c_im, pre, ALU.mult, [K, KB], [KB + "p"])
                self.stt(o_im, t1[:], -1.0, t2[:], ALU.mult, ALU.subtract, [KB, KB + "p"], [KB])
                if getattr(self, "s5_stop", 99) <= 6:
                    continue
                for g in range(GS):
                    self.store(S["QT"].ap((dr_ * 32 + g0 + g) * 64 * 1024, [[1024, 64], [512, 2], [1, 512]]),
                               QTb.ap((dr_ * GS + g) * 512, [[2 * GS * 512, 2], [1, 512]], pn=64), [KB], ["QTscr"], f"s5q{g}")
                if getattr(self, "s5_stop", 99) <= 7:
                    continue
                for g in range(GS):
                    bank = 6 + g % 2
                    for Jp in range(4):
                        for slot in range(3):
                            ri = slot % 2
                            self.tr(self.psb(bank, Jp * 192 + slot * 64, 64),
                                    WTb.ap(((ri * 2 + dr_) * GS + g) * 512 + Jp * 128, [[1, 128]], pn=64),
                                    self.identb.ap(0, [[1, 64]], pn=64), [KB, "identb"], [self.pk(bank)])
                    self.copy(self.ew(), Wst.ap(0, [[1, 768]]), self.psb(bank, 0, 768), [self.pk(bank)], ["Wst"])
                    self.store(S["Ws5"].ap((dr_ * 32 + g0 + g) * 128 * 768, [[768, 128], [1, 768]]),
                               Wst.ap(0, [[1, 768]]), ["Wst"], ["Wscr"], "s5w")
            if getattr(self, "s5_stop", 99) <= 8:
                continue
            for g in range(GS):
                for dl in range(4):
                    for ri in range(2):
                        self.mm(self.ps(4, dl * 128, 128), WPb.ap(((ri * 2 + 0) * GS + g) * 512, [[1, 128]], pn=64),
                                QTb.ap(((ri * 2 + 0) * GS + g) * 512 + dl * 128, [[1, 128]], pn=64), ri == 0, ri == 1,
                                [KB], [self.pk(4)])
                    for ri in range(2):
                        self.mm(self.ps(5, dl * 128, 128), WPb.ap(((ri * 2 + 1) * GS + g) * 512 + dl * 128, [[1, 128]], pn=64),
                                QTb.ap(((ri * 2 + 1) * GS + g) * 512, [[1, 128]], pn=64), ri == 0, ri == 1,
                                [KB], [self.pk(5)])
                if getattr(self, "s5_stop", 99) <= 9:
                    continue
                self.copy("act", Mst.ap(4 * 128, [[1, 384]]), self.ps(4, 128, 384), [self.pk(4)], ["Mst"])
                for dl in range(1, 4):
                    self.copy("act", Mst.ap((3 - dl) * 128, [[1, 128]]), self.ps(5, dl * 128, 128), [self.pk(5)], ["Mst"])
                if getattr(self, "s5_stop", 99) <= 10:
                    continue
                import os
                self.copy("act", mt1[:], self.ps(4, 0, 128), [self.pk(4)], ["mta"])
                self.copy("act", mt2[:], self.ps(5, 0, 128), [self.pk(5)], ["mta"])
                if os.environ.get("S5SUB") == "0":
                    continue
                self.tt(V, mt1[:], mt1[:], msk.ap(0, [[1, 128]]), ALU.mult, ["mta", K], ["mt"])
                self.tt(V, mt2[:], mt2[:], msk.ap(128, [[1, 128]]), ALU.mult, ["mta", K], ["mt"])
                if os.environ.get("S5SUB") == "1":
                    continue
                self.tt(V, mt1[:], mt1[:], mt2[:], ALU.add, ["mt"], ["mt"])
                if os.environ.get("S5SUB") == "2":
                    continue
                self.stt(Mst.ap(3 * 128, [[1, 128]]), self.identf[:], dcol.ap(g0 + g, [[1, 1]]), mt1[:], ALU.mult, ALU.add,
                         ["mt", "identf", K], ["Mst"])
                if getattr(self, "s5_stop", 99) <= 11:
                    continue
                self.store(S["M"].ap((g0 + g) * 128 * 896, [[896, 128], [1, 896]]), Mst.ap(0, [[1, 896]]), ["Mst"], ["Mscr"], "s5m")


    def alloc_B(self, L):
        sb = self.sb
        sb.ptr = self.arena0
        a = sb.alloc
        Kc = L // 32
        self.gfm = a("gfm", [128, 4, L], BF16)
        self.ufm = a("ufm", [128, L], BF16)
        self.U32 = a("U32", [128, 8, 4, Kc], BF16)
        self.Sst = a("Sst", [64, 2, 2, 8, Kc], F32)
        self.Hbf = a("Hbf", [128, 2, 2, 8, Kc], BF16)
        self.G32 = a("G32", [128, 8, 4, Kc], BF16)
        self.sel = a("sel", [128, 64, 128], BF16)
        self.selT = a("selT", [128, 64, 128], BF16)
        self.wglu = a("wglu", [128, 4, 512], BF16)
        self.wr = [a(f"wr{i}", [128, 4, 192], BF16) for i in range(2)]
        self.qr = [a(f"qr{i}", [128, 2, 512], BF16) for i in range(4)]
        self.mr = [a(f"mr{i}", [128, 7, 128], BF16) for i in range(2)]
        self.gt1 = [a(f"gt1{i}", [128, 512], F32) for i in range(2)]
        self.gt2 = [a(f"gt2{i}", [128, 512], F32) for i in range(2)]
        self.yst = [a(f"yst{i}", [128, 512], BF16) for i in range(2)]
        self.Tt = [a(f"Tt{i}", [64, 2, 2, 8], F32) for i in range(3)]

    def phaseB(self, L):
        S, W, P = self.S, self.W, self.P
        LM = self.LMAX
        Kc = L // 32
        self.alloc_B(L)
        V = "dve"
        nt = L // 512
        self.load(self.sel.ap(0, [[1, 8192]]), W["c_sel"].ap(0, [[8192, 128], [1, 8192]]), [], ["sel"], "bsel")
        self.load(self.selT.ap(0, [[1, 8192]]), W["c_selT"].ap(0, [[8192, 128], [1, 8192]]), [], ["selT"], "bsel")
        self.load(self.wglu.ap(0, [[1, 2048]]), S["Wglu"].ap(0, [[2048, 128], [1, 2048]]), ["Wglu"], ["wglu"], "bsel")
        cw = cq = cm = 0
        hz = self.Hbf.ap(0, [[1, 32 * Kc]], p0=64, pn=64)
        P.add("pool", lambda e, hz=hz: e.memset(hz, 0.0), w=["Hbf"])
        for i_ in range(4):
            qz = self.qr[i_].ap(0, [[1, 1024]], p0=64, pn=64)
            P.add("pool", lambda e, qz=qz: e.memset(qz, 0.0), w=[("qr", i_)])
        ristr, dstr, gstr = 2 * 8 * Kc, 8 * Kc, Kc
        for i in range(4):
            self.load(self.ufm.ap(0, [[1, L]]), S["u"].ap(i * 128 * LM, [[LM, 128], [1, L]]),
                      [("uscr", t) for t in range(nt)], ["ufm"], "bu")
            for g8 in range(8):
                bank = g8 % 4
                for Jp in range(4):
                    for jl in range(8):
                        self.mm(self.ps(bank, Jp * Kc, Kc), self.sel.ap((g8 * 8 + jl) * 128, [[1, 128]]),
                                self.ufm.ap(8 * Jp + jl, [[32, Kc]]), jl == 0, jl == 7, ["sel", "ufm"], [self.pk(bank)])
                self.copy(self.ew(), self.U32.ap(g8 * 4 * Kc, [[1, 4 * Kc]]), self.ps(bank, 0, 4 * Kc), [self.pk(bank)], ["U32"])
            for g8 in range(8):
                g = 8 * i + g8
                for dr_ in range(2):
                    s = cw % 2
                    cw += 1
                    self.load(self.wr[s].ap(0, [[1, 768]]), S["Ws5"].ap((dr_ * 32 + g) * 128 * 768, [[768, 128], [1, 768]]),
                              ["Wscr"], [("wr", s)], f"wr{s}")
                    bank = 4 + (g8 * 2 + dr_) % 2
                    for ri in range(2):
                        for Jp in range(4):
                            self.mm(self.ps(bank, ri * Kc, Kc), self.wr[s].ap(Jp * 192 + ri * 64, [[1, 128]]),
                                    self.U32.ap((g8 * 4 + Jp) * Kc, [[1, Kc]]), Jp == 0, Jp == 3, [("wr", s), "U32"], [self.pk(bank)])
                    self.copy(self.ew(), self.Sst.ap(dr_ * dstr + g8 * gstr, [[ristr, 2], [1, Kc]]),
                              self.psv(bank, 0, [[Kc, 2], [1, Kc]], pn=64), [self.pk(bank)], ["Sst"])
            c1 = self.A32.ap(8 * i, [[0, 2], [32, 2], [1, 8]])
            c2 = self.Ac2.ap(8 * i, [[64, 2], [32, 2], [1, 8]])
            T0, T1, T2 = self.Tt
            for k in range(1, Kc):
                dcur = dstr + (Kc - 1 - 2 * k)
                dprv = dstr + (Kc + 1 - 2 * k)
                cur2 = self.Sst.ap(k, [[ristr, 2], [dcur, 2], [gstr, 8]])
                prv2 = self.Sst.ap(k - 1, [[ristr, 2], [dprv, 2], [gstr, 8]])
                prvs = self.Sst.ap(ristr + k - 1, [[-ristr, 2], [dprv, 2], [gstr, 8]])
                self.tt(V, T0[:], prv2, c1, ALU.mult, ["Sst", "A32"], ["Tt0"])
                self.tt(V, T1[:], prvs, c2, ALU.mult, ["Sst", "A32"], ["Tt1"])
                self.tt(V, cur2, cur2, T0[:], ALU.add, ["Tt0", "Sst"], ["Sst"])
                self.tt(V, cur2, cur2, T1[:], ALU.add, ["Tt1", "Sst"], ["Sst"])
            for dr_ in range(2):
                zcol = 0 if dr_ == 0 else Kc - 1
                hz2 = self.Hbf.ap(dr_ * dstr + zcol, [[ristr, 2], [gstr, 8], [1, 1]], pn=64)
                P.add("pool", lambda e, hz2=hz2: e.memset(hz2, 0.0), w=["Hbf"])
                so, do = (0, 1) if dr_ == 0 else (1, 0)
                self.copy("act" if dr_ else "dve", self.Hbf.ap(dr_ * dstr + do, [[ristr, 2], [gstr, 8], [1, Kc - 1]], pn=64),
                          self.Sst.ap(dr_ * dstr + so, [[ristr, 2], [gstr, 8], [1, Kc - 1]]), ["Sst"], ["Hbf"])
            for g8 in range(8):
                g = 8 * i + g8
                sm = cm % 2
                cm += 1
                self.load(self.mr[sm].ap(0, [[1, 896]]), S["M"].ap(g * 128 * 896, [[896, 128], [1, 896]]), ["Mscr"], [("mr", sm)], f"mr{sm}")
                sq = []
                for dr_ in range(2):
                    s = cq % 4
                    cq += 1
                    self.load(self.qr[s].ap(0, [[1, 1024]], pn=64), S["QT"].ap((dr_ * 32 + g) * 64 * 1024, [[1024, 64], [1, 1024]]),
                              ["QTscr"], [("qr", s)], f"qr{s}")
                    sq.append(s)
                bank = g8 % 4
                for J in range(4):
                    first = True
                    for Jp in range(4):
                        self.mm(self.ps(bank, J * Kc, Kc), self.mr[sm].ap((J - Jp + 3) * 128, [[1, 128]]),
                                self.U32.ap((g8 * 4 + Jp) * Kc, [[1, Kc]]), first, False, [("mr", sm), "U32"], [self.pk(bank)])
                        first = False
                    for dr_ in range(2):
                        for ri in range(2):
                            last = (dr_ == 1 and ri == 1)
                            self.mm(self.ps(bank, J * Kc, Kc), self.qr[sq[dr_]].ap(ri * 512 + J * 128, [[1, 128]]),
                                    self.Hbf.ap(ri * ristr + dr_ * dstr + g8 * gstr, [[1, Kc]]), False, last,
                                    [("qr", sq[dr_]), "Hbf"], [self.pk(bank)])
                n = 4 * Kc
                y = self.ps(bank, 0, n)
                gs_ = g8 % 2
                ta, tb = self.gt1[gs_].ap(0, [[1, n]]), self.gt2[gs_].ap(0, [[1, n]])
                self.act(ta, y, AF.Square, [self.pk(bank)], [("gt1", gs_)])
                self.ts(V, ta, ta, 0.044715, 1.0, ALU.mult, ALU.add, [("gt1", gs_)], [("gt1", gs_)])
                self.tt(V, tb, ta, y, ALU.mult, [("gt1", gs_), self.pk(bank)], [("gt2", gs_)])
                self.act(tb, tb, AF.Sigmoid, [("gt2", gs_)], [("gt2", gs_)], scale=1.5957691216057308)
                self.tt(V, self.G32.ap(g8 * 4 * Kc, [[1, n]]), tb, y, ALU.mult, [("gt2", gs_), self.pk(bank)], ["G32"])
            for J in range(4):
                for jh in range(2):
                    bank = 4 + (J * 2 + jh) % 4
                    for j4 in range(4):
                        jl = jh * 4 + j4
                        for g8 in range(8):
                            self.mm(self.ps(bank, j4 * Kc, Kc), self.selT.ap((g8 * 8 + jl) * 128, [[1, 128]]),
                                    self.G32.ap((g8 * 4 + J) * Kc, [[1, Kc]]), g8 == 0, g8 == 7, ["selT", "G32"], [self.pk(bank)])
                    self.copy(self.ew(), self.gfm.ap(i * L + 8 * J + jh * 4, [[1, 4], [32, Kc]]),
                              self.psv(bank, 0, [[Kc, 4], [1, Kc]]), [self.pk(bank)], ["gfm"])
        if self.dbg:
            for i in range(4):
                self.store(S["g"].ap(i * 128 * LM, [[LM, 128], [1, L]]), self.gfm.ap(i * L, [[1, L]]), ["gfm"], ["gscr"], "gdbg")
        c = 0
        for t in range(nt):
            for co in range(4):
                bank = c % 4
                ys = c % 2
                c += 1
                for kc in range(4):
                    self.mm(self.ps(bank), self.wglu.ap(kc * 512 + co * 128, [[1, 128]]), self.gfm.ap(kc * L + t * 512, [[1, 512]]),
                            kc == 0, kc == 3, ["wglu", "gfm"], [self.pk(bank)])
                sg = self.gt1[ys].ap(0, [[1, 512]])
                self.act(sg, self.ps(bank), AF.Sigmoid, [self.pk(bank), "bglu"], [("gt1", ys)], bias=self.bglu.ap(co, [[1, 1]]))
                self.tt(V, self.yst[ys][:], sg, self.gfm.ap(co * L + t * 512, [[1, 512]]), ALU.mult, [("gt1", ys), "gfm"], [("yst", ys)])
                self.store(S["ya"].ap(co * 128 * LM + t * 512, [[LM, 128], [1, 512]]), self.yst[ys][:], [("yst", ys)],
                           [("yascr", t)], f"yst{ys}")

    def build(self):
        P = self.P
        stages = getattr(self, "stages", "ctsv")
        self.setup_consts()
        P.add("pool", lambda e: e.memset(self.halfpi[:], math.pi / 2), w=["halfpi"])
        self.conv_q = []
        if "v" in stages:
            self.conv_weights()
        if "t" in stages:
            self.setup_tables()
            P.barrier()
        if "s" in stages:
            self.setup_s5()
        self.pump(len(self.conv_q))
        P.barrier()
        tok0 = 0
        for L in self.seq_lens:
            NB = 8 if L % 1024 == 0 else 4
            if "A" in self.phases:
                self.phaseA(tok0, L, NB)
                P.barrier()
            if "B" in self.phases:
                self.phaseB(L)
                P.barrier()
            if "C" in self.phases:
                self.phaseC1(L)
                P.barrier()
                self.phaseC2(L, NB)
                P.barrier()
            if "D" in self.phases:
                self.phaseD1(L)
                P.barrier()
                self.phaseD2(tok0, L, NB)
                P.barrier()
            tok0 += L
        P.emit(final_wait_keys=["yst", "xst", "stg0", "stg1", "vst", "yst0", "yst1", "s5m", "s5w", "s5q0", "s5q1", "gdbg", "tabst"])
        return self.nc


SEQ_LENS = [4096, 2048, 2048, 2048, 2048]
_CACHE = {}


def kernel(**inputs):
    xp = np.asarray(inputs["x_prompt"], dtype=np.float32)
    xs = np.asarray(inputs["x_sample"], dtype=np.float32)
    consts = host_constants()
    shared = {nm: np.ascontiguousarray(np.asarray(inputs[nm], dtype=np.float32)) for nm, _ in WEIGHT_SPECS}
    shared.update(consts)
    if "nc" not in _CACHE:
        _CACHE["nc"] = MK(SEQ_LENS).build()
    nc = _CACHE["nc"]
    in_maps = []
    for c in range(8):
        xin = np.concatenate([xp[c].reshape(4096, D), xs[4 * c:4 * c + 4].reshape(4 * 2048, D)], axis=0)
        m = dict(shared)
        m["x_in"] = np.ascontiguousarray(xin)
        in_maps.append(m)
    res = run_bass_kernel_spmd(nc, in_maps, core_ids=list(range(8)))
    yp = np.zeros((8, 4096, D), np.float32)
    ys = np.zeros((32, 2048, D), np.float32)
    for c in range(8):
        y = np.asarray(res.results[c]["y_out"]).reshape(-1, D)
        yp[c] = y[:4096]
        ys[4 * c:4 * c + 4] = y[4096:].reshape(4, 2048, D)
    return (yp, ys)
```
